# Optimizing a Trainium2 kernel written in Bass

```python
import jax, jax.numpy as jnp
from jax import lax
import numpy as np

D_MODEL = 1024
BATCH = 4
SEQ = 4096
DEPTH = 4

GRID_W = 64
CTX_LEN = 256
N_MIXERS = 4
Q_BLOCK = 128
ROPE_THETA = 10000.0
NORM_EPS = 1e-6
CONV_WIDTH = 31
CONV_PAD = CONV_WIDTH // 2
GQA_HEADS = 16
GQA_KV_HEADS = 4
GQA_GROUP = GQA_HEADS // GQA_KV_HEADS
GQA_HEAD_DIM = D_MODEL // GQA_HEADS
POOL_WINDOWS = (2, 4, 8, 16)
POOL_GROUP = D_MODEL // len(POOL_WINDOWS)
MLA_HEADS = 16
MLA_NOPE = 64
MLA_ROPE = 32
MLA_V = 64
MLA_QK = MLA_NOPE + MLA_ROPE
MLA_Q_LORA = 768
MLA_KV_LORA = 256
D_FF = 4 * D_MODEL
N_CONV = (DEPTH + 3) // N_MIXERS
N_GQA = (DEPTH + 2) // N_MIXERS
N_POOL = (DEPTH + 1) // N_MIXERS
N_MLA = DEPTH // N_MIXERS

kernel_name = 'hybrid_interleaved_dit_ctx_prefix'


def rms_norm(x, g):
    xf = x.astype(jnp.float32)
    y = xf * lax.rsqrt(jnp.mean(xf * xf, axis=-1, keepdims=True) + NORM_EPS)
    return (y * g.astype(jnp.float32)).astype(x.dtype)


def layer_norm(x, g, b):
    xf = x.astype(jnp.float32)
    mu = jnp.mean(xf, axis=-1, keepdims=True)
    var = jnp.mean(jnp.square(xf - mu), axis=-1, keepdims=True)
    y = (xf - mu) * lax.rsqrt(var + NORM_EPS)
    return (y * g.astype(jnp.float32) + b.astype(jnp.float32)).astype(x.dtype)


def modulate(h, shift, scale):
    return h * (1 + scale[..., None, :]) + shift[..., None, :]


def axial_tables(row, col, rot_dim):
    quarter = rot_dim // 4
    inv = ROPE_THETA ** (-jnp.arange(quarter, dtype=jnp.float32) / quarter)
    ang_r = row[:, None] * inv
    ang_c = col[:, None] * inv
    return (jnp.cos(ang_r), jnp.sin(ang_r), jnp.cos(ang_c), jnp.sin(ang_c))


def _rotate(v, cos, sin):
    a, b = jnp.split(v, 2, axis=-1)
    cos = cos.astype(v.dtype)
    sin = sin.astype(v.dtype)
    return jnp.concatenate([a * cos - b * sin, b * cos + a * sin], axis=-1)


def apply_axial_rope(x, tables):
    cr, sr, cc, sc = tables
    xr, xc = jnp.split(x, 2, axis=-1)
    return jnp.concatenate([_rotate(xr, cr, sr), _rotate(xc, cc, sc)], axis=-1)


def softmax_attend(q, k, v):
    s = jnp.einsum('bkgqd,bkld->bkgql', q, k).astype(jnp.float32) * (q.shape[-1] ** -0.5)
    p = jax.nn.softmax(s, axis=-1).astype(v.dtype)
    return jnp.einsum('bkgql,bkld->bkgqd', p, v)


def blocked_attend(q, k, v):
    b, kh, g, t, dq = q.shape
    nb = t // Q_BLOCK
    qb = q.reshape(b, kh, g, nb, Q_BLOCK, dq).transpose(3, 0, 1, 2, 4, 5)
    out = lax.map(lambda qq: softmax_attend(qq, k, v), qb)
    return out.transpose(1, 2, 3, 0, 4, 5).reshape(b, kh, g, t, v.shape[-1])


def merge_heads(o):
    b, kh, g, t, dv = o.shape
    return o.transpose(0, 3, 1, 2, 4).reshape(b, t, kh * g * dv)


def conv_module(h, w_pw1, w_dw, b_dw, ln_g, ln_b, w_pw2):
    u = h @ w_pw1
    a, gate = jnp.split(u, 2, axis=-1)
    u = a * jax.nn.sigmoid(gate)
    u = lax.conv_general_dilated(
        u, w_dw[:, None, :].astype(u.dtype), window_strides=(1,),
        padding=[(CONV_PAD, CONV_PAD)], dimension_numbers=('NWC', 'WIO', 'NWC'),
        feature_group_count=D_MODEL) + b_dw
    u = jax.nn.silu(layer_norm(u, ln_g, ln_b))
    return u @ w_pw2


def gqa_project(h, w_qkv, q_norm, k_norm):
    b, t, _ = h.shape
    nq = GQA_HEADS * GQA_HEAD_DIM
    nk = GQA_KV_HEADS * GQA_HEAD_DIM
    qkv = h @ w_qkv
    q = qkv[..., :nq].reshape(b, t, GQA_KV_HEADS, GQA_GROUP, GQA_HEAD_DIM).transpose(0, 2, 3, 1, 4)
    k = qkv[..., nq:nq + nk].reshape(b, t, GQA_KV_HEADS, GQA_HEAD_DIM).transpose(0, 2, 1, 3)
    v = qkv[..., nq + nk:].reshape(b, t, GQA_KV_HEADS, GQA_HEAD_DIM).transpose(0, 2, 1, 3)
    return rms_norm(q, q_norm), rms_norm(k, k_norm), v


def gqa_mixer(hl, hc, w_qkv, q_norm, k_norm, w_o, tables, with_ctx):
    ql, kl, vl = gqa_project(hl, w_qkv, q_norm, k_norm)
    qc, kc, vc = gqa_project(hc, w_qkv, q_norm, k_norm)
    ql = apply_axial_rope(ql, tables)
    kl = apply_axial_rope(kl, tables)
    k_all = jnp.concatenate([kc, kl], axis=2)
    v_all = jnp.concatenate([vc, vl], axis=2)
    ol = merge_heads(blocked_attend(ql, k_all, v_all)) @ w_o
    oc = merge_heads(softmax_attend(qc, kc, vc)) @ w_o if with_ctx else None
    return ol, oc


def pool_mixer(h, w_pool, scale):
    b, t, _ = h.shape
    hf = h.astype(jnp.float32)
    cs = jnp.concatenate([jnp.zeros((b, 1, D_MODEL), jnp.float32), jnp.cumsum(hf, axis=1)], axis=1)
    pos = jnp.arange(t)
    outs = []
    for g, w in enumerate(POOL_WINDOWS):
        lo = jnp.clip(pos - w // 2, 0, t)
        hi = jnp.clip(pos + w - w // 2, 0, t)
        csg = cs[..., g * POOL_GROUP:(g + 1) * POOL_GROUP]
        win = jnp.take(csg, hi, axis=1) - jnp.take(csg, lo, axis=1)
        cnt = (hi - lo).astype(jnp.float32)[:, None]
        outs.append(win / cnt - hf[..., g * POOL_GROUP:(g + 1) * POOL_GROUP])
    p = jnp.stack(outs, axis=2).astype(h.dtype)
    y = jnp.einsum('btgc,gcd->btgd', p, w_pool).reshape(b, t, D_MODEL)
    return y * scale


def mla_project(h, w_dq, q_lora_norm, w_uq, w_dkv, kv_lora_norm, w_ukv, q_norm, k_norm):
    b, t, _ = h.shape
    cq = rms_norm(h @ w_dq, q_lora_norm)
    q = (cq @ w_uq).reshape(b, t, MLA_HEADS, MLA_QK).transpose(0, 2, 1, 3)
    dkv = h @ w_dkv
    ckv = rms_norm(dkv[..., :MLA_KV_LORA], kv_lora_norm)
    k_rope = dkv[..., MLA_KV_LORA:]
    kv = (ckv @ w_ukv).reshape(b, t, MLA_HEADS, MLA_NOPE + MLA_V).transpose(0, 2, 1, 3)
    k_nope, v = kv[..., :MLA_NOPE], kv[..., MLA_NOPE:]
    k = jnp.concatenate([k_nope, jnp.broadcast_to(k_rope[:, None], (b, MLA_HEADS, t, MLA_ROPE))], axis=-1)
    return rms_norm(q, q_norm), rms_norm(k, k_norm), v


def rope_tail(x, tables):
    return jnp.concatenate([x[..., :MLA_NOPE], apply_axial_rope(x[..., MLA_NOPE:], tables)], axis=-1)


def mla_mixer(hl, hc, w_dq, q_lora_norm, w_uq, w_dkv, kv_lora_norm, w_ukv, q_norm, k_norm, w_o,
              tables, with_ctx):
    args = (w_dq, q_lora_norm, w_uq, w_dkv, kv_lora_norm, w_ukv, q_norm, k_norm)
    ql, kl, vl = mla_project(hl, *args)
    qc, kc, vc = mla_project(hc, *args)
    ql = rope_tail(ql, tables)[:, :, None]
    kl = rope_tail(kl, tables)
    k_all = jnp.concatenate([kc, kl], axis=2)
    v_all = jnp.concatenate([vc, vl], axis=2)
    ol = merge_heads(blocked_attend(ql, k_all, v_all)) @ w_o
    oc = merge_heads(softmax_attend(qc[:, :, None], kc, vc)) @ w_o if with_ctx else None
    return ol, oc


def sq_relu_mlp(h, w1, w2):
    return jnp.square(jax.nn.relu(h @ w1)) @ w2


def setup_inputs(seed: int = 0) -> dict:
    key = jax.random.key(seed)
    ks = jax.random.split(key, 31)
    f32 = jnp.float32

    def nrm(k, shape, scale):
        return jax.random.normal(k, shape, f32) * scale

    def gain(k, shape):
        return 1.0 + 0.05 * jax.random.normal(k, shape, f32)

    d = D_MODEL
    return {
        'x': nrm(ks[0], (BATCH, SEQ, d), 1.0),
        'c': nrm(ks[1], (BATCH, d), 1.0),
        'ctx': nrm(ks[2], (BATCH, CTX_LEN, d), 1.0),
        'c_ctx': nrm(ks[3], (d,), 1.0),
        'norm1_g': gain(ks[4], (DEPTH, d)),
        'norm2_g': gain(ks[5], (DEPTH, d)),
        'w_mod': nrm(ks[6], (DEPTH, d, 6 * d), 0.5 * d ** -0.5),
        'b_mod': nrm(ks[7], (DEPTH, 6 * d), 0.02),
        'w_ff1': nrm(ks[8], (DEPTH, d, D_FF), d ** -0.5),
        'w_ff2': nrm(ks[9], (DEPTH, D_FF, d), D_FF ** -0.5),
        'conv_w_pw1': nrm(ks[10], (N_CONV, d, 2 * d), d ** -0.5),
        'conv_w_dw': nrm(ks[11], (N_CONV, CONV_WIDTH, d), CONV_WIDTH ** -0.5),
        'conv_b_dw': nrm(ks[12], (N_CONV, d), 0.02),
        'conv_ln_g': gain(ks[13], (N_CONV, d)),
        'conv_ln_b': nrm(ks[14], (N_CONV, d), 0.02),
        'conv_w_pw2': nrm(ks[15], (N_CONV, d, d), d ** -0.5),
        'gqa_w_qkv': nrm(ks[16], (N_GQA, d, (GQA_HEADS + 2 * GQA_KV_HEADS) * GQA_HEAD_DIM), d ** -0.5),
        'gqa_q_norm': gain(ks[17], (N_GQA, GQA_HEAD_DIM)),
        'gqa_k_norm': gain(ks[18], (N_GQA, GQA_HEAD_DIM)),
        'gqa_w_o': nrm(ks[19], (N_GQA, GQA_HEADS * GQA_HEAD_DIM, d), (GQA_HEADS * GQA_HEAD_DIM) ** -0.5),
        'pool_w': nrm(ks[20], (N_POOL, len(POOL_WINDOWS), POOL_GROUP, POOL_GROUP), POOL_GROUP ** -0.5),
        'pool_scale': gain(ks[21], (N_POOL, d)),
        'mla_w_dq': nrm(ks[22], (N_MLA, d, MLA_Q_LORA), d ** -0.5),
        'mla_q_lora_norm': gain(ks[23], (N_MLA, MLA_Q_LORA)),
        'mla_w_uq': nrm(ks[24], (N_MLA, MLA_Q_LORA, MLA_HEADS * MLA_QK), MLA_Q_LORA ** -0.5),
        'mla_w_dkv': nrm(ks[25], (N_MLA, d, MLA_KV_LORA + MLA_ROPE), d ** -0.5),
        'mla_kv_lora_norm': gain(ks[26], (N_MLA, MLA_KV_LORA)),
        'mla_w_ukv': nrm(ks[27], (N_MLA, MLA_KV_LORA, MLA_HEADS * (MLA_NOPE + MLA_V)), MLA_KV_LORA ** -0.5),
        'mla_q_norm': gain(ks[28], (N_MLA, MLA_QK)),
        'mla_k_norm': gain(ks[29], (N_MLA, MLA_QK)),
        'mla_w_o': nrm(ks[30], (N_MLA, MLA_HEADS * MLA_V, d), (MLA_HEADS * MLA_V) ** -0.5),
    }


def reference(x, c, ctx, c_ctx, norm1_g, norm2_g, w_mod, b_mod, w_ff1, w_ff2,
              conv_w_pw1, conv_w_dw, conv_b_dw, conv_ln_g, conv_ln_b, conv_w_pw2,
              gqa_w_qkv, gqa_q_norm, gqa_k_norm, gqa_w_o, pool_w, pool_scale,
              mla_w_dq, mla_q_lora_norm, mla_w_uq, mla_w_dkv, mla_kv_lora_norm, mla_w_ukv,
              mla_q_norm, mla_k_norm, mla_w_o):
    seq = x.shape[1]
    rows = seq // GRID_W
    row_ids = jnp.repeat(jnp.arange(rows, dtype=jnp.float32), GRID_W)
    col_ids = jnp.tile(jnp.arange(GRID_W, dtype=jnp.float32), rows)
    tab_gqa = axial_tables(row_ids, col_ids, GQA_HEAD_DIM)
    tab_mla = axial_tables(row_ids, col_ids, MLA_ROPE)

    sc = jax.nn.silu(c)
    scc = jax.nn.silu(c_ctx)
    xl, xc = x, ctx
    for i in range(DEPTH):
        m = i % N_MIXERS
        j = i // N_MIXERS
        with_ctx = i < DEPTH - 1
        mod_l = jnp.split(sc @ w_mod[i] + b_mod[i], 6, axis=-1)
        mod_c = jnp.split(scc @ w_mod[i] + b_mod[i], 6, axis=-1)

        hl = modulate(rms_norm(xl, norm1_g[i]), mod_l[0], mod_l[1])
        hc = modulate(rms_norm(xc, norm1_g[i]), mod_c[0], mod_c[1])
        if m == 0:
            cargs = (conv_w_pw1[j], conv_w_dw[j], conv_b_dw[j], conv_ln_g[j], conv_ln_b[j], conv_w_pw2[j])
            ol = conv_module(hl, *cargs)
            oc = conv_module(hc, *cargs) if with_ctx else None
        elif m == 1:
            ol, oc = gqa_mixer(hl, hc, gqa_w_qkv[j], gqa_q_norm[j], gqa_k_norm[j], gqa_w_o[j],
                               tab_gqa, with_ctx)
        elif m == 2:
            ol = pool_mixer(hl, pool_w[j], pool_scale[j])
            oc = pool_mixer(hc, pool_w[j], pool_scale[j]) if with_ctx else None
        else:
            ol, oc = mla_mixer(hl, hc, mla_w_dq[j], mla_q_lora_norm[j], mla_w_uq[j], mla_w_dkv[j],
                               mla_kv_lora_norm[j], mla_w_ukv[j], mla_q_norm[j], mla_k_norm[j],
                               mla_w_o[j], tab_mla, with_ctx)
        xl = xl + mod_l[2][:, None, :] * ol
        if with_ctx:
            xc = xc + mod_c[2] * oc

        hl = modulate(rms_norm(xl, norm2_g[i]), mod_l[3], mod_l[4])
        xl = xl + mod_l[5][:, None, :] * sq_relu_mlp(hl, w_ff1[i], w_ff2[i])
        if with_ctx:
            hc = modulate(rms_norm(xc, norm2_g[i]), mod_c[3], mod_c[4])
            xc = xc + mod_c[5] * sq_relu_mlp(hc, w_ff1[i], w_ff2[i])
    return xl
```

```python
from concourse.bass_utils import run_bass_kernel_spmd

import numpy as np
from contextlib import ExitStack
import concourse.bass as bass
import concourse.mybir as mybir

F32 = mybir.dt.float32
BF16 = mybir.dt.bfloat16
AF = mybir.ActivationFunctionType
ALU = mybir.AluOpType

ENGS = ["pe", "act", "dve", "pool", "sp"]
N_DMA_SEMS = 6


def _region(ap):
    t = ap.tensor
    dims = ap.ap
    off = int(ap.offset)
    cls = type(t).__name__
    if cls.startswith("DRam"):
        ext = 1
        for s, c in dims:
            ext += (c - 1) * abs(s)
        return (t.name, 0, 1, off, off + ext)
    pstep, pcnt = dims[0]
    p0 = off // pstep if pstep > 0 else 0
    lo = off - p0 * pstep
    ext = 1
    for s, c in dims[1:]:
        ext += (c - 1) * abs(s)
    return (t.name, p0, p0 + pcnt, lo, lo + ext)


def _overlap(a, b):
    return a[1] < b[2] and b[1] < a[2] and a[3] < b[4] and b[3] < a[4]


class Op:
    __slots__ = ("eng", "emit", "reads", "writes", "is_dma", "idx", "waits",
                 "signal", "dsem", "dwait_prev", "clock", "dma_inc")


class Prog:
    def __init__(self, nc, same_engine_sync=True):
        self.nc = nc
        self.ops = {e: [] for e in ENGS}
        self.same_engine_sync = same_engine_sync
        self.track = {}
        self.know = {e: {} for e in ENGS}
        self.dma_rr = {e: 0 for e in ENGS}
        self.dma_last = {e: [None] * N_DMA_SEMS for e in ENGS}
        self.dma_count = {e: [0] * N_DMA_SEMS for e in ENGS}
        self.nops = 0
        self.notrack = set()

    def _add(self, eng, emit, reads, writes, is_dma=False, dma_inc=16):
        op = Op()
        op.eng = eng
        op.emit = emit
        op.is_dma = is_dma
        op.dma_inc = dma_inc
        op.idx = len(self.ops[eng])
        op.waits = {}
        op.signal = False
        op.dsem = None
        rr = [_region(a) for a in reads if a.tensor.name not in self.notrack]
        wr = [_region(a) for a in writes]
        deps = []
        for r in rr:
            for rec in self.track.get(r[0], ()):
                if rec[1] is not None and _overlap(rec[0], r):
                    deps.append(rec[1])
        for w in wr:
            for rec in self.track.get(w[0], ()):
                if _overlap(rec[0], w):
                    if rec[1] is not None:
                        deps.append(rec[1])
                    deps.extend(rec[2].values())
        know = self.know[eng]
        if is_dma:
            q = self.dma_rr[eng]
            self.dma_rr[eng] = (q + 1) % N_DMA_SEMS
            prev = self.dma_last[eng][q]
            if prev is not None:
                deps.append(prev)
            self.dma_count[eng][q] += 1
            op.dsem = q
            mytok = ("D", eng, q, self.dma_count[eng][q], None)
            self.dma_last[eng][q] = mytok
        else:
            mytok = ("E", eng, op.idx, None)
        for tok in deps:
            if tok[0] == "E":
                _, e2, i2, _ = tok
                if e2 == eng:
                    if eng == "pe" or not self.same_engine_sync:
                        continue
                key = ("E", e2)
                val = i2
            else:
                _, e2, q2, c2, _ = tok
                key = ("D", e2, q2)
                val = c2
            if know.get(key, -1) >= val:
                continue
            if op.waits.get(key, -1) < val:
                op.waits[key] = val
        for key, val in op.waits.items():
            know[key] = max(know.get(key, -1), val)
            if key[0] == "E":
                src = self.ops[key[1]][val]
                src.signal = True
                for k2, v2 in src.clock.items():
                    if know.get(k2, -1) < v2:
                        know[k2] = v2
        op.clock = dict(know)
        skey = eng if not is_dma else ("dma", eng, op.dsem)
        for r in rr:
            lst = self.track.setdefault(r[0], [])
            for rec in lst:
                if rec[0] == r:
                    rec[2][skey] = mytok
                    break
            else:
                lst.append([r, None, {skey: mytok}])
        for w in wr:
            lst = self.track.setdefault(w[0], [])
            newl = []
            found = False
            for rec in lst:
                r0 = rec[0]
                if r0 == w:
                    rec[1] = mytok
                    rec[2] = {}
                    newl.append(rec)
                    found = True
                elif (w[1] <= r0[1] and r0[2] <= w[2] and w[3] <= r0[3] and r0[4] <= w[4]):
                    continue
                else:
                    newl.append(rec)
            if not found:
                newl.append([w, mytok, {}])
            self.track[w[0]] = newl
        self.ops[eng].append(op)
        self.nops += 1
        return op

    def mm(self, out, lhsT, rhs, start=True, stop=True, **kw):
        rd = [lhsT, rhs] + ([] if start else [out])
        return self._add("pe", lambda e: e.matmul(out, lhsT, rhs, start=start, stop=stop, **kw), rd, [out])

    def transpose(self, out, in_, ident):
        return self._add("pe", lambda e: e.transpose(out, in_, ident), [in_, ident], [out])

    def act(self, out, in_, func, bias=None, scale=None, accum_out=None, eng="act"):
        kw = {}
        rd = [in_]
        if bias is not None:
            kw["bias"] = bias
            if not isinstance(bias, (int, float)):
                rd.append(bias)
        if scale is not None:
            kw["scale"] = scale
            if not isinstance(scale, (int, float)):
                rd.append(scale)
        wr = [out]
        if accum_out is not None:
            kw["accum_out"] = accum_out
            wr.append(accum_out)
        return self._add("act", lambda e: e.activation(out, in_, func, **kw), rd, wr)

    def tt(self, eng, out, a, b, op):
        return self._add(eng, lambda e: e.tensor_tensor(out, a, b, op), [a, b], [out])

    def ts(self, eng, out, a, s1, s2, op0, op1=None, accum_out=None):
        rd = [a]
        if not isinstance(s1, (int, float)):
            rd.append(s1)
        if s2 is not None and not isinstance(s2, (int, float)):
            rd.append(s2)
        wr = [out]
        kw = {}
        if accum_out is not None:
            kw["accum_out"] = accum_out
            wr.append(accum_out)
        if op1 is None:
            return self._add(eng, lambda e: e.tensor_scalar(out, a, s1, None, op0, **kw), rd, wr)
        return self._add(eng, lambda e: e.tensor_scalar(out, a, s1, s2, op0, op1, **kw), rd, wr)

    def stt(self, eng, out, a, s, b, op0, op1):
        rd = [a, b]
        if not isinstance(s, (int, float)):
            rd.append(s)
        return self._add(eng, lambda e: e.scalar_tensor_tensor(out, a, s, b, op0, op1), rd, [out])

    def copy(self, eng, out, in_):
        if eng == "act":
            return self._add("act", lambda e: e.copy(out, in_), [in_], [out])
        return self._add(eng, lambda e: e.tensor_copy(out, in_), [in_], [out])

    def memset(self, eng, out, val):
        return self._add(eng, lambda e: e.memset(out, val), [], [out])

    def recip(self, out, in_):
        return self._add("dve", lambda e: e.reciprocal(out, in_), [in_], [out])

    def dma(self, q, out, in_, **kw):
        return self._add(q, lambda e: e.dma_start(out, in_, **kw), [in_], [out], is_dma=True)

    def custom(self, eng, emit, reads, writes, is_dma=False, dma_inc=16):
        return self._add(eng, emit, reads, writes, is_dma=is_dma, dma_inc=dma_inc)

    def emit(self, final_wait_all_dma=True):
        nc = self.nc
        with ExitStack() as st:
            esem = {e: st.enter_context(nc.semaphore("se_" + e)) for e in ENGS}
            dsem = {e: [st.enter_context(nc.semaphore("sd_%s%d" % (e, i))) for i in range(N_DMA_SEMS)]
                    for e in ENGS}
            sigcount = {}
            for e in ENGS:
                c = 0
                arr = []
                for op in self.ops[e]:
                    if (not op.is_dma) and op.signal:
                        c += 1
                    arr.append(c)
                sigcount[e] = arr
            block = st.enter_context(nc.Block())
            engobj = {"pe": block.tensor, "act": block.scalar, "dve": block.vector,
                      "pool": block.gpsimd, "sp": block.sync}

            def make(e):
                def body(eng):
                    for op in self.ops[e]:
                        for key, val in op.waits.items():
                            if key[0] == "E":
                                eng.wait_ge(esem[key[1]], sigcount[key[1]][val])
                            else:
                                eng.wait_ge(dsem[key[1]][key[2]], val * 16)
                        ins = op.emit(eng)
                        if op.is_dma:
                            ins.then_inc(dsem[e][op.dsem], op.dma_inc)
                        elif op.signal:
                            ins.then_inc(esem[e], 1)
                    for q in range(N_DMA_SEMS):
                        if self.dma_count[e][q] > 0:
                            eng.wait_ge(dsem[e][q], self.dma_count[e][q] * 16)
                return body

            for e in ENGS:
                engobj[e](make(e))

EPS = 1e-6
NT = 2304
CORES = 8
GRID_W = 64


def chunks(total, step=512):
    out = []
    c = 0
    while c < total:
        out.append((c, min(step, total - c)))
        c += step
    return out


class LB:
    def __init__(self):
        self.nc = bass.Bass("TRN2", target_bir_lowering=False)
        self.P = Prog(self.nc)
        self.ps = [self.nc.alloc_psum_tensor("ps%d" % i, [128, 512], F32) for i in range(8)]
        self.in_names = []
        self.out_names = []
        self.dq = 0
        P = self.P
        self.ident = self.sb("ident", [128, 128], F32)
        P.dma("sp", self.ident[:], self.inp("ident_in", [128, 128]))
        self.ones_bf = self.sb("ones_bf", [128, 128], BF16)
        P.memset("dve", self.ones_bf[:], 1.0)
        self.ones_f = self.sb("ones_f", [128, 128], F32)
        P.memset("dve", self.ones_f[:], 1.0)
        self.neghalf = self.sb("neghalf", [128, 512], F32)
        P.memset("pool", self.neghalf[:], -0.5)
        self.sqb = self.sb("sqb", [128, 8, 512], BF16)
        self.rstd = self.sb("rstd", [128, 512], F32)
        self.tmpf = [self.sb("tmpf%d" % i, [128, 512], F32) for i in range(4)]
        self.tmpi = 0
        self.rr = 0

    def inp(self, name, shape, dt=F32):
        t = self.nc.dram_tensor(name, list(shape), dt, kind="ExternalInput")
        self.P.notrack.add(name)
        self.in_names.append(name)
        return t.ap()

    def outp(self, name, shape, dt=F32):
        t = self.nc.dram_tensor(name, list(shape), dt, kind="ExternalOutput")
        self.out_names.append(name)
        return t.ap()

    def sb(self, name, shape, dt):
        return self.nc.alloc_sbuf_tensor(name, list(shape), dt)

    def q(self):
        self.dq ^= 1
        return "sp" if self.dq else "act"

    def tmp(self):
        self.tmpi = (self.tmpi + 1) % 4
        return self.tmpf[self.tmpi]

    def ve(self):
        self.rr ^= 1
        return "dve" if self.rr else "pool"

    def rsqrt(self, out, in_ps, addc):
        self.P.ts("dve", out, in_ps, float(addc), None, ALU.add)
        shp = list(out.shape)
        self.P.tt("pool", out, out, self.neghalf[:shp[0], :shp[1]], ALU.pow)

    def load_cols(self, name, shape):
        t = self.sb(name + "_sb", shape, F32)
        self.P.dma(self.q(), t[:], self.inp(name, shape))
        return t

    def setup_c(self):
        P = self.P
        cv = self.load_cols("cvec", [128, 8, 2])
        self.scT = self.sb("scT", [128, 8, 2], F32)
        P.act(self.scT[:], cv[:], AF.Silu)
        sw = self.slabw = getattr(self, "slabw", 256)
        self.wslab = [self.af[:, i * 8 * sw:(i + 1) * 8 * sw].rearrange("p (k n) -> p k n", k=8) for i in range(2)]

    def compute_mod(self, i):
        P = self.P
        wm = self.inp("wmod%d" % i, [1024, 6144]).rearrange("(k p) n -> p k n", p=128)
        bm = self.load_cols("bmodc%d" % i, [128, 48])
        mod = self.sb("mod%d" % i, [128, 48, 2], F32)
        psm = self.ps[7]
        sw = self.slabw
        for s in range(6144 // sw):
            slab = self.wslab[s % 2]
            P.dma(self.q(), slab[:], wm[:, :, s * sw:(s + 1) * sw])
            for jj in range(sw // 128):
                j = s * (sw // 128) + jj
                for k in range(8):
                    P.mm(psm[:, j * 2:(j + 1) * 2], slab[:, k, jj * 128:(jj + 1) * 128], self.scT[:, k, :],
                         start=(k == 0), stop=(k == 7))
        psv = psm[:, 0:96].rearrange("p (j s) -> p j s", s=2)
        for s in range(2):
            P.tt("dve", mod[:, :, s], psv[:, :, s], bm[:, :], ALU.add)
        return mod

    def make_gain(self, name, mod, m, ng):
        P = self.P
        G = self.sb(name, [128, 8, 2], F32)
        for s in range(2):
            P.ts("dve", G[:, :, s], mod[:, m * 8:(m + 1) * 8, s], 1.0, 32.0, ALU.add, ALU.mult)
            P.tt("dve", G[:, :, s], G[:, :, s], ng[:, :], ALU.mult)
        return G

    def norm_block(self, xT, c0, n, G, mod, msh, s, out_fn, psb=6):
        P = self.P
        P.act(self.sqb[:, :, :n], xT[:, :, c0:c0 + n], AF.Square)
        ps = self.ps[psb]
        for k in range(8):
            P.mm(ps[:, :n], self.ones_bf[:], self.sqb[:, k, :n], start=(k == 0), stop=(k == 7))
        self.rsqrt(self.rstd[:, :n], ps[:, :n], 1024.0 * EPS)
        for k in range(8):
            t = self.tmp()
            P.tt(self.ve(), t[:, :n], xT[:, k, c0:c0 + n], self.rstd[:, :n], ALU.mult)
            P.act(out_fn(k), t[:, :n], AF.Identity, scale=G[:, k, s:s + 1],
                  bias=mod[:, msh * 8 + k, s:s + 1])

    def wload(self, dst, src):
        self.P.dma("pool", dst, src)

    def setup_wslots(self, n=4):
        self.wslots = [self.sb("wslot%d" % i, [128, 4096], BF16) for i in range(n)]

    def ffn(self, xT, hT, blocks, i, mod, G2):
        P = self.P
        w1 = self.inp("wff1_%d" % i, [1024, 4096]).rearrange("(k p) n -> p k n", p=128)
        w2 = self.inp("wff2_%d" % i, [4096, 1024])
        for (xc0, hc0, n, s) in blocks:
            self.norm_block(xT, xc0, n, G2, mod, 3, s, lambda k: hT[:, k, hc0:hc0 + n])
        uT = [self.uT0, self.uT1]

        def issue(g):
            a = self.wslots[(g % 2) * 2][:, :].rearrange("p (k n) -> p k n", k=8)
            b = self.wslots[(g % 2) * 2 + 1][:, :].rearrange("p (k n) -> p k n", k=4)
            self.wload(a, w1[:, :, g * 512:(g + 1) * 512])
            self.wload(b, w2[g * 512:(g + 1) * 512, :].rearrange("(k p) n -> p k n", p=128))
            return a, b

        nxt = issue(0)
        cnt = 0
        for g in range(8):
            W1g, W2g = nxt
            if g + 1 < 8:
                nxt = issue(g + 1)
            for (xc0, hc0, n, s) in blocks:
                u = uT[cnt % 2]
                cnt += 1
                for fc in range(4):
                    ps = self.ps[fc]
                    for k in range(8):
                        P.mm(ps[:, :n], W1g[:, k, fc * 128:(fc + 1) * 128], hT[:, k, hc0:hc0 + n],
                             start=(k == 0), stop=(k == 7))
                    t = self.tmp()
                    P.act(t[:, :n], ps[:, :n], AF.Relu)
                    P.tt("pool", u[:, fc, :n], t[:, :n], t[:, :n], ALU.mult)
                for j in range(8):
                    ps = self.ps[4 + (j % 3)]
                    for fc in range(4):
                        P.mm(ps[:, :n], W2g[:, fc, j * 128:(j + 1) * 128], u[:, fc, :n],
                             start=(fc == 0), stop=(fc == 3))
                    P.stt("dve", xT[:, j, xc0:xc0 + n], ps[:, :n], mod[:, 40 + j, s:s + 1],
                          xT[:, j, xc0:xc0 + n], ALU.mult, ALU.add)

    def qk_norm_rope(self, ps_raw, rows, n, gcol, blk_ones, hd, perm, C, S, out_ap, ps_ss, ps_sw):
        P = self.P
        qg = self.tmp()
        P.act(qg[:rows, :n], ps_raw[:rows, :n], AF.Identity, scale=gcol[:rows, 0:1])
        sq = self.sqb[:, 0, :]
        P.act(sq[:rows, :n], ps_raw[:rows, :n], AF.Square)
        P.mm(ps_ss[:rows, :n], blk_ones[:rows, :rows], sq[:rows, :n])
        rs = self.tmp()
        self.rsqrt(rs[:rows, :n], ps_ss[:rows, :n], float(hd) * EPS)
        if C is None:
            P.stt("dve", out_ap, qg[:rows, :n], float(hd) ** 0.5, rs[:rows, :n], ALU.mult, ALU.mult)
            return
        qb = self.sqb[:, 1, :]
        P.copy("pool", qb[:rows, :n], qg[:rows, :n])
        P.mm(ps_sw[:rows, :n], perm[:rows, :rows], qb[:rows, :n])
        t1 = self.tmp()
        P.tt("dve", t1[:rows, :n], qg[:rows, :n], C, ALU.mult)
        t2 = self.tmp()
        P.tt("dve", t2[:rows, :n], ps_sw[:rows, :n], S, ALU.mult)
        P.tt("pool", t1[:rows, :n], t1[:rows, :n], t2[:rows, :n], ALU.add)
        P.stt("dve", out_ap, t1[:rows, :n], float(hd) ** 0.5, rs[:rows, :n], ALU.mult, ALU.mult)

    def finish(self):
        self.P.emit()
        return self.nc


NTA = 2336
UW = 2368


def build_A():
    L = LB()
    P = L.P
    nc = L.nc
    x_tok = L.inp("x_tok", [NTA, 1024])
    xT = L.sb("xT", [128, 8, NTA], F32)
    abf = L.sb("abf", [128, 39424], BF16)
    af = L.sb("af", [128, 4096], F32)
    L.af = af
    vblk = af[:, :].rearrange("p (k n) -> p k n", k=8)
    xst = [af[:, i * 1024:(i + 1) * 1024] for i in range(2)]
    for r in range(19):
        rows = 128 if r < 18 else 32
        st = xst[r % 2]
        P.dma(L.q(), st[:rows, :], x_tok[r * 128:r * 128 + rows, :])
        for half in range(2):
            ps = L.ps[(r * 2 + half) % 4]
            for kk in range(4):
                k = half * 4 + kk
                P.transpose(ps[:, kk * 128:kk * 128 + rows], st[:rows, k * 128:(k + 1) * 128], L.ident[:rows, :rows])
            src = ps[:, 0:512].rearrange("p (a b) -> p a b", a=4)[:, :, :rows]
            dst = xT[:, half * 4:(half + 1) * 4, r * 128:r * 128 + rows]
            P.copy("act" if half else "dve", dst, src)
    L.setup_c()
    mod0 = L.compute_mod(0)
    mod1 = L.compute_mod(1)
    n1g0 = L.load_cols("n1g0", [128, 8])
    n2g0 = L.load_cols("n2g0", [128, 8])
    n1g1 = L.load_cols("n1g1", [128, 8])
    G1_0 = L.make_gain("G1_0", mod0, 1, n1g0)
    G2_0 = L.make_gain("G2_0", mod0, 4, n2g0)
    G1_1 = L.make_gain("G1_1", mod1, 1, n1g1)
    pw1 = abf[:, 0:16384].rearrange("p (k n) -> p k n", k=8)
    hblk = [abf[:, 16384:20480].rearrange("p (k n) -> p k n", k=8) for i in range(2)]
    U = abf[:, 20480:20480 + 8 * UW].rearrange("p (k n) -> p k n", k=8)
    w_pw1 = L.inp("conv_pw1", [1024, 2048]).rearrange("(k p) n -> p k n", p=128)
    for s in range(4):
        L.wload(pw1[:, :, s * 512:(s + 1) * 512], w_pw1[:, :, s * 512:(s + 1) * 512])
    hmask = L.load_cols("hmask", [128, 32])
    wdw = L.load_cols("conv_dwc", [128, 8, 31])
    bdw = L.load_cols("conv_bdwc", [128, 8])
    lng = L.load_cols("conv_lngc", [128, 8])
    lnb = L.load_cols("conv_lnbc", [128, 8])
    lng32 = L.sb("lng32", [128, 8], F32)
    P.ts("dve", lng32[:], lng[:], 32.0, None, ALU.mult)
    ident_bf = L.sb("ident_bf", [128, 128], BF16)
    P.copy("dve", ident_bf[:], L.ident[:])
    P.memset("pool", U[:, :, 2080:2096], 0.0)
    P.memset("pool", U[:, :, 2352:2368], 0.0)
    blocksA = [(0, 512, 0, 0), (512, 512, 512, 0), (1024, 512, 1024, 0), (1536, 512, 1536, 0),
               (2048, 32, 2048, 0), (2080, 256, 2096, 1)]
    for bi, (c0, n, uc0, s) in enumerate(blocksA):
        hb = hblk[bi % 2]
        L.norm_block(xT, c0, n, G1_0, mod0, 0, s, lambda k: hb[:, k, :n])
        for j in range(8):
            psa = L.ps[(j % 2) * 2]
            psg = L.ps[(j % 2) * 2 + 1]
            for k in range(8):
                P.mm(psa[:, :n], pw1[:, k, j * 128:(j + 1) * 128], hb[:, k, :n], start=(k == 0), stop=(k == 7))
            for k in range(8):
                P.mm(psg[:, :n], pw1[:, k, 1024 + j * 128:1024 + (j + 1) * 128], hb[:, k, :n],
                     start=(k == 0), stop=(k == 7))
            sg = L.tmp()
            P.act(sg[:, :n], psg[:, :n], AF.Sigmoid)
            P.tt("dve", U[:, j, uc0:uc0 + n], psa[:, :n], sg[:, :n], ALU.mult)
    for j in range(8):
        P.tt("pool", U[:, j, 0:16], U[:, j, 0:16], hmask[:, 0:16], ALU.mult)
        P.tt("pool", U[:, j, 2064:2080], U[:, j, 2064:2080], hmask[:, 16:32], ALU.mult)
    pw2 = abf[:, 0:8192].rearrange("p (k n) -> p k n", k=8)
    w_pw2 = L.inp("conv_pw2", [1024, 1024]).rearrange("(k p) n -> p k n", p=128)
    for s in range(2):
        L.wload(pw2[:, :, s * 512:(s + 1) * 512], w_pw2[:, :, s * 512:(s + 1) * 512])
    zblk = abf[:, 8192:12288].rearrange("p (k n) -> p k n", k=8)
    diag = [abf[:, 12288 + i * 3968:12288 + (i + 1) * 3968].rearrange("p (t n) -> p t n", t=31) for i in range(2)]
    blocksC = [(16 + tb * 512, 512, 16 + tb * 512, 0) for tb in range(4)] + [(2080, 256, 2096, 1)]
    dcount = 0
    for (xc0, n, uc0, s) in blocksC:
        for j in range(8):
            dg = diag[dcount % 2]
            dcount += 1
            for tap in range(31):
                P.ts("dve", dg[:, tap, :], ident_bf[:], wdw[:, j, tap:tap + 1], None, ALU.mult)
            ps = L.ps[j % 4]
            for tap in range(31):
                P.mm(ps[:, :n], dg[:, tap, :], U[:, j, uc0 - 15 + tap:uc0 - 15 + tap + n],
                     start=(tap == 0), stop=(tap == 30))
            P.act(vblk[:, j, :n], ps[:, :n], AF.Identity, bias=bdw[:, j:j + 1])
        psm = L.ps[4]
        for j in range(8):
            P.mm(psm[:, :n], L.ones_f[:], vblk[:, j, :n], start=(j == 0), stop=(j == 7))
        mu = L.tmp()
        P.ts("dve", mu[:, :n], psm[:, :n], 1.0 / 1024.0, None, ALU.mult)
        for j in range(8):
            P.tt(L.ve(), vblk[:, j, :n], vblk[:, j, :n], mu[:, :n], ALU.subtract)
        P.act(L.sqb[:, :, :n], vblk[:, :, :n], AF.Square)
        psv = L.ps[5]
        for j in range(8):
            P.mm(psv[:, :n], L.ones_bf[:], L.sqb[:, j, :n], start=(j == 0), stop=(j == 7))
        L.rsqrt(L.rstd[:, :n], psv[:, :n], 1024.0 * EPS)
        for j in range(8):
            t = L.tmp()
            P.tt(L.ve(), t[:, :n], vblk[:, j, :n], L.rstd[:, :n], ALU.mult)
            P.act(zblk[:, j, :n], t[:, :n], AF.Silu, scale=lng32[:, j:j + 1], bias=lnb[:, j:j + 1])
        for j in range(8):
            ps = L.ps[6 + (j % 2)]
            for k in range(8):
                P.mm(ps[:, :n], pw2[:, k, j * 128:(j + 1) * 128], zblk[:, k, :n], start=(k == 0), stop=(k == 7))
            P.stt("dve", xT[:, j, xc0:xc0 + n], ps[:, :n], mod0[:, 16 + j, s:s + 1],
                  xT[:, j, xc0:xc0 + n], ALU.mult, ALU.add)
    hT = abf[:, 0:8 * NT].rearrange("p (k n) -> p k n", k=8)
    L.wslots = [abf[:, 18432 + i * 4096:18432 + (i + 1) * 4096] for i in range(4)]
    L.uT0 = abf[:, 34816:36864].rearrange("p (k n) -> p k n", k=4)
    L.uT1 = abf[:, 36864:38912].rearrange("p (k n) -> p k n", k=4)
    blocksF = [(16 + tb * 512, tb * 512, 512, 0) for tb in range(4)] + [(2080, 2048, 256, 1)]
    L.ffn(xT, hT, blocksF, 0, mod0, G2_0)
    for (xc0, hc0, n, s) in blocksF:
        L.norm_block(xT, xc0, n, G1_1, mod1, 0, s, lambda k: hT[:, k, hc0:hc0 + n])
    wkv = L.wslots[0][:, :].rearrange("p (k n) -> p k n", k=8)
    w_qkv = L.inp("gqa_wqkv", [1024, 1536]).rearrange("(k p) n -> p k n", p=128)
    L.wload(wkv[:, :, :], w_qkv[:, :, 1024:1536])
    gk = L.load_cols("gqa_kgc", [128, 1])
    ropeC_d = L.inp("ropeC64", [128, 2048])
    ropeS_d = L.inp("ropeS64", [128, 2048])
    ropeC = af[:, 3072:3584]
    ropeS = af[:, 3584:4096]
    permf = L.load_cols("perm64", [128, 128])
    perm = L.sb("perm_bf", [128, 128], BF16)
    P.copy("dve", perm[:], permf[:])
    blk1 = L.sb("blk64", [128, 128], BF16)
    P.memset("dve", blk1[:], 0.0)
    P.memset("dve", blk1[0:64, 0:64], 1.0)
    P.memset("dve", blk1[64:128, 64:128], 1.0)
    kT_out = L.outp("kT_out", [128, 2, NT])
    v_out = L.outp("v_out", [128, 18, 256])
    kst = [af[:, i * 512:(i + 1) * 512] for i in range(2)]
    vst = [af[:, 1024 + i * 1024:2048 + i * 1024].rearrange("p (k n) -> p k n", k=4) for i in range(2)]
    cnt = 0
    for (xc0, hc0, n, s) in blocksF:
        if s == 0:
            P.dma("sp", ropeC[:, :n], ropeC_d[:, hc0:hc0 + n])
            P.dma("act", ropeS[:, :n], ropeS_d[:, hc0:hc0 + n])
        for m in range(2):
            ps = L.ps[m]
            for k in range(8):
                P.mm(ps[:, :n], wkv[:, k, m * 128:(m + 1) * 128], hT[:, k, hc0:hc0 + n], start=(k == 0), stop=(k == 7))
            ko = kst[cnt % 2]
            cnt += 1
            if s == 0:
                L.qk_norm_rope(ps, 128, n, gk, blk1, 64, perm, ropeC[:, :n], ropeS[:, :n],
                               ko[:, :n], L.ps[2], L.ps[3])
            else:
                L.qk_norm_rope(ps, 128, n, gk, blk1, 64, None, None, None, ko[:, :n], L.ps[2], L.ps[3])
            P.dma(L.q(), kT_out[:, m, hc0:hc0 + n], ko[:, :n])
        vs = vst[(hc0 // 512) % 2]
        nt = n // 128
        for tt_ in range(nt):
            ps = L.ps[4 + (tt_ % 2)]
            for k in range(8):
                P.mm(ps[:, 0:256], hT[:, k, hc0 + tt_ * 128:hc0 + (tt_ + 1) * 128], wkv[:, k, 256:512],
                     start=(k == 0), stop=(k == 7))
            P.copy("act", vs[:, tt_, :], ps[:, 0:256])
        P.dma(L.q(), v_out[:, hc0 // 128:hc0 // 128 + nt, :], vs[:, :nt, :])
    xT_out = L.outp("xT_out", [128, 8, NT])
    P.dma("sp", xT_out[:, :, 0:2048], xT[:, :, 16:2064])
    P.dma("act", xT_out[:, :, 2048:2304], xT[:, :, 2080:2336])
    return L


def cols(v, k):
    return np.ascontiguousarray(np.asarray(v, np.float32).reshape(k, 128).T)


def rope_tables(rot, pos0, n):
    q = rot // 4
    inv = (10000.0 ** (-np.arange(q, dtype=np.float32) / q)).astype(np.float32)
    t = np.arange(pos0, pos0 + n)
    row = (t // GRID_W).astype(np.float32)
    col = (t % GRID_W).astype(np.float32)
    ar = (inv[:, None] * row[None, :]).astype(np.float32)
    ac = (inv[:, None] * col[None, :]).astype(np.float32)
    C = np.concatenate([np.cos(ar), np.cos(ar), np.cos(ac), np.cos(ac)], 0).astype(np.float32)
    S = np.concatenate([-np.sin(ar), np.sin(ar), -np.sin(ac), np.sin(ac)], 0).astype(np.float32)
    return C, S


def perm_matrix(rot, reps, base=0, size=128):
    Pm = np.zeros((size, size), np.float32)
    q = rot // 4
    for r in range(reps):
        for i in range(rot):
            g = i // (2 * q)
            j = i % (2 * q)
            pj = j + q if j < q else j - q
            Pm[base + r * rot + g * 2 * q + pj, base + r * rot + i] = 1.0
    return Pm


_PROG_CACHE = {}


def get_prog(name, builder):
    if name not in _PROG_CACHE:
        L = builder()
        L.finish()
        _PROG_CACHE[name] = L
    return _PROG_CACHE[name]


def run_launch(name, builder, in_maps):
    L = get_prog(name, builder)
    maps = [{k: np.ascontiguousarray(m[k], dtype=np.float32) for k in L.in_names} for m in in_maps]
    res = run_bass_kernel_spmd(L.nc, maps, core_ids=list(range(CORES)))
    return res.results


def prep_A(I, c):
    b, half = c // 2, c % 2
    st = half * 2048
    x = I["x"][b]
    z16 = np.zeros((16, 1024), np.float32)
    left = x[st - 16:st] if half == 1 else z16
    right = x[st + 2048:st + 2064] if half == 0 else z16
    x_tok = np.concatenate([left, x[st:st + 2048], right, I["ctx"][b]], 0)
    cvec = np.stack([cols(I["c"][b], 8), cols(I["c_ctx"], 8)], -1)
    hmask = np.zeros((128, 32), np.float32)
    hmask[:, 0:16] = 1.0 if half == 1 else 0.0
    hmask[:, 16:32] = 1.0 if half == 0 else 0.0
    C, S = rope_tables(64, st, 2048)
    d = {
        "ident_in": np.eye(128, dtype=np.float32),
        "x_tok": x_tok, "cvec": cvec, "hmask": hmask,
        "wmod0": I["w_mod"][0], "wmod1": I["w_mod"][1],
        "bmodc0": cols(I["b_mod"][0], 48), "bmodc1": cols(I["b_mod"][1], 48),
        "n1g0": cols(I["norm1_g"][0], 8), "n2g0": cols(I["norm2_g"][0], 8), "n1g1": cols(I["norm1_g"][1], 8),
        "conv_pw1": I["conv_w_pw1"][0], "conv_pw2": I["conv_w_pw2"][0],
        "conv_dwc": np.ascontiguousarray(I["conv_w_dw"][0].T.reshape(8, 128, 31).transpose(1, 0, 2)),
        "conv_bdwc": cols(I["conv_b_dw"][0], 8), "conv_lngc": cols(I["conv_ln_g"][0], 8),
        "conv_lnbc": cols(I["conv_ln_b"][0], 8),
        "wff1_0": I["w_ff1"][0], "wff2_0": I["w_ff2"][0],
        "gqa_wqkv": I["gqa_w_qkv"][0],
        "gqa_kgc": np.tile(I["gqa_k_norm"][0], 2).reshape(128, 1),
        "ropeC64": np.tile(C, (2, 1)), "ropeS64": np.tile(S, (2, 1)),
        "perm64": perm_matrix(64, 2),
    }
    return d


def attend_head(L, k_lhsT_fn, q_rhs, v_lhsT_fn, lcs, n, scale, ps_o, side, out_ap, ptiles, ebias=None):
    P = L.P
    nl = len(lcs)
    LOOK = 2

    def qk(idx):
        ps_s = L.ps[idx % 3]
        P.mm(ps_s[:, :n], k_lhsT_fn(lcs[idx]), q_rhs)

    for idx in range(min(LOOK, nl)):
        qk(idx)
    for idx in range(nl):
        if idx + LOOK < nl:
            qk(idx + LOOK)
        pt = ptiles[idx % 3]
        P.act(pt[:, :n], L.ps[idx % 3][:, :n], AF.Exp, scale=scale, bias=ebias)
        P.mm(ps_o[:, :n], v_lhsT_fn(lcs[idx]), pt[:, :n], start=(idx == 0), stop=(idx == nl - 1))
    sl0, sl1 = (0, 64) if side == 0 else (64, 128)
    drow = 64 if side == 0 else 0
    o_sb = L.tmp()
    P.copy("dve", o_sb[:, :n], ps_o[:, :n])
    rd = L.tmp()
    P.recip(rd[drow:drow + 1, :n], o_sb[drow:drow + 1, :n])
    ps_b = L.ps[5]
    P.mm(ps_b[:, :n], L.ones_f[drow:drow + 1, 0:128], rd[drow:drow + 1, :n])
    P.tt("dve", out_ap, o_sb[sl0:sl1, :n], ps_b[sl0:sl1, :n], ALU.mult)


def build_B():
    L = LB()
    P = L.P
    xT = L.sb("xT", [128, 8, NT], F32)
    xT_in = L.inp("xT_in", [128, 8, NT])
    for k in range(8):
        P.dma(L.q(), xT[:, k, :], xT_in[:, k, :])
    abf = L.sb("abf", [128, 50432], BF16)
    af = L.sb("af", [128, 2048], F32)
    L.af = af
    L.slabw = 128
    L.setup_c()
    mod1 = L.compute_mod(1)
    n1g = L.load_cols("n1g1", [128, 8])
    n2g = L.load_cols("n2g1", [128, 8])
    G1 = L.make_gain("G1", mod1, 1, n1g)
    G2 = L.make_gain("G2", mod1, 4, n2g)
    K_all = abf[:, 0:8704].rearrange("p (m n) -> p m n", m=2)
    Vext = abf[:, 8704:21760].rearrange("p (c n) -> p c n", c=34)
    Wq = abf[:, 21760:29952].rearrange("p (k n) -> p k n", k=8)
    Wo = abf[:, 29952:38144].rearrange("p (k n) -> p k n", k=8)
    hblk = abf[:, 38144:42240].rearrange("p (k n) -> p k n", k=8)
    QTb = abf[:, 42240:46336].rearrange("p (k n) -> p k n", k=8)
    OTb = abf[:, 46336:50432].rearrange("p (k n) -> p k n", k=8)
    ptiles = [L.sb("ptile%d" % i, [128, 512], BF16) for i in range(3)]
    kT_own = L.inp("kT_own", [128, 2, NT])
    kT_oth = L.inp("kT_oth", [128, 2, 2048])
    v_own = L.inp("v_own", [128, 18, 256])
    v_oth = L.inp("v_oth", [128, 16, 256])
    for m in range(2):
        L.wload(K_all[:, m, 0:NT], kT_own[:, m, :])
        L.wload(K_all[:, m, NT:4352], kT_oth[:, m, :])
    P.memset("dve", Vext[:, :, 64:128], 0.0)
    P.memset("dve", Vext[:, :, 256:320], 0.0)
    P.memset("dve", Vext[:, :, 64:65], 1.0)
    P.memset("dve", Vext[:, :, 256:257], 1.0)
    for (c0, c1, src) in ((0, 18, v_own), (18, 34, v_oth)):
        L.wload(Vext[:, c0:c1, 0:64], src[:, :, 0:64])
        L.wload(Vext[:, c0:c1, 128:256], src[:, :, 64:192])
        L.wload(Vext[:, c0:c1, 320:384], src[:, :, 192:256])
    vcol = [0, 64, 192, 256]
    w_qkv = L.inp("gqa_wqkv", [1024, 1536]).rearrange("(k p) n -> p k n", p=128)
    w_o = L.inp("gqa_wo", [1024, 1024])
    for m in range(2):
        for i in range(4):
            pb = m * 4 + i
            for side in range(2):
                h = 8 * m + 4 * side + i
                L.wload(Wq[:, :, pb * 128 + side * 64:pb * 128 + (side + 1) * 64], w_qkv[:, :, h * 64:(h + 1) * 64])
                L.wload(Wo[side * 64:(side + 1) * 64, pb, :], w_o[h * 64:(h + 1) * 64, :])
    gq = L.load_cols("gqa_qgc", [128, 1])
    permf = L.load_cols("perm64", [128, 128])
    perm = L.sb("perm_bf", [128, 128], BF16)
    P.copy("dve", perm[:], permf[:])
    blk1 = L.sb("blk64", [128, 128], BF16)
    P.memset("dve", blk1[:], 0.0)
    P.memset("dve", blk1[0:64, 0:64], 1.0)
    P.memset("dve", blk1[64:128, 64:128], 1.0)
    ropeC_d = L.inp("ropeC64", [128, 2048])
    ropeS_d = L.inp("ropeS64", [128, 2048])
    ropeC = af[:, 0:512]
    ropeS = af[:, 512:1024]
    blocks = [(tb * 512, 512, 0) for tb in range(4)] + [(2048, 256, 1)]
    hcount = 0
    for (c0, n, s) in blocks:
        L.norm_block(xT, c0, n, G1, mod1, 0, s, lambda k: hblk[:, k, :n], psb=7)
        if s == 0:
            P.dma("sp", ropeC[:, :n], ropeC_d[:, c0:c0 + n])
            P.dma("act", ropeS[:, :n], ropeS_d[:, c0:c0 + n])
        for pb in range(8):
            ps = L.ps[6]
            for k in range(8):
                P.mm(ps[:, :n], Wq[:, k, pb * 128:(pb + 1) * 128], hblk[:, k, :n], start=(k == 0), stop=(k == 7))
            if s == 0:
                L.qk_norm_rope(ps, 128, n, gq, blk1, 64, perm, ropeC[:, :n], ropeS[:, :n], QTb[:, pb, :n],
                               L.ps[7], L.ps[5])
            else:
                L.qk_norm_rope(ps, 128, n, gq, blk1, 64, None, None, None, QTb[:, pb, :n], L.ps[7], L.ps[5])
        lcs = list(range(34)) if s == 0 else [16, 17]
        for pb in range(8):
            m = pb // 4
            for side in range(2):
                kv = 2 * m + side
                sl0, sl1 = (0, 64) if side == 0 else (64, 128)
                ps_o = L.ps[3 + (hcount % 2)]
                hcount += 1
                attend_head(L,
                            lambda lc: K_all[sl0:sl1, m, lc * 128:(lc + 1) * 128],
                            QTb[sl0:sl1, pb, :n],
                            lambda lc: Vext[:, lc, vcol[kv]:vcol[kv] + 128],
                            lcs, n, 0.125, ps_o, side, OTb[sl0:sl1, pb, :n], ptiles)
        for j in range(8):
            ps = L.ps[6 + (j % 2)]
            for pb in range(8):
                P.mm(ps[:, :n], Wo[:, pb, j * 128:(j + 1) * 128], OTb[:, pb, :n], start=(pb == 0), stop=(pb == 7))
            P.stt("dve", xT[:, j, c0:c0 + n], ps[:, :n], mod1[:, 16 + j, s:s + 1], xT[:, j, c0:c0 + n],
                  ALU.mult, ALU.add)
    hT = abf[:, 0:8 * NT].rearrange("p (k n) -> p k n", k=8)
    L.wslots = [abf[:, 18432 + i * 4096:18432 + (i + 1) * 4096] for i in range(4)]
    L.uT0 = abf[:, 34816:36864].rearrange("p (k n) -> p k n", k=4)
    L.uT1 = abf[:, 36864:38912].rearrange("p (k n) -> p k n", k=4)
    blocksF = [(c0, c0, n, s) for (c0, n, s) in blocks]
    L.ffn(xT, hT, blocksF, 1, mod1, G2)
    xT_out = L.outp("xT_out", [128, 8, NT])
    for k in range(8):
        P.dma(L.q(), xT_out[:, k, :], xT[:, k, :])
    return L


def prep_B(I, c, resA):
    b, half = c // 2, c % 2
    o = c ^ 1
    C, S = rope_tables(64, half * 2048, 2048)
    return {
        "ident_in": np.eye(128, dtype=np.float32),
        "xT_in": resA[c]["xT_out"],
        "cvec": np.stack([cols(I["c"][b], 8), cols(I["c_ctx"], 8)], -1),
        "wmod1": I["w_mod"][1], "bmodc1": cols(I["b_mod"][1], 48),
        "n1g1": cols(I["norm1_g"][1], 8), "n2g1": cols(I["norm2_g"][1], 8),
        "kT_own": resA[c]["kT_out"], "kT_oth": resA[o]["kT_out"][:, :, 0:2048],
        "v_own": resA[c]["v_out"], "v_oth": resA[o]["v_out"][:, 0:16, :],
        "gqa_wqkv": I["gqa_w_qkv"][0], "gqa_wo": I["gqa_w_o"][0],
        "gqa_qgc": np.tile(I["gqa_q_norm"][0], 2).reshape(128, 1),
        "perm64": perm_matrix(64, 2),
        "ropeC64": np.tile(C, (2, 1)), "ropeS64": np.tile(S, (2, 1)),
        "wff1_1": I["w_ff1"][1], "wff2_1": I["w_ff2"][1],
    }


NTC = 2320


def build_C():
    L = LB()
    P = L.P
    xT = L.sb("xT", [128, 8, NTC], F32)
    xT_in = L.inp("xT_in", [128, 8, NTC])
    for k in range(8):
        P.dma(L.q(), xT[:, k, :], xT_in[:, k, :])
    abf = L.sb("abf", [128, 38912], BF16)
    af = L.sb("af", [128, 7680], F32)
    L.af = af
    L.setup_c()
    mod2 = L.compute_mod(2)
    mod3 = L.compute_mod(3)
    n1g2 = L.load_cols("n1g2", [128, 8])
    n2g2 = L.load_cols("n2g2", [128, 8])
    n1g3 = L.load_cols("n1g3", [128, 8])
    G1 = L.make_gain("G1_2", mod2, 1, n1g2)
    G2 = L.make_gain("G2_2", mod2, 4, n2g2)
    G1_3 = L.make_gain("G1_3", mod3, 1, n1g3)
    pscale = L.load_cols("pool_scalec", [128, 8])
    psg = L.sb("psg", [128, 8, 2], F32)
    for s in range(2):
        P.tt("dve", psg[:, :, s], mod2[:, 16:24, s], pscale[:, :], ALU.mult)
    hmask = L.load_cols("hmask", [128, 16])
    invc = L.load_cols("invc", [128, 2, 4, 16])
    Wp = abf[:, 0:2048].rearrange("p (g c n) -> p g c n", g=4, c=2)
    pblk = abf[:, 2048:6144].rearrange("p (k n) -> p k n", k=8)
    w_pool = L.inp("pool_w", [4, 256, 256])
    for g in range(4):
        L.wload(Wp[:, g, :, :], w_pool[g].rearrange("(c p) n -> p c n", p=128))
    hE = af[:, 0:4224].rearrange("p (k n) -> p k n", k=8)
    wa = af[:, 4224:4752]
    wb = af[:, 4752:5280]
    rstd_all = af[:, 5280:7600]
    for (c0, n) in [(0, 512), (512, 512), (1024, 512), (1536, 512), (2048, 272)]:
        P.act(L.sqb[:, :, :n], xT[:, :, c0:c0 + n], AF.Square)
        ps = L.ps[6]
        for k in range(8):
            P.mm(ps[:, :n], L.ones_bf[:], L.sqb[:, k, :n], start=(k == 0), stop=(k == 7))
        L.rsqrt(rstd_all[:, c0:c0 + n], ps[:, :n], 1024.0 * EPS)
    blocksP = [(8 + tb * 512, 512, 0, tb) for tb in range(4)] + [(2064, 256, 1, 4)]
    for (c0, n, s, bi) in blocksP:
        ne = n + 16
        if s == 0:
            for k in range(8):
                t = L.tmp()
                for (a0, a1) in ((0, 512), (512, ne)):
                    P.tt(L.ve(), t[:, 0:a1 - a0], xT[:, k, c0 - 8 + a0:c0 - 8 + a1],
                         rstd_all[:, c0 - 8 + a0:c0 - 8 + a1], ALU.mult)
                    P.act(hE[:, k, a0:a1], t[:, 0:a1 - a0], AF.Identity, scale=G1[:, k, s:s + 1],
                          bias=mod2[:, k, s:s + 1])
                if bi == 0:
                    P.tt("pool", hE[:, k, 0:8], hE[:, k, 0:8], hmask[:, 0:8], ALU.mult)
                if bi == 3:
                    P.tt("pool", hE[:, k, ne - 8:ne], hE[:, k, ne - 8:ne], hmask[:, 8:16], ALU.mult)
        else:
            P.memset("pool", hE[:, :, 0:8], 0.0)
            P.memset("pool", hE[:, :, 8 + n:16 + n], 0.0)
            for k in range(8):
                t = L.tmp()
                P.tt(L.ve(), t[:, :n], xT[:, k, c0:c0 + n], rstd_all[:, c0:c0 + n], ALU.mult)
                P.act(hE[:, k, 8:8 + n], t[:, :n], AF.Identity, scale=G1[:, k, s:s + 1], bias=mod2[:, k, s:s + 1])
        for k in range(8):
            wi = k // 2
            w = 2 << wi
            eng = "dve" if k % 2 == 0 else "pool"
            cur = hE[:, k, :]
            ln_ = ne
            step = 1
            bufs = [wa, wb]
            bi_ = 0
            while step < w:
                nxt = bufs[bi_ % 2]
                bi_ += 1
                ln2 = ln_ - step
                P.tt(eng, nxt[:, 0:ln2], cur[:, 0:ln2], cur[:, step:step + ln2], ALU.add)
                cur = nxt
                ln_ = ln2
                step *= 2
            off = 8 - w // 2
            P.stt("dve", pblk[:, k, :n], cur[:, off:off + n], 1.0 / w, hE[:, k, 8:8 + n], ALU.mult, ALU.subtract)
            edges = []
            if s == 1 or bi == 0:
                edges.append((0, 0))
            if s == 1 or bi == 3:
                edges.append((n - 8, 8))
            for (t0, e0) in edges:
                t = L.tmp()
                P.tt("dve", t[:, 0:8], cur[:, off + t0:off + t0 + 8], invc[:, s, wi, e0:e0 + 8], ALU.mult)
                P.tt("dve", pblk[:, k, t0:t0 + 8], t[:, 0:8], hE[:, k, 8 + t0:16 + t0], ALU.subtract)
        for g in range(4):
            for oc in range(2):
                j = 2 * g + oc
                ps = L.ps[j % 4]
                for kc in range(2):
                    P.mm(ps[:, :n], Wp[:, g, kc, oc * 128:(oc + 1) * 128], pblk[:, 2 * g + kc, :n],
                         start=(kc == 0), stop=(kc == 1))
                P.stt("dve", xT[:, j, c0:c0 + n], ps[:, :n], psg[:, j, s:s + 1], xT[:, j, c0:c0 + n],
                      ALU.mult, ALU.add)
    hT = abf[:, 0:8 * NT].rearrange("p (k n) -> p k n", k=8)
    L.wslots = [abf[:, 18432 + i * 4096:18432 + (i + 1) * 4096] for i in range(4)]
    L.uT0 = abf[:, 34816:36864].rearrange("p (k n) -> p k n", k=4)
    L.uT1 = abf[:, 36864:38912].rearrange("p (k n) -> p k n", k=4)
    blocksF = [(8 + tb * 512, tb * 512, 512, 0) for tb in range(4)] + [(2064, 2048, 256, 1)]
    L.ffn(xT, hT, blocksF, 2, mod2, G2)
    xT_out = L.outp("xT_out", [128, 8, NT])
    P.dma("sp", xT_out[:, :, 0:2048], xT[:, :, 8:2056])
    P.dma("act", xT_out[:, :, 2048:2304], xT[:, :, 2064:2320])
    for (xc0, hc0, n, s) in blocksF:
        L.norm_block(xT, xc0, n, G1_3, mod3, 0, s, lambda k: hT[:, k, hc0:hc0 + n])
    Wd = abf[:, 18432:21504].rearrange("p (k n) -> p k n", k=8)
    Wukv = abf[:, 21504:25600].rearrange("p (c n) -> p c n", c=2)
    ckvn = abf[:, 25600:26624].rearrange("p (c n) -> p c n", c=2)
    krg_bf = abf[:, 26624:27136]
    sq_r = abf[:, 27136:27648]
    sq_n = abf[:, 27648:28160]
    w_dkv = L.inp("mla_wdkv", [1024, 288]).rearrange("(k p) n -> p k n", p=128)
    w_ukv = L.inp("mla_wukv", [256, 2048]).rearrange("(c p) n -> p c n", p=128)
    P.memset("dve", Wd[:, :, 256:384], 0.0)
    L.wload(Wd[:, :, 0:256], w_dkv[:, :, 0:256])
    L.wload(Wd[:, :, 320:352], w_dkv[:, :, 256:288])
    L.wload(Wukv[:, :, :], w_ukv)
    gkvl = L.load_cols("mla_kvlgc", [128, 2])
    gk96 = L.load_cols("mla_kgc", [128, 1])
    permf = L.load_cols("perm96", [128, 128])
    perm = L.sb("perm_bf", [128, 128], BF16)
    P.copy("dve", perm[:], permf[:])
    ropeC_d = L.inp("ropeC96", [128, 2048])
    ropeS_d = L.inp("ropeS96", [128, 2048])
    ropeC = af[:, 0:512]
    ropeS = af[:, 512:1024]
    krr = af[:, 1024:1536]
    khs = [af[:, 1536:2048], af[:, 2048:2560]]
    vst = af[:, 2560:6656].rearrange("p (t n) -> p t n", t=4)
    kT3_out = L.outp("kT3_out", [96, 16, NT])
    v3_out = L.outp("v3_out", [128, 18, 1024])
    Wukv4 = Wukv.rearrange("p c (h n) -> p c h n", h=16)
    kcnt = 0
    for (xc0, hc0, n, s) in blocksF:
        if s == 0:
            P.dma("sp", ropeC[:, :n], ropeC_d[:, hc0:hc0 + n])
            P.dma("act", ropeS[:, :n], ropeS_d[:, hc0:hc0 + n])
        for cc in range(2):
            ps = L.ps[cc]
            for k in range(8):
                P.mm(ps[:, :n], Wd[:, k, cc * 128:(cc + 1) * 128], hT[:, k, hc0:hc0 + n], start=(k == 0), stop=(k == 7))
            P.act(L.sqb[:, cc, :n], ps[:, :n], AF.Square)
        pss = L.ps[2]
        for cc in range(2):
            P.mm(pss[:, :n], L.ones_bf[:], L.sqb[:, cc, :n], start=(cc == 0), stop=(cc == 1))
        rs = L.tmp()
        L.rsqrt(rs[:, :n], pss[:, :n], 256.0 * EPS)
        for cc in range(2):
            t = L.tmp()
            P.act(t[:, :n], L.ps[cc][:, :n], AF.Identity, scale=gkvl[:, cc:cc + 1])
            P.stt("dve", ckvn[:, cc, :n], t[:, :n], 16.0, rs[:, :n], ALU.mult, ALU.mult)
        psr = L.ps[3]
        for k in range(8):
            P.mm(psr[:, :n], Wd[:, k, 256:384], hT[:, k, hc0:hc0 + n], start=(k == 0), stop=(k == 7))
        P.act(sq_r[64:96, :n], psr[64:96, :n], AF.Square)
        krg = L.tmp()
        P.act(krg[64:96, :n], psr[64:96, :n], AF.Identity, scale=gk96[64:96, 0:1])
        if s == 0:
            P.copy("pool", krg_bf[64:96, :n], krg[64:96, :n])
            psw = L.ps[4]
            P.mm(psw[0:96, :n], perm[64:96, 0:96], krg_bf[64:96, :n])
            t2 = L.tmp()
            P.tt("dve", t2[64:96, :n], psw[64:96, :n], ropeS[64:96, :n], ALU.mult)
            P.tt("dve", krr[64:96, :n], krg[64:96, :n], ropeC[64:96, :n], ALU.mult)
            P.tt("pool", krr[64:96, :n], krr[64:96, :n], t2[64:96, :n], ALU.add)
        else:
            P.copy("dve", krr[64:96, :n], krg[64:96, :n])
        for h in range(16):
            psk = L.ps[5 + (h % 2)]
            for cc in range(2):
                P.mm(psk[0:64, :n], Wukv4[:, cc, h, 0:64], ckvn[:, cc, :n], start=(cc == 0), stop=(cc == 1))
            P.act(sq_n[0:64, :n], psk[0:64, :n], AF.Square)
            ps2 = L.ps[7]
            P.mm(ps2[0:96, :n], L.ones_bf[0:64, 0:96], sq_n[0:64, :n], start=True, stop=False)
            P.mm(ps2[0:96, :n], L.ones_bf[64:96, 0:96], sq_r[64:96, :n], start=False, stop=True)
            rs96 = L.tmp()
            L.rsqrt(rs96[0:96, :n], ps2[0:96, :n], 96.0 * EPS)
            t = L.tmp()
            P.act(t[0:64, :n], psk[0:64, :n], AF.Identity, scale=gk96[0:64, 0:1])
            kh = khs[kcnt % 2]
            kcnt += 1
            P.stt("dve", kh[0:64, :n], t[0:64, :n], 96.0 ** 0.5, rs96[0:64, :n], ALU.mult, ALU.mult)
            P.stt("dve", kh[64:96, :n], krr[64:96, :n], 96.0 ** 0.5, rs96[64:96, :n], ALU.mult, ALU.mult)
            P.dma(L.q(), kT3_out[:, h, hc0:hc0 + n], kh[0:96, :n])
        nt = n // 128
        for tt_ in range(nt):
            for hh in range(2):
                ps = L.ps[hh]
                for cc in range(2):
                    P.mm(ps[:, 0:512].rearrange("p (h n) -> p h n", h=8),
                         ckvn[:, cc, tt_ * 128:(tt_ + 1) * 128], Wukv4[:, cc, hh * 8:(hh + 1) * 8, 64:128],
                         start=(cc == 0), stop=(cc == 1))
                P.copy("act" if hh else "dve", vst[:, tt_, hh * 512:(hh + 1) * 512], ps[:, 0:512])
        P.dma(L.q(), v3_out[:, hc0 // 128:hc0 // 128 + nt, :], vst[:, :nt, :])
    return L


def pool_invc(half):
    t = np.zeros((128, 2, 4, 16), np.float32)
    for wi, w in enumerate((2, 4, 8, 16)):
        for seg in range(2):
            for e in range(8):
                cnt_s = (e + w // 2) - max(e - w // 2, 0)
                cnt_e = w // 2 + min(w // 2, 8 - e)
                real_s = (seg == 1) or (half == 0)
                real_e = (seg == 1) or (half == 1)
                t[:, seg, wi, e] = 1.0 / (cnt_s if real_s else w)
                t[:, seg, wi, 8 + e] = 1.0 / (cnt_e if real_e else w)
    return t


def prep_C(I, c, resB):
    b, half = c // 2, c % 2
    o = c ^ 1
    own = resB[c]["xT_out"]
    oth = resB[o]["xT_out"]
    z8 = np.zeros((128, 8, 8), np.float32)
    left = oth[:, :, 2040:2048] if half == 1 else z8
    right = oth[:, :, 0:8] if half == 0 else z8
    xT_in = np.concatenate([left, own[:, :, 0:2048], right, own[:, :, 2048:2304]], 2)
    hmask = np.zeros((128, 16), np.float32)
    hmask[:, 0:8] = 1.0 if half == 1 else 0.0
    hmask[:, 8:16] = 1.0 if half == 0 else 0.0
    C32, S32 = rope_tables(32, half * 2048, 2048)
    C = np.ones((128, 2048), np.float32)
    S = np.zeros((128, 2048), np.float32)
    C[64:96] = C32
    S[64:96] = S32
    gk = np.zeros((128, 1), np.float32)
    gk[0:96, 0] = I["mla_k_norm"][0]
    return {
        "ident_in": np.eye(128, dtype=np.float32),
        "xT_in": xT_in, "hmask": hmask, "invc": pool_invc(half),
        "cvec": np.stack([cols(I["c"][b], 8), cols(I["c_ctx"], 8)], -1),
        "wmod2": I["w_mod"][2], "bmodc2": cols(I["b_mod"][2], 48),
        "wmod3": I["w_mod"][3], "bmodc3": cols(I["b_mod"][3], 48),
        "n1g2": cols(I["norm1_g"][2], 8), "n2g2": cols(I["norm2_g"][2], 8), "n1g3": cols(I["norm1_g"][3], 8),
        "pool_scalec": cols(I["pool_scale"][0], 8), "pool_w": I["pool_w"][0],
        "wff1_2": I["w_ff1"][2], "wff2_2": I["w_ff2"][2],
        "mla_wdkv": I["mla_w_dkv"][0], "mla_wukv": I["mla_w_ukv"][0],
        "mla_kvlgc": cols(I["mla_kv_lora_norm"][0], 2), "mla_kgc": gk,
        "perm96": perm_matrix(32, 1, base=64),
        "ropeC96": C, "ropeS96": S,
    }


NL = 2048


def build_D():
    L = LB()
    P = L.P
    nc = L.nc
    xT = L.sb("xT", [128, 8, NL], F32)
    xT_in = L.inp("xT_in", [128, 8, NL])
    for k in range(8):
        P.dma(L.q(), xT[:, k, :], xT_in[:, k, :])
    abf = L.sb("abf", [128, 43008], BF16)
    af = L.sb("af", [128, 4096], F32)
    L.af = af
    L.setup_c()
    mod3 = L.compute_mod(3)
    n1g = L.load_cols("n1g3", [128, 8])
    n2g = L.load_cols("n2g3", [128, 8])
    G1 = L.make_gain("G1", mod3, 1, n1g)
    G2 = L.make_gain("G2", mod3, 4, n2g)
    ebias = L.sb("ebias", [128, 1], F32)
    P.memset("dve", ebias[:], -4.0)
    hblk = abf[:, 0:4096].rearrange("p (k n) -> p k n", k=8)
    Wdq = abf[:, 4096:10240].rearrange("p (k n) -> p k n", k=8)
    Wuq = abf[:, 10240:19456].rearrange("p (c n) -> p c n", c=6)
    cqn = abf[:, 19456:22528].rearrange("p (c n) -> p c n", c=6)
    qst = [abf[:, 22528 + i * 512:23040 + i * 512] for i in range(2)]
    w_dq = L.inp("mla_wdq", [1024, 768]).rearrange("(k p) n -> p k n", p=128)
    w_uq = L.inp("mla_wuq", [768, 1536]).rearrange("(c p) n -> p c n", p=128)
    L.wload(Wdq[:, :, 0:384], w_dq[:, :, 0:384])
    L.wload(Wdq[:, :, 384:768], w_dq[:, :, 384:768])
    for s3 in range(3):
        L.wload(Wuq[:, :, s3 * 512:(s3 + 1) * 512], w_uq[:, :, s3 * 512:(s3 + 1) * 512])
    gql = L.load_cols("mla_qlgc", [128, 6])
    gq96 = L.load_cols("mla_qgc", [128, 1])
    permf = L.load_cols("perm96", [128, 128])
    perm = L.sb("perm_bf", [128, 128], BF16)
    P.copy("dve", perm[:], permf[:])
    ropeC_d = L.inp("ropeC96", [128, 2048])
    ropeS_d = L.inp("ropeS96", [128, 2048])
    ropeC = af[:, 0:512]
    ropeS = af[:, 512:1024]
    qscr = nc.dram_tensor("qscratch", [16, 96, NL], BF16, kind="Internal").ap()
    qc = 0
    for tb in range(4):
        c0, n = tb * 512, 512
        L.norm_block(xT, c0, n, G1, mod3, 0, 0, lambda k: hblk[:, k, :n], psb=7)
        P.dma("sp", ropeC[:, :n], ropeC_d[:, c0:c0 + n])
        P.dma("act", ropeS[:, :n], ropeS_d[:, c0:c0 + n])
        for cc in range(6):
            ps = L.ps[cc]
            for k in range(8):
                P.mm(ps[:, :n], Wdq[:, k, cc * 128:(cc + 1) * 128], hblk[:, k, :n], start=(k == 0), stop=(k == 7))
            P.act(L.sqb[:, cc, :n], ps[:, :n], AF.Square)
        pss = L.ps[6]
        for cc in range(6):
            P.mm(pss[:, :n], L.ones_bf[:], L.sqb[:, cc, :n], start=(cc == 0), stop=(cc == 5))
        rs = L.rstd
        L.rsqrt(rs[:, :n], pss[:, :n], 768.0 * EPS)
        for cc in range(6):
            t = L.tmp()
            P.act(t[:, :n], L.ps[cc][:, :n], AF.Identity, scale=gql[:, cc:cc + 1])
            P.stt("dve", cqn[:, cc, :n], t[:, :n], 768.0 ** 0.5, rs[:, :n], ALU.mult, ALU.mult)
        for h in range(16):
            ps = L.ps[h % 2]
            for cc in range(6):
                P.mm(ps[0:96, :n], Wuq[:, cc, h * 96:(h + 1) * 96], cqn[:, cc, :n], start=(cc == 0), stop=(cc == 5))
            qo = qst[qc % 2]
            qc += 1
            L.qk_norm_rope(ps, 96, n, gq96, L.ones_bf, 96, perm, ropeC[0:96, :n], ropeS[0:96, :n], qo[0:96, :n],
                           L.ps[2 + (h % 2)], L.ps[4 + (h % 2)])
            P.dma(L.q(), qscr[h, :, c0:c0 + n], qo[0:96, :n])
    OT = abf[:, 0:16384].rearrange("p (k n) -> p k n", k=8)
    Kh = [abf[:, 16384 + i * 4352:16384 + (i + 1) * 4352] for i in range(2)]
    Vh = [abf[:, 25088 + i * 6528:25088 + (i + 1) * 6528].rearrange("p (c n) -> p c n", c=34) for i in range(2)]
    Qh = [abf[:, 38144 + i * 2048:38144 + (i + 1) * 2048] for i in range(2)]
    ptiles = [L.sb("ptile%d" % i, [128, 512], BF16) for i in range(3)]
    kT_own = L.inp("kT3_own", [96, 16, NT])
    kT_oth = L.inp("kT3_oth", [96, 16, NL])
    v_own = L.inp("v3_own", [128, 18, 1024])
    v_oth = L.inp("v3_oth", [128, 16, 1024])
    for i in range(2):
        P.memset("dve", Vh[i][:, :, 0:64], 0.0)
        P.memset("dve", Vh[i][:, :, 128:192], 0.0)
        P.memset("dve", Vh[i][:, :, 0:1], 1.0)
        P.memset("dve", Vh[i][:, :, 128:129], 1.0)

    def load_head(h):
        i = h % 2
        L.wload(Kh[i][0:96, 0:NT], kT_own[:, h, :])
        L.wload(Kh[i][0:96, NT:4352], kT_oth[:, h, :])
        L.wload(Vh[i][:, 0:18, 64:128], v_own[:, :, h * 64:(h + 1) * 64])
        L.wload(Vh[i][:, 18:34, 64:128], v_oth[:, :, h * 64:(h + 1) * 64])
        P.dma("sp", Qh[i][0:96, :], qscr[h, :, :])

    load_head(0)
    hcount = 0
    for h in range(16):
        if h + 1 < 16:
            load_head(h + 1)
        i = h % 2
        side = h % 2
        sl0, sl1 = (0, 64) if side == 0 else (64, 128)
        vc = 64 if side == 0 else 0
        for qb in range(4):
            ps_o = L.ps[3 + (hcount % 2)]
            hcount += 1
            attend_head(L,
                        lambda lc: Kh[i][0:96, lc * 128:(lc + 1) * 128],
                        Qh[i][0:96, qb * 512:(qb + 1) * 512],
                        lambda lc: Vh[i][:, lc, vc:vc + 128],
                        list(range(34)), 512, 96.0 ** -0.5, ps_o, side,
                        OT[sl0:sl1, h // 2, qb * 512:(qb + 1) * 512], ptiles, ebias=ebias[:, 0:1])
    Wo = abf[:, 16384:24576].rearrange("p (k n) -> p k n", k=8)
    w_o = L.inp("mla_wo", [1024, 1024]).rearrange("(k p) n -> p k n", p=128)
    for s2 in range(2):
        L.wload(Wo[:, :, s2 * 512:(s2 + 1) * 512], w_o[:, :, s2 * 512:(s2 + 1) * 512])
    for qb in range(4):
        for j in range(8):
            ps = L.ps[6 + (j % 2)]
            for pb in range(8):
                P.mm(ps[:, :512], Wo[:, pb, j * 128:(j + 1) * 128], OT[:, pb, qb * 512:(qb + 1) * 512],
                     start=(pb == 0), stop=(pb == 7))
            P.stt("dve", xT[:, j, qb * 512:(qb + 1) * 512], ps[:, :512], mod3[:, 16 + j, 0:1],
                  xT[:, j, qb * 512:(qb + 1) * 512], ALU.mult, ALU.add)
    hT = abf[:, 0:8 * NL].rearrange("p (k n) -> p k n", k=8)
    L.wslots = [abf[:, 16384 + i * 4096:16384 + (i + 1) * 4096] for i in range(4)]
    L.uT0 = abf[:, 32768:34816].rearrange("p (k n) -> p k n", k=4)
    L.uT1 = abf[:, 34816:36864].rearrange("p (k n) -> p k n", k=4)
    blocksF = [(tb * 512, tb * 512, 512, 0) for tb in range(4)]
    L.ffn(xT, hT, blocksF, 3, mod3, G2)
    out = L.outp("out", [NL, 1024])
    ost = [af[:, i * 1024:(i + 1) * 1024] for i in range(2)]
    for r in range(16):
        st = ost[r % 2]
        for half in range(2):
            ps = L.ps[(r * 2 + half) % 4]
            for kk in range(4):
                k = half * 4 + kk
                P.transpose(ps[:, kk * 128:(kk + 1) * 128], xT[:, k, r * 128:(r + 1) * 128], L.ident[:, :])
            P.copy("act" if half else "dve", st[:, half * 512:(half + 1) * 512], ps[:, 0:512])
        P.dma(L.q(), out[r * 128:(r + 1) * 128, :], st[:, :])
    return L


def prep_D(I, c, resC):
    b, half = c // 2, c % 2
    o = c ^ 1
    C32, S32 = rope_tables(32, half * 2048, 2048)
    C = np.ones((128, 2048), np.float32)
    S = np.zeros((128, 2048), np.float32)
    C[64:96] = C32
    S[64:96] = S32
    gq = np.zeros((128, 1), np.float32)
    gq[0:96, 0] = I["mla_q_norm"][0]
    return {
        "ident_in": np.eye(128, dtype=np.float32),
        "xT_in": resC[c]["xT_out"][:, :, 0:2048],
        "cvec": np.stack([cols(I["c"][b], 8), cols(I["c_ctx"], 8)], -1),
        "wmod3": I["w_mod"][3], "bmodc3": cols(I["b_mod"][3], 48),
        "n1g3": cols(I["norm1_g"][3], 8), "n2g3": cols(I["norm2_g"][3], 8),
        "mla_wdq": I["mla_w_dq"][0], "mla_wuq": I["mla_w_uq"][0], "mla_wo": I["mla_w_o"][0],
        "mla_qlgc": cols(I["mla_q_lora_norm"][0], 6), "mla_qgc": gq,
        "perm96": perm_matrix(32, 1, base=64),
        "ropeC96": C, "ropeS96": S,
        "kT3_own": resC[c]["kT3_out"], "kT3_oth": resC[o]["kT3_out"][:, :, 0:2048],
        "v3_own": resC[c]["v3_out"], "v3_oth": resC[o]["v3_out"][:, 0:16, :],
        "wff1_3": I["w_ff1"][3], "wff2_3": I["w_ff2"][3],
    }


def kernel(**inputs):
    I = {k: np.asarray(v) for k, v in inputs.items()}
    resA = run_launch("A", build_A, [prep_A(I, c) for c in range(CORES)])
    resB = run_launch("B", build_B, [prep_B(I, c, resA) for c in range(CORES)])
    resC = run_launch("C", build_C, [prep_C(I, c, resB) for c in range(CORES)])
    resD = run_launch("D", build_D, [prep_D(I, c, resC) for c in range(CORES)])
    out = np.empty((4, 4096, 1024), np.float32)
    for c in range(CORES):
        out[c // 2, (c % 2) * 2048:(c % 2 + 1) * 2048, :] = resD[c]["out"]
    return out
```

```python
from concourse.bass_utils import run_bass_kernel_spmd

import numpy as np
from contextlib import ExitStack
import concourse.bass as bass
import concourse.mybir as mybir

F32 = mybir.dt.float32
BF16 = mybir.dt.bfloat16
AF = mybir.ActivationFunctionType
ALU = mybir.AluOpType

ENGS = ["pe", "act", "dve", "pool", "sp"]
N_DMA_SEMS = 6


def _region(ap):
    t = ap.tensor
    dims = ap.ap
    off = int(ap.offset)
    cls = type(t).__name__
    if cls.startswith("DRam"):
        ext = 1
        for s, c in dims:
            ext += (c - 1) * abs(s)
        return (t.name, 0, 1, off, off + ext)
    pstep, pcnt = dims[0]
    p0 = off // pstep if pstep > 0 else 0
    lo = off - p0 * pstep
    ext = 1
    for s, c in dims[1:]:
        ext += (c - 1) * abs(s)
    return (t.name, p0, p0 + pcnt, lo, lo + ext)


def _overlap(a, b):
    return a[1] < b[2] and b[1] < a[2] and a[3] < b[4] and b[3] < a[4]


class Op:
    __slots__ = ("eng", "emit", "reads", "writes", "is_dma", "idx", "waits",
                 "signal", "dsem", "dwait_prev", "clock", "dma_inc")


class Prog:
    def __init__(self, nc, same_engine_sync=True):
        self.nc = nc
        self.ops = {e: [] for e in ENGS}
        self.same_engine_sync = same_engine_sync
        self.track = {}
        self.know = {e: {} for e in ENGS}
        self.dma_rr = {e: 0 for e in ENGS}
        self.dma_last = {e: [None] * N_DMA_SEMS for e in ENGS}
        self.dma_count = {e: [0] * N_DMA_SEMS for e in ENGS}
        self.nops = 0
        self.notrack = set()

    def _add(self, eng, emit, reads, writes, is_dma=False, dma_inc=16):
        op = Op()
        op.eng = eng
        op.emit = emit
        op.is_dma = is_dma
        op.dma_inc = dma_inc
        op.idx = len(self.ops[eng])
        op.waits = {}
        op.signal = False
        op.dsem = None
        rr = [_region(a) for a in reads if a.tensor.name not in self.notrack]
        wr = [_region(a) for a in writes]
        deps = []
        for r in rr:
            for rec in self.track.get(r[0], ()):
                if rec[1] is not None and _overlap(rec[0], r):
                    deps.append(rec[1])
        for w in wr:
            for rec in self.track.get(w[0], ()):
                if _overlap(rec[0], w):
                    if rec[1] is not None:
                        deps.append(rec[1])
                    deps.extend(rec[2].values())
        know = self.know[eng]
        if is_dma:
            q = self.dma_rr[eng]
            self.dma_rr[eng] = (q + 1) % N_DMA_SEMS
            prev = self.dma_last[eng][q]
            if prev is not None:
                deps.append(prev)
            self.dma_count[eng][q] += 1
            op.dsem = q
            mytok = ("D", eng, q, self.dma_count[eng][q], None)
            self.dma_last[eng][q] = mytok
        else:
            mytok = ("E", eng, op.idx, None)
        for tok in deps:
            if tok[0] == "E":
                _, e2, i2, _ = tok
                if e2 == eng:
                    if eng == "pe" or not self.same_engine_sync:
                        continue
                key = ("E", e2)
                val = i2
            else:
                _, e2, q2, c2, _ = tok
                key = ("D", e2, q2)
                val = c2
            if know.get(key, -1) >= val:
                continue
            if op.waits.get(key, -1) < val:
                op.waits[key] = val
        for key, val in op.waits.items():
            know[key] = max(know.get(key, -1), val)
            if key[0] == "E":
                src = self.ops[key[1]][val]
                src.signal = True
                for k2, v2 in src.clock.items():
                    if know.get(k2, -1) < v2:
                        know[k2] = v2
        op.clock = dict(know)
        skey = eng if not is_dma else ("dma", eng, op.dsem)
        for r in rr:
            lst = self.track.setdefault(r[0], [])
            for rec in lst:
                if rec[0] == r:
                    rec[2][skey] = mytok
                    break
            else:
                lst.append([r, None, {skey: mytok}])
        for w in wr:
            lst = self.track.setdefault(w[0], [])
            newl = []
            found = False
            for rec in lst:
                r0 = rec[0]
                if r0 == w:
                    rec[1] = mytok
                    rec[2] = {}
                    newl.append(rec)
                    found = True
                elif (w[1] <= r0[1] and r0[2] <= w[2] and w[3] <= r0[3] and r0[4] <= w[4]):
                    continue
                else:
                    newl.append(rec)
            if not found:
                newl.append([w, mytok, {}])
            self.track[w[0]] = newl
        self.ops[eng].append(op)
        self.nops += 1
        return op

    def mm(self, out, lhsT, rhs, start=True, stop=True, **kw):
        rd = [lhsT, rhs] + ([] if start else [out])
        return self._add("pe", lambda e: e.matmul(out, lhsT, rhs, start=start, stop=stop, **kw), rd, [out])

    def transpose(self, out, in_, ident):
        return self._add("pe", lambda e: e.transpose(out, in_, ident), [in_, ident], [out])

    def act(self, out, in_, func, bias=None, scale=None, accum_out=None, eng="act"):
        kw = {}
        rd = [in_]
        if bias is not None:
            kw["bias"] = bias
            if not isinstance(bias, (int, float)):
                rd.append(bias)
        if scale is not None:
            kw["scale"] = scale
            if not isinstance(scale, (int, float)):
                rd.append(scale)
        wr = [out]
        if accum_out is not None:
            kw["accum_out"] = accum_out
            wr.append(accum_out)
        return self._add("act", lambda e: e.activation(out, in_, func, **kw), rd, wr)

    def tt(self, eng, out, a, b, op):
        return self._add(eng, lambda e: e.tensor_tensor(out, a, b, op), [a, b], [out])

    def ts(self, eng, out, a, s1, s2, op0, op1=None, accum_out=None):
        rd = [a]
        if not isinstance(s1, (int, float)):
            rd.append(s1)
        if s2 is not None and not isinstance(s2, (int, float)):
            rd.append(s2)
        wr = [out]
        kw = {}
        if accum_out is not None:
            kw["accum_out"] = accum_out
            wr.append(accum_out)
        if op1 is None:
            return self._add(eng, lambda e: e.tensor_scalar(out, a, s1, None, op0, **kw), rd, wr)
        return self._add(eng, lambda e: e.tensor_scalar(out, a, s1, s2, op0, op1, **kw), rd, wr)

    def stt(self, eng, out, a, s, b, op0, op1):
        rd = [a, b]
        if not isinstance(s, (int, float)):
            rd.append(s)
        return self._add(eng, lambda e: e.scalar_tensor_tensor(out, a, s, b, op0, op1), rd, [out])

    def copy(self, eng, out, in_):
        if eng == "act":
            return self._add("act", lambda e: e.copy(out, in_), [in_], [out])
        return self._add(eng, lambda e: e.tensor_copy(out, in_), [in_], [out])

    def memset(self, eng, out, val):
        return self._add(eng, lambda e: e.memset(out, val), [], [out])

    def recip(self, out, in_):
        return self._add("dve", lambda e: e.reciprocal(out, in_), [in_], [out])

    def dma(self, q, out, in_, **kw):
        return self._add(q, lambda e: e.dma_start(out, in_, **kw), [in_], [out], is_dma=True)

    def custom(self, eng, emit, reads, writes, is_dma=False, dma_inc=16):
        return self._add(eng, emit, reads, writes, is_dma=is_dma, dma_inc=dma_inc)

    def emit(self, final_wait_all_dma=True):
        nc = self.nc
        with ExitStack() as st:
            esem = {e: st.enter_context(nc.semaphore("se_" + e)) for e in ENGS}
            dsem = {e: [st.enter_context(nc.semaphore("sd_%s%d" % (e, i))) for i in range(N_DMA_SEMS)]
                    for e in ENGS}
            sigcount = {}
            for e in ENGS:
                c = 0
                arr = []
                for op in self.ops[e]:
                    if (not op.is_dma) and op.signal:
                        c += 1
                    arr.append(c)
                sigcount[e] = arr
            block = st.enter_context(nc.Block())
            engobj = {"pe": block.tensor, "act": block.scalar, "dve": block.vector,
                      "pool": block.gpsimd, "sp": block.sync}

            def make(e):
                def body(eng):
                    for op in self.ops[e]:
                        for key, val in op.waits.items():
                            if key[0] == "E":
                                eng.wait_ge(esem[key[1]], sigcount[key[1]][val])
                            else:
                                eng.wait_ge(dsem[key[1]][key[2]], val * 16)
                        ins = op.emit(eng)
                        if op.is_dma:
                            ins.then_inc(dsem[e][op.dsem], op.dma_inc)
                        elif op.signal:
                            ins.then_inc(esem[e], 1)
                    for q in range(N_DMA_SEMS):
                        if self.dma_count[e][q] > 0:
                            eng.wait_ge(dsem[e][q], self.dma_count[e][q] * 16)
                return body

            for e in ENGS:
                engobj[e](make(e))

EPS = 1e-6
NT = 2304
CORES = 8
GRID_W = 64


def chunks(total, step=512):
    out = []
    c = 0
    while c < total:
        out.append((c, min(step, total - c)))
        c += step
    return out


class LB:
    def __init__(self):
        self.nc = bass.Bass("TRN2", target_bir_lowering=False)
        self.P = Prog(self.nc)
        self.ps = [self.nc.alloc_psum_tensor("ps%d" % i, [128, 512], F32) for i in range(8)]
        self.in_names = []
        self.out_names = []
        self.dq = 0
        P = self.P
        self.ident = self.sb("ident", [128, 128], F32)
        P.dma("sp", self.ident[:], self.inp("ident_in", [128, 128]))
        self.ones_bf = self.sb("ones_bf", [128, 128], BF16)
        P.memset("dve", self.ones_bf[:], 1.0)
        self.ones_f = self.sb("ones_f", [128, 128], F32)
        P.memset("dve", self.ones_f[:], 1.0)
        self.sqb = self.sb("sqb", [128, 8, 512], BF16)
        self.rstd = self.sb("rstd", [128, 512], F32)
        self.tmpf = [self.sb("tmpf%d" % i, [128, 512], F32) for i in range(4)]
        self.tmpi = 0
        self.rr = 0

    def inp(self, name, shape, dt=F32):
        t = self.nc.dram_tensor(name, list(shape), dt, kind="ExternalInput")
        self.P.notrack.add(name)
        self.in_names.append(name)
        return t.ap()

    def outp(self, name, shape, dt=F32):
        t = self.nc.dram_tensor(name, list(shape), dt, kind="ExternalOutput")
        self.out_names.append(name)
        return t.ap()

    def sb(self, name, shape, dt):
        return self.nc.alloc_sbuf_tensor(name, list(shape), dt)

    def q(self):
        self.dq ^= 1
        return "sp" if self.dq else "act"

    def tmp(self):
        self.tmpi = (self.tmpi + 1) % 4
        return self.tmpf[self.tmpi]

    def ve(self):
        self.rr ^= 1
        return "dve" if self.rr else "pool"

    def cconst(self, val):
        if not hasattr(self, "_cc"):
            self._cc = {}
        if val not in self._cc:
            t = self.sb("cc%d" % len(self._cc), [128, 1], F32)
            self.P.memset("dve", t[:], float(val))
            self._cc[val] = t
        return self._cc[val]

    def rsqrt(self, out, in_ps, addc):
        shp = list(out.shape)
        p0 = out.base_partition()
        cb = self.cconst(float(addc))
        self.P.act(out, in_ps, AF.Sqrt, bias=cb[p0:p0 + shp[0], 0:1])
        self.P.recip(out, out)

    def load_cols(self, name, shape):
        t = self.sb(name + "_sb", shape, F32)
        self.P.dma(self.q(), t[:], self.inp(name, shape))
        return t

    def setup_c(self):
        P = self.P
        cv = self.load_cols("cvec", [128, 8, 2])
        self.scT = self.sb("scT", [128, 8, 2], F32)
        P.act(self.scT[:], cv[:], AF.Silu)
        sw = self.slabw = getattr(self, "slabw", 256)
        self.wslab = [self.af[:, i * 8 * sw:(i + 1) * 8 * sw].rearrange("p (k n) -> p k n", k=8) for i in range(2)]

    def compute_mod(self, i):
        P = self.P
        wm = self.inp("wmod%d" % i, [1024, 6144]).rearrange("(k p) n -> p k n", p=128)
        bm = self.load_cols("bmodc%d" % i, [128, 48])
        mod = self.sb("mod%d" % i, [128, 48, 2], F32)
        psm = self.ps[7]
        sw = self.slabw
        for s in range(6144 // sw):
            slab = self.wslab[s % 2]
            P.dma(self.q(), slab[:], wm[:, :, s * sw:(s + 1) * sw])
            for jj in range(sw // 128):
                j = s * (sw // 128) + jj
                for k in range(8):
                    P.mm(psm[:, j * 2:(j + 1) * 2], slab[:, k, jj * 128:(jj + 1) * 128], self.scT[:, k, :],
                         start=(k == 0), stop=(k == 7))
        psv = psm[:, 0:96].rearrange("p (j s) -> p j s", s=2)
        for s in range(2):
            P.tt("dve", mod[:, :, s], psv[:, :, s], bm[:, :], ALU.add)
        return mod

    def make_gain(self, name, mod, m, ng):
        P = self.P
        G = self.sb(name, [128, 8, 2], F32)
        for s in range(2):
            P.ts("dve", G[:, :, s], mod[:, m * 8:(m + 1) * 8, s], 1.0, 32.0, ALU.add, ALU.mult)
            P.tt("dve", G[:, :, s], G[:, :, s], ng[:, :], ALU.mult)
        return G

    def norm_block(self, xT, c0, n, G, mod, msh, s, out_fn, psb=6):
        P = self.P
        P.act(self.sqb[:, :, :n], xT[:, :, c0:c0 + n], AF.Square)
        ps = self.ps[psb]
        for k in range(8):
            P.mm(ps[:, :n], self.ones_bf[:], self.sqb[:, k, :n], start=(k == 0), stop=(k == 7))
        self.rsqrt(self.rstd[:, :n], ps[:, :n], 1024.0 * EPS)
        for k in range(8):
            t = self.tmp()
            P.tt(self.ve(), t[:, :n], xT[:, k, c0:c0 + n], self.rstd[:, :n], ALU.mult)
            P.act(out_fn(k), t[:, :n], AF.Identity, scale=G[:, k, s:s + 1],
                  bias=mod[:, msh * 8 + k, s:s + 1])

    def wload(self, dst, src):
        self.P.dma("pool", dst, src)

    def setup_wslots(self, n=4):
        self.wslots = [self.sb("wslot%d" % i, [128, 4096], BF16) for i in range(n)]

    def ffn(self, xT, hT, blocks, i, mod, G2):
        P = self.P
        w1 = self.inp("wff1_%d" % i, [1024, 4096]).rearrange("(k p) n -> p k n", p=128)
        w2 = self.inp("wff2_%d" % i, [4096, 1024])
        for (xc0, hc0, n, s) in blocks:
            self.norm_block(xT, xc0, n, G2, mod, 3, s, lambda k: hT[:, k, hc0:hc0 + n])
        uT = [self.uT0, self.uT1]

        def issue(g):
            a = self.wslots[(g % 2) * 2][:, :].rearrange("p (k n) -> p k n", k=8)
            b = self.wslots[(g % 2) * 2 + 1][:, :].rearrange("p (k n) -> p k n", k=4)
            self.wload(a, w1[:, :, g * 512:(g + 1) * 512])
            self.wload(b, w2[g * 512:(g + 1) * 512, :].rearrange("(k p) n -> p k n", p=128))
            return a, b

        nxt = issue(0)
        cnt = 0
        for g in range(8):
            W1g, W2g = nxt
            if g + 1 < 8:
                nxt = issue(g + 1)
            for (xc0, hc0, n, s) in blocks:
                u = uT[cnt % 2]
                cnt += 1
                for fc in range(4):
                    ps = self.ps[fc]
                    for k in range(8):
                        P.mm(ps[:, :n], W1g[:, k, fc * 128:(fc + 1) * 128], hT[:, k, hc0:hc0 + n],
                             start=(k == 0), stop=(k == 7))
                    t = self.tmp()
                    P.act(t[:, :n], ps[:, :n], AF.Relu)
                    P.tt("pool", u[:, fc, :n], t[:, :n], t[:, :n], ALU.mult)
                for j in range(8):
                    ps = self.ps[4 + (j % 3)]
                    for fc in range(4):
                        P.mm(ps[:, :n], W2g[:, fc, j * 128:(j + 1) * 128], u[:, fc, :n],
                             start=(fc == 0), stop=(fc == 3))
                    P.stt("dve", xT[:, j, xc0:xc0 + n], ps[:, :n], mod[:, 40 + j, s:s + 1],
                          xT[:, j, xc0:xc0 + n], ALU.mult, ALU.add)

    def qk_norm_rope(self, ps_raw, rows, n, gcol, blk_ones, hd, perm, C, S, out_ap, ps_ss, ps_sw):
        P = self.P
        qg = self.tmp()
        P.act(qg[:rows, :n], ps_raw[:rows, :n], AF.Identity, scale=gcol[:rows, 0:1])
        sq = self.sqb[:, 0, :]
        P.act(sq[:rows, :n], ps_raw[:rows, :n], AF.Square)
        P.mm(ps_ss[:rows, :n], blk_ones[:rows, :rows], sq[:rows, :n])
        rs = self.tmp()
        self.rsqrt(rs[:rows, :n], ps_ss[:rows, :n], float(hd) * EPS)
        if C is None:
            P.stt("dve", out_ap, qg[:rows, :n], float(hd) ** 0.5, rs[:rows, :n], ALU.mult, ALU.mult)
            return
        qb = self.sqb[:, 1, :]
        P.copy("pool", qb[:rows, :n], qg[:rows, :n])
        P.mm(ps_sw[:rows, :n], perm[:rows, :rows], qb[:rows, :n])
        t1 = self.tmp()
        P.tt("dve", t1[:rows, :n], qg[:rows, :n], C, ALU.mult)
        t2 = self.tmp()
        P.tt("dve", t2[:rows, :n], ps_sw[:rows, :n], S, ALU.mult)
        P.tt("pool", t1[:rows, :n], t1[:rows, :n], t2[:rows, :n], ALU.add)
        P.stt("dve", out_ap, t1[:rows, :n], float(hd) ** 0.5, rs[:rows, :n], ALU.mult, ALU.mult)

    def finish(self):
        self.P.emit()
        return self.nc


NTA = 2336
UW = 2368


def build_A():
    L = LB()
    P = L.P
    nc = L.nc
    x_tok = L.inp("x_tok", [NTA, 1024])
    xT = L.sb("xT", [128, 8, NTA], F32)
    abf = L.sb("abf", [128, 39424], BF16)
    af = L.sb("af", [128, 4096], F32)
    L.af = af
    vblk = af[:, :].rearrange("p (k n) -> p k n", k=8)
    xst = [af[:, i * 1024:(i + 1) * 1024] for i in range(2)]
    for r in range(19):
        rows = 128 if r < 18 else 32
        st = xst[r % 2]
        P.dma(L.q(), st[:rows, :], x_tok[r * 128:r * 128 + rows, :])
        for half in range(2):
            ps = L.ps[(r * 2 + half) % 4]
            for kk in range(4):
                k = half * 4 + kk
                P.transpose(ps[:, kk * 128:kk * 128 + rows], st[:rows, k * 128:(k + 1) * 128], L.ident[:rows, :rows])
            src = ps[:, 0:512].rearrange("p (a b) -> p a b", a=4)[:, :, :rows]
            dst = xT[:, half * 4:(half + 1) * 4, r * 128:r * 128 + rows]
            P.copy("act" if half else "dve", dst, src)
    L.setup_c()
    mod0 = L.compute_mod(0)
    mod1 = L.compute_mod(1)
    n1g0 = L.load_cols("n1g0", [128, 8])
    n2g0 = L.load_cols("n2g0", [128, 8])
    n1g1 = L.load_cols("n1g1", [128, 8])
    G1_0 = L.make_gain("G1_0", mod0, 1, n1g0)
    G2_0 = L.make_gain("G2_0", mod0, 4, n2g0)
    G1_1 = L.make_gain("G1_1", mod1, 1, n1g1)
    pw1 = abf[:, 0:16384].rearrange("p (k n) -> p k n", k=8)
    hblk = [abf[:, 16384:20480].rearrange("p (k n) -> p k n", k=8) for i in range(2)]
    U = abf[:, 20480:20480 + 8 * UW].rearrange("p (k n) -> p k n", k=8)
    w_pw1 = L.inp("conv_pw1", [1024, 2048]).rearrange("(k p) n -> p k n", p=128)
    for s in range(4):
        L.wload(pw1[:, :, s * 512:(s + 1) * 512], w_pw1[:, :, s * 512:(s + 1) * 512])
    hmask = L.load_cols("hmask", [128, 32])
    wdw = L.load_cols("conv_dwc", [128, 8, 31])
    bdw = L.load_cols("conv_bdwc", [128, 8])
    lng = L.load_cols("conv_lngc", [128, 8])
    lnb = L.load_cols("conv_lnbc", [128, 8])
    lng32 = L.sb("lng32", [128, 8], F32)
    P.ts("dve", lng32[:], lng[:], 32.0, None, ALU.mult)
    ident_bf = L.sb("ident_bf", [128, 128], BF16)
    P.copy("dve", ident_bf[:], L.ident[:])
    P.memset("pool", U[:, :, 2080:2096], 0.0)
    P.memset("pool", U[:, :, 2352:2368], 0.0)
    blocksA = [(0, 512, 0, 0), (512, 512, 512, 0), (1024, 512, 1024, 0), (1536, 512, 1536, 0),
               (2048, 32, 2048, 0), (2080, 256, 2096, 1)]
    for bi, (c0, n, uc0, s) in enumerate(blocksA):
        hb = hblk[bi % 2]
        L.norm_block(xT, c0, n, G1_0, mod0, 0, s, lambda k: hb[:, k, :n])
        for j in range(8):
            psa = L.ps[(j % 2) * 2]
            psg = L.ps[(j % 2) * 2 + 1]
            for k in range(8):
                P.mm(psa[:, :n], pw1[:, k, j * 128:(j + 1) * 128], hb[:, k, :n], start=(k == 0), stop=(k == 7))
            for k in range(8):
                P.mm(psg[:, :n], pw1[:, k, 1024 + j * 128:1024 + (j + 1) * 128], hb[:, k, :n],
                     start=(k == 0), stop=(k == 7))
            sg = L.tmp()
            P.act(sg[:, :n], psg[:, :n], AF.Sigmoid)
            P.tt("dve", U[:, j, uc0:uc0 + n], psa[:, :n], sg[:, :n], ALU.mult)
    for j in range(8):
        P.tt("pool", U[:, j, 0:16], U[:, j, 0:16], hmask[:, 0:16], ALU.mult)
        P.tt("pool", U[:, j, 2064:2080], U[:, j, 2064:2080], hmask[:, 16:32], ALU.mult)
    pw2 = abf[:, 0:8192].rearrange("p (k n) -> p k n", k=8)
    w_pw2 = L.inp("conv_pw2", [1024, 1024]).rearrange("(k p) n -> p k n", p=128)
    for s in range(2):
        L.wload(pw2[:, :, s * 512:(s + 1) * 512], w_pw2[:, :, s * 512:(s + 1) * 512])
    zblk = abf[:, 8192:12288].rearrange("p (k n) -> p k n", k=8)
    diag = [abf[:, 12288 + i * 3968:12288 + (i + 1) * 3968].rearrange("p (t n) -> p t n", t=31) for i in range(2)]
    blocksC = [(16 + tb * 512, 512, 16 + tb * 512, 0) for tb in range(4)] + [(2080, 256, 2096, 1)]
    dcount = 0
    for (xc0, n, uc0, s) in blocksC:
        for j in range(8):
            dg = diag[dcount % 2]
            dcount += 1
            for tap in range(31):
                P.ts("dve", dg[:, tap, :], ident_bf[:], wdw[:, j, tap:tap + 1], None, ALU.mult)
            ps = L.ps[j % 4]
            for tap in range(31):
                P.mm(ps[:, :n], dg[:, tap, :], U[:, j, uc0 - 15 + tap:uc0 - 15 + tap + n],
                     start=(tap == 0), stop=(tap == 30))
            P.act(vblk[:, j, :n], ps[:, :n], AF.Identity, bias=bdw[:, j:j + 1])
        psm = L.ps[4]
        for j in range(8):
            P.mm(psm[:, :n], L.ones_f[:], vblk[:, j, :n], start=(j == 0), stop=(j == 7))
        mu = L.tmp()
        P.ts("dve", mu[:, :n], psm[:, :n], 1.0 / 1024.0, None, ALU.mult)
        for j in range(8):
            P.tt(L.ve(), vblk[:, j, :n], vblk[:, j, :n], mu[:, :n], ALU.subtract)
        P.act(L.sqb[:, :, :n], vblk[:, :, :n], AF.Square)
        psv = L.ps[5]
        for j in range(8):
            P.mm(psv[:, :n], L.ones_bf[:], L.sqb[:, j, :n], start=(j == 0), stop=(j == 7))
        L.rsqrt(L.rstd[:, :n], psv[:, :n], 1024.0 * EPS)
        for j in range(8):
            t = L.tmp()
            P.tt(L.ve(), t[:, :n], vblk[:, j, :n], L.rstd[:, :n], ALU.mult)
            P.act(zblk[:, j, :n], t[:, :n], AF.Silu, scale=lng32[:, j:j + 1], bias=lnb[:, j:j + 1])
        for j in range(8):
            ps = L.ps[6 + (j % 2)]
            for k in range(8):
                P.mm(ps[:, :n], pw2[:, k, j * 128:(j + 1) * 128], zblk[:, k, :n], start=(k == 0), stop=(k == 7))
            P.stt("dve", xT[:, j, xc0:xc0 + n], ps[:, :n], mod0[:, 16 + j, s:s + 1],
                  xT[:, j, xc0:xc0 + n], ALU.mult, ALU.add)
    hT = abf[:, 0:8 * NT].rearrange("p (k n) -> p k n", k=8)
    L.wslots = [abf[:, 18432 + i * 4096:18432 + (i + 1) * 4096] for i in range(4)]
    L.uT0 = abf[:, 34816:36864].rearrange("p (k n) -> p k n", k=4)
    L.uT1 = abf[:, 36864:38912].rearrange("p (k n) -> p k n", k=4)
    blocksF = [(16 + tb * 512, tb * 512, 512, 0) for tb in range(4)] + [(2080, 2048, 256, 1)]
    L.ffn(xT, hT, blocksF, 0, mod0, G2_0)
    for (xc0, hc0, n, s) in blocksF:
        L.norm_block(xT, xc0, n, G1_1, mod1, 0, s, lambda k: hT[:, k, hc0:hc0 + n])
    wkv = L.wslots[0][:, :].rearrange("p (k n) -> p k n", k=8)
    w_qkv = L.inp("gqa_wqkv", [1024, 1536]).rearrange("(k p) n -> p k n", p=128)
    L.wload(wkv[:, :, :], w_qkv[:, :, 1024:1536])
    gk = L.load_cols("gqa_kgc", [128, 1])
    ropeC_d = L.inp("ropeC64", [128, 2048])
    ropeS_d = L.inp("ropeS64", [128, 2048])
    ropeC = af[:, 3072:3584]
    ropeS = af[:, 3584:4096]
    permf = L.load_cols("perm64", [128, 128])
    perm = L.sb("perm_bf", [128, 128], BF16)
    P.copy("dve", perm[:], permf[:])
    blk1 = L.sb("blk64", [128, 128], BF16)
    P.memset("dve", blk1[:], 0.0)
    P.memset("dve", blk1[0:64, 0:64], 1.0)
    P.memset("dve", blk1[64:128, 64:128], 1.0)
    kT_out = L.outp("kT_out", [128, 2, NT])
    v_out = L.outp("v_out", [128, 18, 256])
    kst = [af[:, i * 512:(i + 1) * 512] for i in range(2)]
    vst = [af[:, 1024 + i * 1024:2048 + i * 1024].rearrange("p (k n) -> p k n", k=4) for i in range(2)]
    cnt = 0
    for (xc0, hc0, n, s) in blocksF:
        if s == 0:
            P.dma("sp", ropeC[:, :n], ropeC_d[:, hc0:hc0 + n])
            P.dma("act", ropeS[:, :n], ropeS_d[:, hc0:hc0 + n])
        for m in range(2):
            ps = L.ps[m]
            for k in range(8):
                P.mm(ps[:, :n], wkv[:, k, m * 128:(m + 1) * 128], hT[:, k, hc0:hc0 + n], start=(k == 0), stop=(k == 7))
            ko = kst[cnt % 2]
            cnt += 1
            if s == 0:
                L.qk_norm_rope(ps, 128, n, gk, blk1, 64, perm, ropeC[:, :n], ropeS[:, :n],
                               ko[:, :n], L.ps[2], L.ps[3])
            else:
                L.qk_norm_rope(ps, 128, n, gk, blk1, 64, None, None, None, ko[:, :n], L.ps[2], L.ps[3])
            P.dma(L.q(), kT_out[:, m, hc0:hc0 + n], ko[:, :n])
        vs = vst[(hc0 // 512) % 2]
        nt = n // 128
        for tt_ in range(nt):
            ps = L.ps[4 + (tt_ % 2)]
            for k in range(8):
                P.mm(ps[:, 0:256], hT[:, k, hc0 + tt_ * 128:hc0 + (tt_ + 1) * 128], wkv[:, k, 256:512],
                     start=(k == 0), stop=(k == 7))
            P.copy("act", vs[:, tt_, :], ps[:, 0:256])
        P.dma(L.q(), v_out[:, hc0 // 128:hc0 // 128 + nt, :], vs[:, :nt, :])
    xT_out = L.outp("xT_out", [128, 8, NT])
    P.dma("sp", xT_out[:, :, 0:2048], xT[:, :, 16:2064])
    P.dma("act", xT_out[:, :, 2048:2304], xT[:, :, 2080:2336])
    return L


def cols(v, k):
    return np.ascontiguousarray(np.asarray(v, np.float32).reshape(k, 128).T)


def rope_tables(rot, pos0, n):
    q = rot // 4
    inv = (10000.0 ** (-np.arange(q, dtype=np.float32) / q)).astype(np.float32)
    t = np.arange(pos0, pos0 + n)
    row = (t // GRID_W).astype(np.float32)
    col = (t % GRID_W).astype(np.float32)
    ar = (inv[:, None] * row[None, :]).astype(np.float32)
    ac = (inv[:, None] * col[None, :]).astype(np.float32)
    C = np.concatenate([np.cos(ar), np.cos(ar), np.cos(ac), np.cos(ac)], 0).astype(np.float32)
    S = np.concatenate([-np.sin(ar), np.sin(ar), -np.sin(ac), np.sin(ac)], 0).astype(np.float32)
    return C, S


def perm_matrix(rot, reps, base=0, size=128):
    Pm = np.zeros((size, size), np.float32)
    q = rot // 4
    for r in range(reps):
        for i in range(rot):
            g = i // (2 * q)
            j = i % (2 * q)
            pj = j + q if j < q else j - q
            Pm[base + r * rot + g * 2 * q + pj, base + r * rot + i] = 1.0
    return Pm


_PROG_CACHE = {}


def get_prog(name, builder):
    if name not in _PROG_CACHE:
        L = builder()
        L.finish()
        _PROG_CACHE[name] = L
    return _PROG_CACHE[name]


def run_launch(name, builder, in_maps):
    L = get_prog(name, builder)
    maps = [{k: np.ascontiguousarray(m[k], dtype=np.float32) for k in L.in_names} for m in in_maps]
    res = run_bass_kernel_spmd(L.nc, maps, core_ids=list(range(CORES)))
    return res.results


def prep_A(I, c):
    b, half = c // 2, c % 2
    st = half * 2048
    x = I["x"][b]
    z16 = np.zeros((16, 1024), np.float32)
    left = x[st - 16:st] if half == 1 else z16
    right = x[st + 2048:st + 2064] if half == 0 else z16
    x_tok = np.concatenate([left, x[st:st + 2048], right, I["ctx"][b]], 0)
    cvec = np.stack([cols(I["c"][b], 8), cols(I["c_ctx"], 8)], -1)
    hmask = np.zeros((128, 32), np.float32)
    hmask[:, 0:16] = 1.0 if half == 1 else 0.0
    hmask[:, 16:32] = 1.0 if half == 0 else 0.0
    C, S = rope_tables(64, st, 2048)
    d = {
        "ident_in": np.eye(128, dtype=np.float32),
        "x_tok": x_tok, "cvec": cvec, "hmask": hmask,
        "wmod0": I["w_mod"][0], "wmod1": I["w_mod"][1],
        "bmodc0": cols(I["b_mod"][0], 48), "bmodc1": cols(I["b_mod"][1], 48),
        "n1g0": cols(I["norm1_g"][0], 8), "n2g0": cols(I["norm2_g"][0], 8), "n1g1": cols(I["norm1_g"][1], 8),
        "conv_pw1": I["conv_w_pw1"][0], "conv_pw2": I["conv_w_pw2"][0],
        "conv_dwc": np.ascontiguousarray(I["conv_w_dw"][0].T.reshape(8, 128, 31).transpose(1, 0, 2)),
        "conv_bdwc": cols(I["conv_b_dw"][0], 8), "conv_lngc": cols(I["conv_ln_g"][0], 8),
        "conv_lnbc": cols(I["conv_ln_b"][0], 8),
        "wff1_0": I["w_ff1"][0], "wff2_0": I["w_ff2"][0],
        "gqa_wqkv": I["gqa_w_qkv"][0],
        "gqa_kgc": np.tile(I["gqa_k_norm"][0], 2).reshape(128, 1),
        "ropeC64": np.tile(C, (2, 1)), "ropeS64": np.tile(S, (2, 1)),
        "perm64": perm_matrix(64, 2),
    }
    return d


def attend_head(L, k_lhsT_fn, q_rhs, v_lhsT_fn, lcs, n, scale, ps_o, side, out_ap, ptiles, ebias=None):
    P = L.P
    nl = len(lcs)
    LOOK = 2

    def qk(idx):
        ps_s = L.ps[idx % 3]
        P.mm(ps_s[:, :n], k_lhsT_fn(lcs[idx]), q_rhs)

    for idx in range(min(LOOK, nl)):
        qk(idx)
    for idx in range(nl):
        if idx + LOOK < nl:
            qk(idx + LOOK)
        pt = ptiles[idx % 3]
        P.act(pt[:, :n], L.ps[idx % 3][:, :n], AF.Exp, scale=scale, bias=ebias)
        P.mm(ps_o[:, :n], v_lhsT_fn(lcs[idx]), pt[:, :n], start=(idx == 0), stop=(idx == nl - 1))
    sl0, sl1 = (0, 64) if side == 0 else (64, 128)
    drow = 64 if side == 0 else 0
    o_sb = L.tmp()
    P.copy("dve", o_sb[:, :n], ps_o[:, :n])
    rd = L.tmp()
    P.recip(rd[drow:drow + 1, :n], o_sb[drow:drow + 1, :n])
    ps_b = L.ps[5]
    P.mm(ps_b[:, :n], L.ones_f[drow:drow + 1, 0:128], rd[drow:drow + 1, :n])
    P.tt("dve", out_ap, o_sb[sl0:sl1, :n], ps_b[sl0:sl1, :n], ALU.mult)


def build_B():
    L = LB()
    P = L.P
    xT = L.sb("xT", [128, 8, NT], F32)
    xT_in = L.inp("xT_in", [128, 8, NT])
    for k in range(8):
        P.dma(L.q(), xT[:, k, :], xT_in[:, k, :])
    abf = L.sb("abf", [128, 50432], BF16)
    af = L.sb("af", [128, 2048], F32)
    L.af = af
    L.slabw = 128
    L.setup_c()
    mod1 = L.compute_mod(1)
    n1g = L.load_cols("n1g1", [128, 8])
    n2g = L.load_cols("n2g1", [128, 8])
    G1 = L.make_gain("G1", mod1, 1, n1g)
    G2 = L.make_gain("G2", mod1, 4, n2g)
    K_all = abf[:, 0:8704].rearrange("p (m n) -> p m n", m=2)
    Vext = abf[:, 8704:21760].rearrange("p (c n) -> p c n", c=34)
    Wq = abf[:, 21760:29952].rearrange("p (k n) -> p k n", k=8)
    Wo = abf[:, 29952:38144].rearrange("p (k n) -> p k n", k=8)
    hblk = abf[:, 38144:42240].rearrange("p (k n) -> p k n", k=8)
    QTb = abf[:, 42240:46336].rearrange("p (k n) -> p k n", k=8)
    OTb = abf[:, 46336:50432].rearrange("p (k n) -> p k n", k=8)
    ptiles = [L.sb("ptile%d" % i, [128, 512], BF16) for i in range(3)]
    kT_own = L.inp("kT_own", [128, 2, NT])
    kT_oth = L.inp("kT_oth", [128, 2, 2048])
    v_own = L.inp("v_own", [128, 18, 256])
    v_oth = L.inp("v_oth", [128, 16, 256])
    for m in range(2):
        L.wload(K_all[:, m, 0:NT], kT_own[:, m, :])
        L.wload(K_all[:, m, NT:4352], kT_oth[:, m, :])
    P.memset("dve", Vext[:, :, 64:128], 0.0)
    P.memset("dve", Vext[:, :, 256:320], 0.0)
    P.memset("dve", Vext[:, :, 64:65], 1.0)
    P.memset("dve", Vext[:, :, 256:257], 1.0)
    for (c0, c1, src) in ((0, 18, v_own), (18, 34, v_oth)):
        L.wload(Vext[:, c0:c1, 0:64], src[:, :, 0:64])
        L.wload(Vext[:, c0:c1, 128:256], src[:, :, 64:192])
        L.wload(Vext[:, c0:c1, 320:384], src[:, :, 192:256])
    vcol = [0, 64, 192, 256]
    w_qkv = L.inp("gqa_wqkv", [1024, 1536]).rearrange("(k p) n -> p k n", p=128)
    w_o = L.inp("gqa_wo", [1024, 1024])
    for m in range(2):
        for i in range(4):
            pb = m * 4 + i
            for side in range(2):
                h = 8 * m + 4 * side + i
                L.wload(Wq[:, :, pb * 128 + side * 64:pb * 128 + (side + 1) * 64], w_qkv[:, :, h * 64:(h + 1) * 64])
                L.wload(Wo[side * 64:(side + 1) * 64, pb, :], w_o[h * 64:(h + 1) * 64, :])
    gq = L.load_cols("gqa_qgc", [128, 1])
    permf = L.load_cols("perm64", [128, 128])
    perm = L.sb("perm_bf", [128, 128], BF16)
    P.copy("dve", perm[:], permf[:])
    blk1 = L.sb("blk64", [128, 128], BF16)
    P.memset("dve", blk1[:], 0.0)
    P.memset("dve", blk1[0:64, 0:64], 1.0)
    P.memset("dve", blk1[64:128, 64:128], 1.0)
    ropeC_d = L.inp("ropeC64", [128, 2048])
    ropeS_d = L.inp("ropeS64", [128, 2048])
    ropeC = af[:, 0:512]
    ropeS = af[:, 512:1024]
    blocks = [(tb * 512, 512, 0) for tb in range(4)] + [(2048, 256, 1)]
    hcount = 0
    for (c0, n, s) in blocks:
        L.norm_block(xT, c0, n, G1, mod1, 0, s, lambda k: hblk[:, k, :n], psb=7)
        if s == 0:
            P.dma("sp", ropeC[:, :n], ropeC_d[:, c0:c0 + n])
            P.dma("act", ropeS[:, :n], ropeS_d[:, c0:c0 + n])
        for pb in range(8):
            ps = L.ps[6]
            for k in range(8):
                P.mm(ps[:, :n], Wq[:, k, pb * 128:(pb + 1) * 128], hblk[:, k, :n], start=(k == 0), stop=(k == 7))
            if s == 0:
                L.qk_norm_rope(ps, 128, n, gq, blk1, 64, perm, ropeC[:, :n], ropeS[:, :n], QTb[:, pb, :n],
                               L.ps[7], L.ps[5])
            else:
                L.qk_norm_rope(ps, 128, n, gq, blk1, 64, None, None, None, QTb[:, pb, :n], L.ps[7], L.ps[5])
        lcs = list(range(34)) if s == 0 else [16, 17]
        for pb in range(8):
            m = pb // 4
            for side in range(2):
                kv = 2 * m + side
                sl0, sl1 = (0, 64) if side == 0 else (64, 128)
                ps_o = L.ps[3 + (hcount % 2)]
                hcount += 1
                attend_head(L,
                            lambda lc: K_all[sl0:sl1, m, lc * 128:(lc + 1) * 128],
                            QTb[sl0:sl1, pb, :n],
                            lambda lc: Vext[:, lc, vcol[kv]:vcol[kv] + 128],
                            lcs, n, 0.125, ps_o, side, OTb[sl0:sl1, pb, :n], ptiles)
        for j in range(8):
            ps = L.ps[6 + (j % 2)]
            for pb in range(8):
                P.mm(ps[:, :n], Wo[:, pb, j * 128:(j + 1) * 128], OTb[:, pb, :n], start=(pb == 0), stop=(pb == 7))
            P.stt("dve", xT[:, j, c0:c0 + n], ps[:, :n], mod1[:, 16 + j, s:s + 1], xT[:, j, c0:c0 + n],
                  ALU.mult, ALU.add)
    hT = abf[:, 0:8 * NT].rearrange("p (k n) -> p k n", k=8)
    L.wslots = [abf[:, 18432 + i * 4096:18432 + (i + 1) * 4096] for i in range(4)]
    L.uT0 = abf[:, 34816:36864].rearrange("p (k n) -> p k n", k=4)
    L.uT1 = abf[:, 36864:38912].rearrange("p (k n) -> p k n", k=4)
    blocksF = [(c0, c0, n, s) for (c0, n, s) in blocks]
    L.ffn(xT, hT, blocksF, 1, mod1, G2)
    xT_out = L.outp("xT_out", [128, 8, NT])
    for k in range(8):
        P.dma(L.q(), xT_out[:, k, :], xT[:, k, :])
    return L


def prep_B(I, c, resA):
    b, half = c // 2, c % 2
    o = c ^ 1
    C, S = rope_tables(64, half * 2048, 2048)
    return {
        "ident_in": np.eye(128, dtype=np.float32),
        "xT_in": resA[c]["xT_out"],
        "cvec": np.stack([cols(I["c"][b], 8), cols(I["c_ctx"], 8)], -1),
        "wmod1": I["w_mod"][1], "bmodc1": cols(I["b_mod"][1], 48),
        "n1g1": cols(I["norm1_g"][1], 8), "n2g1": cols(I["norm2_g"][1], 8),
        "kT_own": resA[c]["kT_out"], "kT_oth": resA[o]["kT_out"][:, :, 0:2048],
        "v_own": resA[c]["v_out"], "v_oth": resA[o]["v_out"][:, 0:16, :],
        "gqa_wqkv": I["gqa_w_qkv"][0], "gqa_wo": I["gqa_w_o"][0],
        "gqa_qgc": np.tile(I["gqa_q_norm"][0], 2).reshape(128, 1),
        "perm64": perm_matrix(64, 2),
        "ropeC64": np.tile(C, (2, 1)), "ropeS64": np.tile(S, (2, 1)),
        "wff1_1": I["w_ff1"][1], "wff2_1": I["w_ff2"][1],
    }


NTC = 2320


def build_C():
    L = LB()
    P = L.P
    xT = L.sb("xT", [128, 8, NTC], F32)
    xT_in = L.inp("xT_in", [128, 8, NTC])
    for k in range(8):
        P.dma(L.q(), xT[:, k, :], xT_in[:, k, :])
    abf = L.sb("abf", [128, 38912], BF16)
    af = L.sb("af", [128, 7680], F32)
    L.af = af
    L.setup_c()
    mod2 = L.compute_mod(2)
    mod3 = L.compute_mod(3)
    n1g2 = L.load_cols("n1g2", [128, 8])
    n2g2 = L.load_cols("n2g2", [128, 8])
    n1g3 = L.load_cols("n1g3", [128, 8])
    G1 = L.make_gain("G1_2", mod2, 1, n1g2)
    G2 = L.make_gain("G2_2", mod2, 4, n2g2)
    G1_3 = L.make_gain("G1_3", mod3, 1, n1g3)
    pscale = L.load_cols("pool_scalec", [128, 8])
    psg = L.sb("psg", [128, 8, 2], F32)
    for s in range(2):
        P.tt("dve", psg[:, :, s], mod2[:, 16:24, s], pscale[:, :], ALU.mult)
    hmask = L.load_cols("hmask", [128, 16])
    invc = L.load_cols("invc", [128, 2, 4, 16])
    Wp = abf[:, 0:2048].rearrange("p (g c n) -> p g c n", g=4, c=2)
    pblk = abf[:, 2048:6144].rearrange("p (k n) -> p k n", k=8)
    w_pool = L.inp("pool_w", [4, 256, 256])
    for g in range(4):
        L.wload(Wp[:, g, :, :], w_pool[g].rearrange("(c p) n -> p c n", p=128))
    hE = af[:, 0:4224].rearrange("p (k n) -> p k n", k=8)
    wa = af[:, 4224:4752]
    wb = af[:, 4752:5280]
    rstd_all = af[:, 5280:7600]
    for (c0, n) in [(0, 512), (512, 512), (1024, 512), (1536, 512), (2048, 272)]:
        P.act(L.sqb[:, :, :n], xT[:, :, c0:c0 + n], AF.Square)
        ps = L.ps[6]
        for k in range(8):
            P.mm(ps[:, :n], L.ones_bf[:], L.sqb[:, k, :n], start=(k == 0), stop=(k == 7))
        L.rsqrt(rstd_all[:, c0:c0 + n], ps[:, :n], 1024.0 * EPS)
    blocksP = [(8 + tb * 512, 512, 0, tb) for tb in range(4)] + [(2064, 256, 1, 4)]
    for (c0, n, s, bi) in blocksP:
        ne = n + 16
        if s == 0:
            for k in range(8):
                t = L.tmp()
                for (a0, a1) in ((0, 512), (512, ne)):
                    P.tt(L.ve(), t[:, 0:a1 - a0], xT[:, k, c0 - 8 + a0:c0 - 8 + a1],
                         rstd_all[:, c0 - 8 + a0:c0 - 8 + a1], ALU.mult)
                    P.act(hE[:, k, a0:a1], t[:, 0:a1 - a0], AF.Identity, scale=G1[:, k, s:s + 1],
                          bias=mod2[:, k, s:s + 1])
                if bi == 0:
                    P.tt("pool", hE[:, k, 0:8], hE[:, k, 0:8], hmask[:, 0:8], ALU.mult)
                if bi == 3:
                    P.tt("pool", hE[:, k, ne - 8:ne], hE[:, k, ne - 8:ne], hmask[:, 8:16], ALU.mult)
        else:
            P.memset("pool", hE[:, :, 0:8], 0.0)
            P.memset("pool", hE[:, :, 8 + n:16 + n], 0.0)
            for k in range(8):
                t = L.tmp()
                P.tt(L.ve(), t[:, :n], xT[:, k, c0:c0 + n], rstd_all[:, c0:c0 + n], ALU.mult)
                P.act(hE[:, k, 8:8 + n], t[:, :n], AF.Identity, scale=G1[:, k, s:s + 1], bias=mod2[:, k, s:s + 1])
        for k in range(8):
            wi = k // 2
            w = 2 << wi
            eng = "dve" if k % 2 == 0 else "pool"
            cur = hE[:, k, :]
            ln_ = ne
            step = 1
            bufs = [wa, wb]
            bi_ = 0
            while step < w:
                nxt = bufs[bi_ % 2]
                bi_ += 1
                ln2 = ln_ - step
                P.tt(eng, nxt[:, 0:ln2], cur[:, 0:ln2], cur[:, step:step + ln2], ALU.add)
                cur = nxt
                ln_ = ln2
                step *= 2
            off = 8 - w // 2
            P.stt("dve", pblk[:, k, :n], cur[:, off:off + n], 1.0 / w, hE[:, k, 8:8 + n], ALU.mult, ALU.subtract)
            edges = []
            if s == 1 or bi == 0:
                edges.append((0, 0))
            if s == 1 or bi == 3:
                edges.append((n - 8, 8))
            for (t0, e0) in edges:
                t = L.tmp()
                P.tt("dve", t[:, 0:8], cur[:, off + t0:off + t0 + 8], invc[:, s, wi, e0:e0 + 8], ALU.mult)
                P.tt("dve", pblk[:, k, t0:t0 + 8], t[:, 0:8], hE[:, k, 8 + t0:16 + t0], ALU.subtract)
        for g in range(4):
            for oc in range(2):
                j = 2 * g + oc
                ps = L.ps[j % 4]
                for kc in range(2):
                    P.mm(ps[:, :n], Wp[:, g, kc, oc * 128:(oc + 1) * 128], pblk[:, 2 * g + kc, :n],
                         start=(kc == 0), stop=(kc == 1))
                P.stt("dve", xT[:, j, c0:c0 + n], ps[:, :n], psg[:, j, s:s + 1], xT[:, j, c0:c0 + n],
                      ALU.mult, ALU.add)
    hT = abf[:, 0:8 * NT].rearrange("p (k n) -> p k n", k=8)
    L.wslots = [abf[:, 18432 + i * 4096:18432 + (i + 1) * 4096] for i in range(4)]
    L.uT0 = abf[:, 34816:36864].rearrange("p (k n) -> p k n", k=4)
    L.uT1 = abf[:, 36864:38912].rearrange("p (k n) -> p k n", k=4)
    blocksF = [(8 + tb * 512, tb * 512, 512, 0) for tb in range(4)] + [(2064, 2048, 256, 1)]
    L.ffn(xT, hT, blocksF, 2, mod2, G2)
    xT_out = L.outp("xT_out", [128, 8, NT])
    P.dma("sp", xT_out[:, :, 0:2048], xT[:, :, 8:2056])
    P.dma("act", xT_out[:, :, 2048:2304], xT[:, :, 2064:2320])
    for (xc0, hc0, n, s) in blocksF:
        L.norm_block(xT, xc0, n, G1_3, mod3, 0, s, lambda k: hT[:, k, hc0:hc0 + n])
    Wd = abf[:, 18432:21504].rearrange("p (k n) -> p k n", k=8)
    Wukv = abf[:, 21504:25600].rearrange("p (c n) -> p c n", c=2)
    ckvn = abf[:, 25600:26624].rearrange("p (c n) -> p c n", c=2)
    krg_bf = abf[:, 26624:27136]
    sq_r = abf[:, 27136:27648]
    sq_n = abf[:, 27648:28160]
    w_dkv = L.inp("mla_wdkv", [1024, 288]).rearrange("(k p) n -> p k n", p=128)
    w_ukv = L.inp("mla_wukv", [256, 2048]).rearrange("(c p) n -> p c n", p=128)
    P.memset("dve", Wd[:, :, 256:384], 0.0)
    L.wload(Wd[:, :, 0:256], w_dkv[:, :, 0:256])
    L.wload(Wd[:, :, 320:352], w_dkv[:, :, 256:288])
    L.wload(Wukv[:, :, :], w_ukv)
    gkvl = L.load_cols("mla_kvlgc", [128, 2])
    gk96 = L.load_cols("mla_kgc", [128, 1])
    permf = L.load_cols("perm96", [128, 128])
    perm = L.sb("perm_bf", [128, 128], BF16)
    P.copy("dve", perm[:], permf[:])
    ropeC_d = L.inp("ropeC96", [128, 2048])
    ropeS_d = L.inp("ropeS96", [128, 2048])
    ropeC = af[:, 0:512]
    ropeS = af[:, 512:1024]
    krr = af[:, 1024:1536]
    khs = [af[:, 1536:2048], af[:, 2048:2560]]
    vst = af[:, 2560:6656].rearrange("p (t n) -> p t n", t=4)
    kT3_out = L.outp("kT3_out", [96, 16, NT])
    v3_out = L.outp("v3_out", [128, 18, 1024])
    Wukv4 = Wukv.rearrange("p c (h n) -> p c h n", h=16)
    kcnt = 0
    for (xc0, hc0, n, s) in blocksF:
        if s == 0:
            P.dma("sp", ropeC[:, :n], ropeC_d[:, hc0:hc0 + n])
            P.dma("act", ropeS[:, :n], ropeS_d[:, hc0:hc0 + n])
        for cc in range(2):
            ps = L.ps[cc]
            for k in range(8):
                P.mm(ps[:, :n], Wd[:, k, cc * 128:(cc + 1) * 128], hT[:, k, hc0:hc0 + n], start=(k == 0), stop=(k == 7))
            P.act(L.sqb[:, cc, :n], ps[:, :n], AF.Square)
        pss = L.ps[2]
        for cc in range(2):
            P.mm(pss[:, :n], L.ones_bf[:], L.sqb[:, cc, :n], start=(cc == 0), stop=(cc == 1))
        rs = L.tmp()
        L.rsqrt(rs[:, :n], pss[:, :n], 256.0 * EPS)
        for cc in range(2):
            t = L.tmp()
            P.act(t[:, :n], L.ps[cc][:, :n], AF.Identity, scale=gkvl[:, cc:cc + 1])
            P.stt("dve", ckvn[:, cc, :n], t[:, :n], 16.0, rs[:, :n], ALU.mult, ALU.mult)
        psr = L.ps[3]
        for k in range(8):
            P.mm(psr[:, :n], Wd[:, k, 256:384], hT[:, k, hc0:hc0 + n], start=(k == 0), stop=(k == 7))
        P.act(sq_r[64:96, :n], psr[64:96, :n], AF.Square)
        krg = L.tmp()
        P.act(krg[64:96, :n], psr[64:96, :n], AF.Identity, scale=gk96[64:96, 0:1])
        if s == 0:
            P.copy("pool", krg_bf[64:96, :n], krg[64:96, :n])
            psw = L.ps[4]
            P.mm(psw[0:96, :n], perm[64:96, 0:96], krg_bf[64:96, :n])
            t2 = L.tmp()
            P.tt("dve", t2[64:96, :n], psw[64:96, :n], ropeS[64:96, :n], ALU.mult)
            P.tt("dve", krr[64:96, :n], krg[64:96, :n], ropeC[64:96, :n], ALU.mult)
            P.tt("pool", krr[64:96, :n], krr[64:96, :n], t2[64:96, :n], ALU.add)
        else:
            P.copy("dve", krr[64:96, :n], krg[64:96, :n])
        for h in range(16):
            psk = L.ps[5 + (h % 2)]
            for cc in range(2):
                P.mm(psk[0:64, :n], Wukv4[:, cc, h, 0:64], ckvn[:, cc, :n], start=(cc == 0), stop=(cc == 1))
            P.act(sq_n[0:64, :n], psk[0:64, :n], AF.Square)
            ps2 = L.ps[7]
            P.mm(ps2[0:96, :n], L.ones_bf[0:64, 0:96], sq_n[0:64, :n], start=True, stop=False)
            P.mm(ps2[0:96, :n], L.ones_bf[64:96, 0:96], sq_r[64:96, :n], start=False, stop=True)
            rs96 = L.tmp()
            L.rsqrt(rs96[0:96, :n], ps2[0:96, :n], 96.0 * EPS)
            t = L.tmp()
            P.act(t[0:64, :n], psk[0:64, :n], AF.Identity, scale=gk96[0:64, 0:1])
            kh = khs[kcnt % 2]
            kcnt += 1
            P.stt("dve", kh[0:64, :n], t[0:64, :n], 96.0 ** 0.5, rs96[0:64, :n], ALU.mult, ALU.mult)
            P.stt("dve", kh[64:96, :n], krr[64:96, :n], 96.0 ** 0.5, rs96[64:96, :n], ALU.mult, ALU.mult)
            P.dma(L.q(), kT3_out[:, h, hc0:hc0 + n], kh[0:96, :n])
        nt = n // 128
        for tt_ in range(nt):
            for hh in range(2):
                ps = L.ps[hh]
                for cc in range(2):
                    P.mm(ps[:, 0:512].rearrange("p (h n) -> p h n", h=8),
                         ckvn[:, cc, tt_ * 128:(tt_ + 1) * 128], Wukv4[:, cc, hh * 8:(hh + 1) * 8, 64:128],
                         start=(cc == 0), stop=(cc == 1))
                P.copy("act" if hh else "dve", vst[:, tt_, hh * 512:(hh + 1) * 512], ps[:, 0:512])
        P.dma(L.q(), v3_out[:, hc0 // 128:hc0 // 128 + nt, :], vst[:, :nt, :])
    return L


def pool_invc(half):
    t = np.zeros((128, 2, 4, 16), np.float32)
    for wi, w in enumerate((2, 4, 8, 16)):
        for seg in range(2):
            for e in range(8):
                cnt_s = (e + w // 2) - max(e - w // 2, 0)
                cnt_e = w // 2 + min(w // 2, 8 - e)
                real_s = (seg == 1) or (half == 0)
                real_e = (seg == 1) or (half == 1)
                t[:, seg, wi, e] = 1.0 / (cnt_s if real_s else w)
                t[:, seg, wi, 8 + e] = 1.0 / (cnt_e if real_e else w)
    return t


def prep_C(I, c, resB):
    b, half = c // 2, c % 2
    o = c ^ 1
    own = resB[c]["xT_out"]
    oth = resB[o]["xT_out"]
    z8 = np.zeros((128, 8, 8), np.float32)
    left = oth[:, :, 2040:2048] if half == 1 else z8
    right = oth[:, :, 0:8] if half == 0 else z8
    xT_in = np.concatenate([left, own[:, :, 0:2048], right, own[:, :, 2048:2304]], 2)
    hmask = np.zeros((128, 16), np.float32)
    hmask[:, 0:8] = 1.0 if half == 1 else 0.0
    hmask[:, 8:16] = 1.0 if half == 0 else 0.0
    C32, S32 = rope_tables(32, half * 2048, 2048)
    C = np.ones((128, 2048), np.float32)
    S = np.zeros((128, 2048), np.float32)
    C[64:96] = C32
    S[64:96] = S32
    gk = np.zeros((128, 1), np.float32)
    gk[0:96, 0] = I["mla_k_norm"][0]
    return {
        "ident_in": np.eye(128, dtype=np.float32),
        "xT_in": xT_in, "hmask": hmask, "invc": pool_invc(half),
        "cvec": np.stack([cols(I["c"][b], 8), cols(I["c_ctx"], 8)], -1),
        "wmod2": I["w_mod"][2], "bmodc2": cols(I["b_mod"][2], 48),
        "wmod3": I["w_mod"][3], "bmodc3": cols(I["b_mod"][3], 48),
        "n1g2": cols(I["norm1_g"][2], 8), "n2g2": cols(I["norm2_g"][2], 8), "n1g3": cols(I["norm1_g"][3], 8),
        "pool_scalec": cols(I["pool_scale"][0], 8), "pool_w": I["pool_w"][0],
        "wff1_2": I["w_ff1"][2], "wff2_2": I["w_ff2"][2],
        "mla_wdkv": I["mla_w_dkv"][0], "mla_wukv": I["mla_w_ukv"][0],
        "mla_kvlgc": cols(I["mla_kv_lora_norm"][0], 2), "mla_kgc": gk,
        "perm96": perm_matrix(32, 1, base=64),
        "ropeC96": C, "ropeS96": S,
    }


NL = 2048


def build_D():
    L = LB()
    P = L.P
    nc = L.nc
    xT = L.sb("xT", [128, 8, NL], F32)
    xT_in = L.inp("xT_in", [128, 8, NL])
    for k in range(8):
        P.dma(L.q(), xT[:, k, :], xT_in[:, k, :])
    abf = L.sb("abf", [128, 43008], BF16)
    af = L.sb("af", [128, 4096], F32)
    L.af = af
    L.setup_c()
    mod3 = L.compute_mod(3)
    n1g = L.load_cols("n1g3", [128, 8])
    n2g = L.load_cols("n2g3", [128, 8])
    G1 = L.make_gain("G1", mod3, 1, n1g)
    G2 = L.make_gain("G2", mod3, 4, n2g)
    ebias = L.sb("ebias", [128, 1], F32)
    P.memset("dve", ebias[:], -4.0)
    hblk = abf[:, 0:4096].rearrange("p (k n) -> p k n", k=8)
    Wdq = abf[:, 4096:10240].rearrange("p (k n) -> p k n", k=8)
    Wuq = abf[:, 10240:19456].rearrange("p (c n) -> p c n", c=6)
    cqn = abf[:, 19456:22528].rearrange("p (c n) -> p c n", c=6)
    qst = [abf[:, 22528 + i * 512:23040 + i * 512] for i in range(2)]
    w_dq = L.inp("mla_wdq", [1024, 768]).rearrange("(k p) n -> p k n", p=128)
    w_uq = L.inp("mla_wuq", [768, 1536]).rearrange("(c p) n -> p c n", p=128)
    L.wload(Wdq[:, :, 0:384], w_dq[:, :, 0:384])
    L.wload(Wdq[:, :, 384:768], w_dq[:, :, 384:768])
    for s3 in range(3):
        L.wload(Wuq[:, :, s3 * 512:(s3 + 1) * 512], w_uq[:, :, s3 * 512:(s3 + 1) * 512])
    gql = L.load_cols("mla_qlgc", [128, 6])
    gq96 = L.load_cols("mla_qgc", [128, 1])
    permf = L.load_cols("perm96", [128, 128])
    perm = L.sb("perm_bf", [128, 128], BF16)
    P.copy("dve", perm[:], permf[:])
    ropeC_d = L.inp("ropeC96", [128, 2048])
    ropeS_d = L.inp("ropeS96", [128, 2048])
    ropeC = af[:, 0:512]
    ropeS = af[:, 512:1024]
    qscr = nc.dram_tensor("qscratch", [16, 96, NL], BF16, kind="Internal").ap()
    qc = 0
    for tb in range(4):
        c0, n = tb * 512, 512
        L.norm_block(xT, c0, n, G1, mod3, 0, 0, lambda k: hblk[:, k, :n], psb=7)
        P.dma("sp", ropeC[:, :n], ropeC_d[:, c0:c0 + n])
        P.dma("act", ropeS[:, :n], ropeS_d[:, c0:c0 + n])
        for cc in range(6):
            ps = L.ps[cc]
            for k in range(8):
                P.mm(ps[:, :n], Wdq[:, k, cc * 128:(cc + 1) * 128], hblk[:, k, :n], start=(k == 0), stop=(k == 7))
            P.act(L.sqb[:, cc, :n], ps[:, :n], AF.Square)
        pss = L.ps[6]
        for cc in range(6):
            P.mm(pss[:, :n], L.ones_bf[:], L.sqb[:, cc, :n], start=(cc == 0), stop=(cc == 5))
        rs = L.rstd
        L.rsqrt(rs[:, :n], pss[:, :n], 768.0 * EPS)
        for cc in range(6):
            t = L.tmp()
            P.act(t[:, :n], L.ps[cc][:, :n], AF.Identity, scale=gql[:, cc:cc + 1])
            P.stt("dve", cqn[:, cc, :n], t[:, :n], 768.0 ** 0.5, rs[:, :n], ALU.mult, ALU.mult)
        for h in range(16):
            ps = L.ps[h % 2]
            for cc in range(6):
                P.mm(ps[0:96, :n], Wuq[:, cc, h * 96:(h + 1) * 96], cqn[:, cc, :n], start=(cc == 0), stop=(cc == 5))
            qo = qst[qc % 2]
            qc += 1
            L.qk_norm_rope(ps, 96, n, gq96, L.ones_bf, 96, perm, ropeC[0:96, :n], ropeS[0:96, :n], qo[0:96, :n],
                           L.ps[2 + (h % 2)], L.ps[4 + (h % 2)])
            P.dma(L.q(), qscr[h, :, c0:c0 + n], qo[0:96, :n])
    OT = abf[:, 0:16384].rearrange("p (k n) -> p k n", k=8)
    Kh = [abf[:, 16384 + i * 4352:16384 + (i + 1) * 4352] for i in range(2)]
    Vh = [abf[:, 25088 + i * 6528:25088 + (i + 1) * 6528].rearrange("p (c n) -> p c n", c=34) for i in range(2)]
    Qh = [abf[:, 38144 + i * 2048:38144 + (i + 1) * 2048] for i in range(2)]
    ptiles = [L.sb("ptile%d" % i, [128, 512], BF16) for i in range(3)]
    kT_own = L.inp("kT3_own", [96, 16, NT])
    kT_oth = L.inp("kT3_oth", [96, 16, NL])
    v_own = L.inp("v3_own", [128, 18, 1024])
    v_oth = L.inp("v3_oth", [128, 16, 1024])
    for i in range(2):
        P.memset("dve", Vh[i][:, :, 0:64], 0.0)
        P.memset("dve", Vh[i][:, :, 128:192], 0.0)
        P.memset("dve", Vh[i][:, :, 0:1], 1.0)
        P.memset("dve", Vh[i][:, :, 128:129], 1.0)

    def load_head(h):
        i = h % 2
        L.wload(Kh[i][0:96, 0:NT], kT_own[:, h, :])
        L.wload(Kh[i][0:96, NT:4352], kT_oth[:, h, :])
        L.wload(Vh[i][:, 0:18, 64:128], v_own[:, :, h * 64:(h + 1) * 64])
        L.wload(Vh[i][:, 18:34, 64:128], v_oth[:, :, h * 64:(h + 1) * 64])
        P.dma("sp", Qh[i][0:96, :], qscr[h, :, :])

    load_head(0)
    hcount = 0
    for h in range(16):
        if h + 1 < 16:
            load_head(h + 1)
        i = h % 2
        side = h % 2
        sl0, sl1 = (0, 64) if side == 0 else (64, 128)
        vc = 64 if side == 0 else 0
        for qb in range(4):
            ps_o = L.ps[3 + (hcount % 2)]
            hcount += 1
            attend_head(L,
                        lambda lc: Kh[i][0:96, lc * 128:(lc + 1) * 128],
                        Qh[i][0:96, qb * 512:(qb + 1) * 512],
                        lambda lc: Vh[i][:, lc, vc:vc + 128],
                        list(range(34)), 512, 96.0 ** -0.5, ps_o, side,
                        OT[sl0:sl1, h // 2, qb * 512:(qb + 1) * 512], ptiles, ebias=ebias[:, 0:1])
    Wo = abf[:, 16384:24576].rearrange("p (k n) -> p k n", k=8)
    w_o = L.inp("mla_wo", [1024, 1024]).rearrange("(k p) n -> p k n", p=128)
    for s2 in range(2):
        L.wload(Wo[:, :, s2 * 512:(s2 + 1) * 512], w_o[:, :, s2 * 512:(s2 + 1) * 512])
    for qb in range(4):
        for j in range(8):
            ps = L.ps[6 + (j % 2)]
            for pb in range(8):
                P.mm(ps[:, :512], Wo[:, pb, j * 128:(j + 1) * 128], OT[:, pb, qb * 512:(qb + 1) * 512],
                     start=(pb == 0), stop=(pb == 7))
            P.stt("dve", xT[:, j, qb * 512:(qb + 1) * 512], ps[:, :512], mod3[:, 16 + j, 0:1],
                  xT[:, j, qb * 512:(qb + 1) * 512], ALU.mult, ALU.add)
    hT = abf[:, 0:8 * NL].rearrange("p (k n) -> p k n", k=8)
    L.wslots = [abf[:, 16384 + i * 4096:16384 + (i + 1) * 4096] for i in range(4)]
    L.uT0 = abf[:, 32768:34816].rearrange("p (k n) -> p k n", k=4)
    L.uT1 = abf[:, 34816:36864].rearrange("p (k n) -> p k n", k=4)
    blocksF = [(tb * 512, tb * 512, 512, 0) for tb in range(4)]
    L.ffn(xT, hT, blocksF, 3, mod3, G2)
    out = L.outp("out", [NL, 1024])
    ost = [af[:, i * 1024:(i + 1) * 1024] for i in range(2)]
    for r in range(16):
        st = ost[r % 2]
        for half in range(2):
            ps = L.ps[(r * 2 + half) % 4]
            for kk in range(4):
                k = half * 4 + kk
                P.transpose(ps[:, kk * 128:(kk + 1) * 128], xT[:, k, r * 128:(r + 1) * 128], L.ident[:, :])
            P.copy("act" if half else "dve", st[:, half * 512:(half + 1) * 512], ps[:, 0:512])
        P.dma(L.q(), out[r * 128:(r + 1) * 128, :], st[:, :])
    return L


def prep_D(I, c, resC):
    b, half = c // 2, c % 2
    o = c ^ 1
    C32, S32 = rope_tables(32, half * 2048, 2048)
    C = np.ones((128, 2048), np.float32)
    S = np.zeros((128, 2048), np.float32)
    C[64:96] = C32
    S[64:96] = S32
    gq = np.zeros((128, 1), np.float32)
    gq[0:96, 0] = I["mla_q_norm"][0]
    return {
        "ident_in": np.eye(128, dtype=np.float32),
        "xT_in": resC[c]["xT_out"][:, :, 0:2048],
        "cvec": np.stack([cols(I["c"][b], 8), cols(I["c_ctx"], 8)], -1),
        "wmod3": I["w_mod"][3], "bmodc3": cols(I["b_mod"][3], 48),
        "n1g3": cols(I["norm1_g"][3], 8), "n2g3": cols(I["norm2_g"][3], 8),
        "mla_wdq": I["mla_w_dq"][0], "mla_wuq": I["mla_w_uq"][0], "mla_wo": I["mla_w_o"][0],
        "mla_qlgc": cols(I["mla_q_lora_norm"][0], 6), "mla_qgc": gq,
        "perm96": perm_matrix(32, 1, base=64),
        "ropeC96": C, "ropeS96": S,
        "kT3_own": resC[c]["kT3_out"], "kT3_oth": resC[o]["kT3_out"][:, :, 0:2048],
        "v3_own": resC[c]["v3_out"], "v3_oth": resC[o]["v3_out"][:, 0:16, :],
        "wff1_3": I["w_ff1"][3], "wff2_3": I["w_ff2"][3],
    }


def kernel(**inputs):
    I = {k: np.asarray(v) for k, v in inputs.items()}
    resA = run_launch("A", build_A, [prep_A(I, c) for c in range(CORES)])
    resB = run_launch("B", build_B, [prep_B(I, c, resA) for c in range(CORES)])
    resC = run_launch("C", build_C, [prep_C(I, c, resB) for c in range(CORES)])
    resD = run_launch("D", build_D, [prep_D(I, c, resC) for c in range(CORES)])
    out = np.empty((4, 4096, 1024), np.float32)
    for c in range(CORES):
        out[c // 2, (c % 2) * 2048:(c % 2 + 1) * 2048, :] = resD[c]["out"]
    return out
```

```python
from concourse.bass_utils import run_bass_kernel_spmd

import numpy as np
from contextlib import ExitStack
import concourse.bass as bass
import concourse.mybir as mybir

F32 = mybir.dt.float32
BF16 = mybir.dt.bfloat16
AF = mybir.ActivationFunctionType
ALU = mybir.AluOpType

ENGS = ["pe", "act", "dve", "pool", "sp"]
N_DMA_SEMS = 6


def _region(ap):
    t = ap.tensor
    dims = ap.ap
    off = int(ap.offset)
    cls = type(t).__name__
    if cls.startswith("DRam"):
        ext = 1
        for s, c in dims:
            ext += (c - 1) * abs(s)
        return (t.name, 0, 1, off, off + ext)
    pstep, pcnt = dims[0]
    p0 = off // pstep if pstep > 0 else 0
    lo = off - p0 * pstep
    ext = 1
    for s, c in dims[1:]:
        ext += (c - 1) * abs(s)
    return (t.name, p0, p0 + pcnt, lo, lo + ext)


def _overlap(a, b):
    return a[1] < b[2] and b[1] < a[2] and a[3] < b[4] and b[3] < a[4]


class Op:
    __slots__ = ("eng", "emit", "reads", "writes", "is_dma", "idx", "waits",
                 "signal", "dsem", "dwait_prev", "clock", "dma_inc")


class Prog:
    def __init__(self, nc, same_engine_sync=True):
        self.nc = nc
        self.ops = {e: [] for e in ENGS}
        self.same_engine_sync = same_engine_sync
        self.track = {}
        self.know = {e: {} for e in ENGS}
        self.dma_rr = {e: 0 for e in ENGS}
        self.dma_last = {e: [None] * N_DMA_SEMS for e in ENGS}
        self.dma_count = {e: [0] * N_DMA_SEMS for e in ENGS}
        self.nops = 0
        self.notrack = set()

    def _add(self, eng, emit, reads, writes, is_dma=False, dma_inc=16):
        op = Op()
        op.eng = eng
        op.emit = emit
        op.is_dma = is_dma
        op.dma_inc = dma_inc
        op.idx = len(self.ops[eng])
        op.waits = {}
        op.signal = False
        op.dsem = None
        rr = [_region(a) for a in reads if a.tensor.name not in self.notrack]
        wr = [_region(a) for a in writes]
        deps = []
        for r in rr:
            for rec in self.track.get(r[0], ()):
                if rec[1] is not None and _overlap(rec[0], r):
                    deps.append(rec[1])
        for w in wr:
            for rec in self.track.get(w[0], ()):
                if _overlap(rec[0], w):
                    if rec[1] is not None:
                        deps.append(rec[1])
                    deps.extend(rec[2].values())
        know = self.know[eng]
        if is_dma:
            q = self.dma_rr[eng]
            self.dma_rr[eng] = (q + 1) % N_DMA_SEMS
            prev = self.dma_last[eng][q]
            if prev is not None:
                deps.append(prev)
            self.dma_count[eng][q] += 1
            op.dsem = q
            mytok = ("D", eng, q, self.dma_count[eng][q], None)
            self.dma_last[eng][q] = mytok
        else:
            mytok = ("E", eng, op.idx, None)
        for tok in deps:
            if tok[0] == "E":
                _, e2, i2, _ = tok
                if e2 == eng:
                    if eng == "pe" or not self.same_engine_sync:
                        continue
                key = ("E", e2)
                val = i2
            else:
                _, e2, q2, c2, _ = tok
                key = ("D", e2, q2)
                val = c2
            if know.get(key, -1) >= val:
                continue
            if op.waits.get(key, -1) < val:
                op.waits[key] = val
        for key, val in op.waits.items():
            know[key] = max(know.get(key, -1), val)
            if key[0] == "E":
                src = self.ops[key[1]][val]
                src.signal = True
                for k2, v2 in src.clock.items():
                    if know.get(k2, -1) < v2:
                        know[k2] = v2
        op.clock = dict(know)
        skey = eng if not is_dma else ("dma", eng, op.dsem)
        for r in rr:
            lst = self.track.setdefault(r[0], [])
            for rec in lst:
                if rec[0] == r:
                    rec[2][skey] = mytok
                    break
            else:
                lst.append([r, None, {skey: mytok}])
        for w in wr:
            lst = self.track.setdefault(w[0], [])
            newl = []
            found = False
            for rec in lst:
                r0 = rec[0]
                if r0 == w:
                    rec[1] = mytok
                    rec[2] = {}
                    newl.append(rec)
                    found = True
                elif (w[1] <= r0[1] and r0[2] <= w[2] and w[3] <= r0[3] and r0[4] <= w[4]):
                    continue
                else:
                    newl.append(rec)
            if not found:
                newl.append([w, mytok, {}])
            self.track[w[0]] = newl
        self.ops[eng].append(op)
        self.nops += 1
        return op

    def mm(self, out, lhsT, rhs, start=True, stop=True, **kw):
        rd = [lhsT, rhs] + ([] if start else [out])
        return self._add("pe", lambda e: e.matmul(out, lhsT, rhs, start=start, stop=stop, **kw), rd, [out])

    def transpose(self, out, in_, ident):
        return self._add("pe", lambda e: e.transpose(out, in_, ident), [in_, ident], [out])

    def act(self, out, in_, func, bias=None, scale=None, accum_out=None, eng="act"):
        kw = {}
        rd = [in_]
        if bias is not None:
            kw["bias"] = bias
            if not isinstance(bias, (int, float)):
                rd.append(bias)
        if scale is not None:
            kw["scale"] = scale
            if not isinstance(scale, (int, float)):
                rd.append(scale)
        wr = [out]
        if accum_out is not None:
            kw["accum_out"] = accum_out
            wr.append(accum_out)
        return self._add("act", lambda e: e.activation(out, in_, func, **kw), rd, wr)

    def tt(self, eng, out, a, b, op):
        return self._add(eng, lambda e: e.tensor_tensor(out, a, b, op), [a, b], [out])

    def ts(self, eng, out, a, s1, s2, op0, op1=None, accum_out=None):
        rd = [a]
        if not isinstance(s1, (int, float)):
            rd.append(s1)
        if s2 is not None and not isinstance(s2, (int, float)):
            rd.append(s2)
        wr = [out]
        kw = {}
        if accum_out is not None:
            kw["accum_out"] = accum_out
            wr.append(accum_out)
        if op1 is None:
            return self._add(eng, lambda e: e.tensor_scalar(out, a, s1, None, op0, **kw), rd, wr)
        return self._add(eng, lambda e: e.tensor_scalar(out, a, s1, s2, op0, op1, **kw), rd, wr)

    def stt(self, eng, out, a, s, b, op0, op1):
        rd = [a, b]
        if not isinstance(s, (int, float)):
            rd.append(s)
        return self._add(eng, lambda e: e.scalar_tensor_tensor(out, a, s, b, op0, op1), rd, [out])

    def copy(self, eng, out, in_):
        if eng == "act":
            return self._add("act", lambda e: e.copy(out, in_), [in_], [out])
        return self._add(eng, lambda e: e.tensor_copy(out, in_), [in_], [out])

    def memset(self, eng, out, val):
        return self._add(eng, lambda e: e.memset(out, val), [], [out])

    def recip(self, out, in_):
        return self._add("dve", lambda e: e.reciprocal(out, in_), [in_], [out])

    def dma(self, q, out, in_, **kw):
        return self._add(q, lambda e: e.dma_start(out, in_, **kw), [in_], [out], is_dma=True)

    def custom(self, eng, emit, reads, writes, is_dma=False, dma_inc=16):
        return self._add(eng, emit, reads, writes, is_dma=is_dma, dma_inc=dma_inc)

    def emit(self, final_wait_all_dma=True):
        nc = self.nc
        with ExitStack() as st:
            esem = {e: st.enter_context(nc.semaphore("se_" + e)) for e in ENGS}
            dsem = {e: [st.enter_context(nc.semaphore("sd_%s%d" % (e, i))) for i in range(N_DMA_SEMS)]
                    for e in ENGS}
            sigcount = {}
            for e in ENGS:
                c = 0
                arr = []
                for op in self.ops[e]:
                    if (not op.is_dma) and op.signal:
                        c += 1
                    arr.append(c)
                sigcount[e] = arr
            block = st.enter_context(nc.Block())
            engobj = {"pe": block.tensor, "act": block.scalar, "dve": block.vector,
                      "pool": block.gpsimd, "sp": block.sync}

            def make(e):
                def body(eng):
                    for op in self.ops[e]:
                        for key, val in op.waits.items():
                            if key[0] == "E":
                                eng.wait_ge(esem[key[1]], sigcount[key[1]][val])
                            else:
                                eng.wait_ge(dsem[key[1]][key[2]], val * 16)
                        ins = op.emit(eng)
                        if op.is_dma:
                            ins.then_inc(dsem[e][op.dsem], op.dma_inc)
                        elif op.signal:
                            ins.then_inc(esem[e], 1)
                    for q in range(N_DMA_SEMS):
                        if self.dma_count[e][q] > 0:
                            eng.wait_ge(dsem[e][q], self.dma_count[e][q] * 16)
                return body

            for e in ENGS:
                engobj[e](make(e))

EPS = 1e-6
NT = 2304
CORES = 8
GRID_W = 64


def chunks(total, step=512):
    out = []
    c = 0
    while c < total:
        out.append((c, min(step, total - c)))
        c += step
    return out


class LB:
    def __init__(self):
        self.nc = bass.Bass("TRN2", target_bir_lowering=False)
        self.P = Prog(self.nc)
        self.psbig = self.nc.alloc_psum_tensor("psbig", [128, 4096], F32)
        self.ps = [self.psbig[:, i * 512:(i + 1) * 512] for i in range(8)]
        self.in_names = []
        self.out_names = []
        self.dq = 0
        P = self.P
        self.ident = self.sb("ident", [128, 128], F32)
        P.dma("sp", self.ident[:], self.inp("ident_in", [128, 128]))
        self.ones_bf = self.sb("ones_bf", [128, 128], BF16)
        P.memset("dve", self.ones_bf[:], 1.0)
        self.ones_f = self.sb("ones_f", [128, 128], F32)
        P.memset("dve", self.ones_f[:], 1.0)
        self.sqb = self.sb("sqb", [128, 8, 512], BF16)
        self.rstd = self.sb("rstd", [128, 512], F32)
        self.tmpf = [self.sb("tmpf%d" % i, [128, 512], F32) for i in range(4)]
        self.tmpi = 0
        self.rr = 0

    def inp(self, name, shape, dt=F32):
        t = self.nc.dram_tensor(name, list(shape), dt, kind="ExternalInput")
        self.P.notrack.add(name)
        self.in_names.append(name)
        return t.ap()

    def outp(self, name, shape, dt=F32):
        t = self.nc.dram_tensor(name, list(shape), dt, kind="ExternalOutput")
        self.out_names.append(name)
        return t.ap()

    def sb(self, name, shape, dt):
        return self.nc.alloc_sbuf_tensor(name, list(shape), dt)

    def q(self):
        self.dq ^= 1
        return "sp" if self.dq else "act"

    def tmp(self):
        self.tmpi = (self.tmpi + 1) % 4
        return self.tmpf[self.tmpi]

    def ve(self):
        self.rr ^= 1
        return "dve" if self.rr else "pool"

    def cconst(self, val):
        if not hasattr(self, "_cc"):
            self._cc = {}
        if val not in self._cc:
            t = self.sb("cc%d" % len(self._cc), [128, 1], F32)
            self.P.memset("dve", t[:], float(val))
            self._cc[val] = t
        return self._cc[val]

    def rsqrt(self, out, in_ps, addc):
        shp = list(out.shape)
        p0 = out.base_partition()
        cb = self.cconst(float(addc))
        self.P.act(out, in_ps, AF.Ln, bias=cb[p0:p0 + shp[0], 0:1])
        self.P.act(out, out, AF.Exp, scale=-0.5)

    def load_cols(self, name, shape):
        t = self.sb(name + "_sb", shape, F32)
        self.P.dma(self.q(), t[:], self.inp(name, shape))
        return t

    def setup_c(self):
        P = self.P
        cv = self.load_cols("cvec", [128, 8, 2])
        self.scT = self.sb("scT", [128, 8, 2], F32)
        P.act(self.scT[:], cv[:], AF.Silu)
        sw = self.slabw = getattr(self, "slabw", 256)
        self.wslab = [self.af[:, i * 8 * sw:(i + 1) * 8 * sw].rearrange("p (k n) -> p k n", k=8) for i in range(2)]

    def compute_mod(self, i):
        P = self.P
        wm = self.inp("wmod%d" % i, [1024, 6144]).rearrange("(k p) n -> p k n", p=128)
        bm = self.load_cols("bmodc%d" % i, [128, 48])
        mod = self.sb("mod%d" % i, [128, 48, 2], F32)
        psm = self.ps[7]
        sw = self.slabw
        for s in range(6144 // sw):
            slab = self.wslab[s % 2]
            P.dma(self.q(), slab[:], wm[:, :, s * sw:(s + 1) * sw])
            for jj in range(sw // 128):
                j = s * (sw // 128) + jj
                for k in range(8):
                    P.mm(psm[:, j * 2:(j + 1) * 2], slab[:, k, jj * 128:(jj + 1) * 128], self.scT[:, k, :],
                         start=(k == 0), stop=(k == 7))
        psv = psm[:, 0:96].rearrange("p (j s) -> p j s", s=2)
        for s in range(2):
            P.tt("dve", mod[:, :, s], psv[:, :, s], bm[:, :], ALU.add)
        return mod

    def make_gain(self, name, mod, m, ng):
        P = self.P
        G = self.sb(name, [128, 8, 2], F32)
        for s in range(2):
            P.ts("dve", G[:, :, s], mod[:, m * 8:(m + 1) * 8, s], 1.0, 32.0, ALU.add, ALU.mult)
            P.tt("dve", G[:, :, s], G[:, :, s], ng[:, :], ALU.mult)
        return G

    def norm_block(self, xT, c0, n, G, mod, msh, s, out_fn, psb=6):
        P = self.P
        P.act(self.sqb[:, :, :n], xT[:, :, c0:c0 + n], AF.Square)
        ps = self.ps[psb]
        for k in range(8):
            P.mm(ps[:, :n], self.ones_bf[:], self.sqb[:, k, :n], start=(k == 0), stop=(k == 7))
        self.rsqrt(self.rstd[:, :n], ps[:, :n], 1024.0 * EPS)
        for k in range(8):
            t = self.tmp()
            P.tt(self.ve(), t[:, :n], xT[:, k, c0:c0 + n], self.rstd[:, :n], ALU.mult)
            P.act(out_fn(k), t[:, :n], AF.Identity, scale=G[:, k, s:s + 1],
                  bias=mod[:, msh * 8 + k, s:s + 1])

    def wload(self, dst, src):
        self.P.dma("pool", dst, src)

    def setup_wslots(self, n=4):
        self.wslots = [self.sb("wslot%d" % i, [128, 4096], BF16) for i in range(n)]

    def ffn(self, xT, hT, blocks, i, mod, G2):
        P = self.P
        w1 = self.inp("wff1_%d" % i, [1024, 4096]).rearrange("(k p) n -> p k n", p=128)
        w2 = self.inp("wff2_%d" % i, [4096, 1024])
        for (xc0, hc0, n, s) in blocks:
            self.norm_block(xT, xc0, n, G2, mod, 3, s, lambda k: hT[:, k, hc0:hc0 + n])
        uT = [self.uT0, self.uT1]

        def issue(g):
            a = self.wslots[(g % 2) * 2][:, :].rearrange("p (k n) -> p k n", k=8)
            b = self.wslots[(g % 2) * 2 + 1][:, :].rearrange("p (k n) -> p k n", k=4)
            self.wload(a, w1[:, :, g * 512:(g + 1) * 512])
            self.wload(b, w2[g * 512:(g + 1) * 512, :].rearrange("(k p) n -> p k n", p=128))
            return a, b

        nxt = issue(0)
        cnt = 0
        for g in range(8):
            W1g, W2g = nxt
            if g + 1 < 8:
                nxt = issue(g + 1)
            for (xc0, hc0, n, s) in blocks:
                u = uT[cnt % 2]
                cnt += 1
                for fc in range(4):
                    ps = self.ps[fc]
                    for k in range(8):
                        P.mm(ps[:, :n], W1g[:, k, fc * 128:(fc + 1) * 128], hT[:, k, hc0:hc0 + n],
                             start=(k == 0), stop=(k == 7))
                    t = self.tmp()
                    P.act(t[:, :n], ps[:, :n], AF.Relu)
                    P.tt("pool", u[:, fc, :n], t[:, :n], t[:, :n], ALU.mult)
                for j in range(8):
                    ps = self.ps[4 + (j % 3)]
                    for fc in range(4):
                        P.mm(ps[:, :n], W2g[:, fc, j * 128:(j + 1) * 128], u[:, fc, :n],
                             start=(fc == 0), stop=(fc == 3))
                    P.stt("dve", xT[:, j, xc0:xc0 + n], ps[:, :n], mod[:, 40 + j, s:s + 1],
                          xT[:, j, xc0:xc0 + n], ALU.mult, ALU.add)

    def qk_norm_rope(self, ps_raw, rows, n, gcol, blk_ones, hd, perm, C, S, out_ap, ps_ss, ps_sw):
        P = self.P
        qg = self.tmp()
        P.act(qg[:rows, :n], ps_raw[:rows, :n], AF.Identity, scale=gcol[:rows, 0:1])
        sq = self.sqb[:, 0, :]
        P.act(sq[:rows, :n], ps_raw[:rows, :n], AF.Square)
        P.mm(ps_ss[:rows, :n], blk_ones[:rows, :rows], sq[:rows, :n])
        rs = self.tmp()
        self.rsqrt(rs[:rows, :n], ps_ss[:rows, :n], float(hd) * EPS)
        if C is None:
            P.stt("dve", out_ap, qg[:rows, :n], float(hd) ** 0.5, rs[:rows, :n], ALU.mult, ALU.mult)
            return
        qb = self.sqb[:, 1, :]
        P.copy("pool", qb[:rows, :n], qg[:rows, :n])
        P.mm(ps_sw[:rows, :n], perm[:rows, :rows], qb[:rows, :n])
        t1 = self.tmp()
        P.tt("dve", t1[:rows, :n], qg[:rows, :n], C, ALU.mult)
        t2 = self.tmp()
        P.tt("dve", t2[:rows, :n], ps_sw[:rows, :n], S, ALU.mult)
        P.tt("pool", t1[:rows, :n], t1[:rows, :n], t2[:rows, :n], ALU.add)
        P.stt("dve", out_ap, t1[:rows, :n], float(hd) ** 0.5, rs[:rows, :n], ALU.mult, ALU.mult)

    def finish(self):
        self.P.emit()
        return self.nc


NTA = 2336
UW = 2368


def build_A():
    L = LB()
    P = L.P
    nc = L.nc
    x_tok = L.inp("x_tok", [NTA, 1024])
    xT = L.sb("xT", [128, 8, NTA], F32)
    abf = L.sb("abf", [128, 39424], BF16)
    af = L.sb("af", [128, 4096], F32)
    L.af = af
    vblk = af[:, :].rearrange("p (k n) -> p k n", k=8)
    xst = [af[:, i * 1024:(i + 1) * 1024] for i in range(2)]
    for r in range(19):
        rows = 128 if r < 18 else 32
        st = xst[r % 2]
        P.dma(L.q(), st[:rows, :], x_tok[r * 128:r * 128 + rows, :])
        for half in range(2):
            ps = L.ps[(r * 2 + half) % 4]
            for kk in range(4):
                k = half * 4 + kk
                P.transpose(ps[:, kk * 128:kk * 128 + rows], st[:rows, k * 128:(k + 1) * 128], L.ident[:rows, :rows])
            src = ps[:, 0:512].rearrange("p (a b) -> p a b", a=4)[:, :, :rows]
            dst = xT[:, half * 4:(half + 1) * 4, r * 128:r * 128 + rows]
            P.copy("act" if half else "dve", dst, src)
    L.setup_c()
    mod0 = L.compute_mod(0)
    mod1 = L.compute_mod(1)
    n1g0 = L.load_cols("n1g0", [128, 8])
    n2g0 = L.load_cols("n2g0", [128, 8])
    n1g1 = L.load_cols("n1g1", [128, 8])
    G1_0 = L.make_gain("G1_0", mod0, 1, n1g0)
    G2_0 = L.make_gain("G2_0", mod0, 4, n2g0)
    G1_1 = L.make_gain("G1_1", mod1, 1, n1g1)
    pw1 = abf[:, 0:16384].rearrange("p (k n) -> p k n", k=8)
    hblk = [abf[:, 16384:20480].rearrange("p (k n) -> p k n", k=8) for i in range(2)]
    U = abf[:, 20480:20480 + 8 * UW].rearrange("p (k n) -> p k n", k=8)
    w_pw1 = L.inp("conv_pw1", [1024, 2048]).rearrange("(k p) n -> p k n", p=128)
    for s in range(4):
        L.wload(pw1[:, :, s * 512:(s + 1) * 512], w_pw1[:, :, s * 512:(s + 1) * 512])
    hmask = L.load_cols("hmask", [128, 32])
    wdw = L.load_cols("conv_dwc", [128, 8, 31])
    bdw = L.load_cols("conv_bdwc", [128, 8])
    lng = L.load_cols("conv_lngc", [128, 8])
    lnb = L.load_cols("conv_lnbc", [128, 8])
    lng32 = L.sb("lng32", [128, 8], F32)
    P.ts("dve", lng32[:], lng[:], 32.0, None, ALU.mult)
    ident_bf = L.sb("ident_bf", [128, 128], BF16)
    P.copy("dve", ident_bf[:], L.ident[:])
    P.memset("pool", U[:, :, 2080:2096], 0.0)
    P.memset("pool", U[:, :, 2352:2368], 0.0)
    blocksA = [(0, 512, 0, 0), (512, 512, 512, 0), (1024, 512, 1024, 0), (1536, 512, 1536, 0),
               (2048, 32, 2048, 0), (2080, 256, 2096, 1)]
    for bi, (c0, n, uc0, s) in enumerate(blocksA):
        hb = hblk[bi % 2]
        L.norm_block(xT, c0, n, G1_0, mod0, 0, s, lambda k: hb[:, k, :n])
        for j in range(8):
            psa = L.ps[(j % 2) * 2]
            psg = L.ps[(j % 2) * 2 + 1]
            for k in range(8):
                P.mm(psa[:, :n], pw1[:, k, j * 128:(j + 1) * 128], hb[:, k, :n], start=(k == 0), stop=(k == 7))
            for k in range(8):
                P.mm(psg[:, :n], pw1[:, k, 1024 + j * 128:1024 + (j + 1) * 128], hb[:, k, :n],
                     start=(k == 0), stop=(k == 7))
            sg = L.tmp()
            P.act(sg[:, :n], psg[:, :n], AF.Sigmoid)
            P.tt("dve", U[:, j, uc0:uc0 + n], psa[:, :n], sg[:, :n], ALU.mult)
    for j in range(8):
        P.tt("pool", U[:, j, 0:16], U[:, j, 0:16], hmask[:, 0:16], ALU.mult)
        P.tt("pool", U[:, j, 2064:2080], U[:, j, 2064:2080], hmask[:, 16:32], ALU.mult)
    pw2 = abf[:, 0:8192].rearrange("p (k n) -> p k n", k=8)
    w_pw2 = L.inp("conv_pw2", [1024, 1024]).rearrange("(k p) n -> p k n", p=128)
    for s in range(2):
        L.wload(pw2[:, :, s * 512:(s + 1) * 512], w_pw2[:, :, s * 512:(s + 1) * 512])
    zblk = abf[:, 8192:12288].rearrange("p (k n) -> p k n", k=8)
    diag = [abf[:, 12288 + i * 3968:12288 + (i + 1) * 3968].rearrange("p (t n) -> p t n", t=31) for i in range(2)]
    blocksC = [(16 + tb * 512, 512, 16 + tb * 512, 0) for tb in range(4)] + [(2080, 256, 2096, 1)]
    dcount = 0
    for (xc0, n, uc0, s) in blocksC:
        for j in range(8):
            dg = diag[dcount % 2]
            dcount += 1
            for tap in range(31):
                P.ts("dve", dg[:, tap, :], ident_bf[:], wdw[:, j, tap:tap + 1], None, ALU.mult)
            ps = L.ps[j % 4]
            for tap in range(31):
                P.mm(ps[:, :n], dg[:, tap, :], U[:, j, uc0 - 15 + tap:uc0 - 15 + tap + n],
                     start=(tap == 0), stop=(tap == 30))
            P.act(vblk[:, j, :n], ps[:, :n], AF.Identity, bias=bdw[:, j:j + 1])
        psm = L.ps[4]
        for j in range(8):
            P.mm(psm[:, :n], L.ones_f[:], vblk[:, j, :n], start=(j == 0), stop=(j == 7))
        mu = L.tmp()
        P.ts("dve", mu[:, :n], psm[:, :n], 1.0 / 1024.0, None, ALU.mult)
        for j in range(8):
            P.tt(L.ve(), vblk[:, j, :n], vblk[:, j, :n], mu[:, :n], ALU.subtract)
        P.act(L.sqb[:, :, :n], vblk[:, :, :n], AF.Square)
        psv = L.ps[5]
        for j in range(8):
            P.mm(psv[:, :n], L.ones_bf[:], L.sqb[:, j, :n], start=(j == 0), stop=(j == 7))
        L.rsqrt(L.rstd[:, :n], psv[:, :n], 1024.0 * EPS)
        for j in range(8):
            t = L.tmp()
            P.tt(L.ve(), t[:, :n], vblk[:, j, :n], L.rstd[:, :n], ALU.mult)
            P.act(zblk[:, j, :n], t[:, :n], AF.Silu, scale=lng32[:, j:j + 1], bias=lnb[:, j:j + 1])
        for j in range(8):
            ps = L.ps[6 + (j % 2)]
            for k in range(8):
                P.mm(ps[:, :n], pw2[:, k, j * 128:(j + 1) * 128], zblk[:, k, :n], start=(k == 0), stop=(k == 7))
            P.stt("dve", xT[:, j, xc0:xc0 + n], ps[:, :n], mod0[:, 16 + j, s:s + 1],
                  xT[:, j, xc0:xc0 + n], ALU.mult, ALU.add)
    hT = abf[:, 0:8 * NT].rearrange("p (k n) -> p k n", k=8)
    L.wslots = [abf[:, 18432 + i * 4096:18432 + (i + 1) * 4096] for i in range(4)]
    L.uT0 = abf[:, 34816:36864].rearrange("p (k n) -> p k n", k=4)
    L.uT1 = abf[:, 36864:38912].rearrange("p (k n) -> p k n", k=4)
    blocksF = [(16 + tb * 512, tb * 512, 512, 0) for tb in range(4)] + [(2080, 2048, 256, 1)]
    L.ffn(xT, hT, blocksF, 0, mod0, G2_0)
    for (xc0, hc0, n, s) in blocksF:
        L.norm_block(xT, xc0, n, G1_1, mod1, 0, s, lambda k: hT[:, k, hc0:hc0 + n])
    wkv = L.wslots[0][:, :].rearrange("p (k n) -> p k n", k=8)
    w_qkv = L.inp("gqa_wqkv", [1024, 1536]).rearrange("(k p) n -> p k n", p=128)
    L.wload(wkv[:, :, :], w_qkv[:, :, 1024:1536])
    gk = L.load_cols("gqa_kgc", [128, 1])
    ropeC_d = L.inp("ropeC64", [128, 2048])
    ropeS_d = L.inp("ropeS64", [128, 2048])
    ropeC = af[:, 3072:3584]
    ropeS = af[:, 3584:4096]
    permf = L.load_cols("perm64", [128, 128])
    perm = L.sb("perm_bf", [128, 128], BF16)
    P.copy("dve", perm[:], permf[:])
    blk1 = L.sb("blk64", [128, 128], BF16)
    P.memset("dve", blk1[:], 0.0)
    P.memset("dve", blk1[0:64, 0:64], 1.0)
    P.memset("dve", blk1[64:128, 64:128], 1.0)
    kT_out = L.outp("kT_out", [128, 2, NT])
    v_out = L.outp("v_out", [128, 18, 256])
    kst = [af[:, i * 512:(i + 1) * 512] for i in range(2)]
    vst = [af[:, 1024 + i * 1024:2048 + i * 1024].rearrange("p (k n) -> p k n", k=4) for i in range(2)]
    cnt = 0
    for (xc0, hc0, n, s) in blocksF:
        if s == 0:
            P.dma("sp", ropeC[:, :n], ropeC_d[:, hc0:hc0 + n])
            P.dma("act", ropeS[:, :n], ropeS_d[:, hc0:hc0 + n])
        for m in range(2):
            ps = L.ps[m]
            for k in range(8):
                P.mm(ps[:, :n], wkv[:, k, m * 128:(m + 1) * 128], hT[:, k, hc0:hc0 + n], start=(k == 0), stop=(k == 7))
            ko = kst[cnt % 2]
            cnt += 1
            if s == 0:
                L.qk_norm_rope(ps, 128, n, gk, blk1, 64, perm, ropeC[:, :n], ropeS[:, :n],
                               ko[:, :n], L.ps[2], L.ps[3])
            else:
                L.qk_norm_rope(ps, 128, n, gk, blk1, 64, None, None, None, ko[:, :n], L.ps[2], L.ps[3])
            P.dma(L.q(), kT_out[:, m, hc0:hc0 + n], ko[:, :n])
        vs = vst[(hc0 // 512) % 2]
        nt = n // 128
        for tt_ in range(nt):
            ps = L.ps[4 + (tt_ % 2)]
            for k in range(8):
                P.mm(ps[:, 0:256], hT[:, k, hc0 + tt_ * 128:hc0 + (tt_ + 1) * 128], wkv[:, k, 256:512],
                     start=(k == 0), stop=(k == 7))
            P.copy("act", vs[:, tt_, :], ps[:, 0:256])
        P.dma(L.q(), v_out[:, hc0 // 128:hc0 // 128 + nt, :], vs[:, :nt, :])
    mod_out = L.outp("mod_out", [128, 48, 2])
    P.dma("sp", mod_out, mod1[:])
    xT_out = L.outp("xT_out", [128, 8, NT])
    P.dma("sp", xT_out[:, :, 0:2048], xT[:, :, 16:2064])
    P.dma("act", xT_out[:, :, 2048:2304], xT[:, :, 2080:2336])
    return L


def cols(v, k):
    return np.ascontiguousarray(np.asarray(v, np.float32).reshape(k, 128).T)


def rope_tables(rot, pos0, n):
    q = rot // 4
    inv = (10000.0 ** (-np.arange(q, dtype=np.float32) / q)).astype(np.float32)
    t = np.arange(pos0, pos0 + n)
    row = (t // GRID_W).astype(np.float32)
    col = (t % GRID_W).astype(np.float32)
    ar = (inv[:, None] * row[None, :]).astype(np.float32)
    ac = (inv[:, None] * col[None, :]).astype(np.float32)
    C = np.concatenate([np.cos(ar), np.cos(ar), np.cos(ac), np.cos(ac)], 0).astype(np.float32)
    S = np.concatenate([-np.sin(ar), np.sin(ar), -np.sin(ac), np.sin(ac)], 0).astype(np.float32)
    return C, S


def perm_matrix(rot, reps, base=0, size=128):
    Pm = np.zeros((size, size), np.float32)
    q = rot // 4
    for r in range(reps):
        for i in range(rot):
            g = i // (2 * q)
            j = i % (2 * q)
            pj = j + q if j < q else j - q
            Pm[base + r * rot + g * 2 * q + pj, base + r * rot + i] = 1.0
    return Pm


_PROG_CACHE = {}


def get_prog(name, builder):
    if name not in _PROG_CACHE:
        L = builder()
        L.finish()
        _PROG_CACHE[name] = L
    return _PROG_CACHE[name]


def run_launch(name, builder, in_maps):
    L = get_prog(name, builder)
    maps = [{k: np.ascontiguousarray(m[k], dtype=np.float32) for k in L.in_names} for m in in_maps]
    res = run_bass_kernel_spmd(L.nc, maps, core_ids=list(range(CORES)))
    return res.results


def prep_A(I, c):
    b, half = c // 2, c % 2
    st = half * 2048
    x = I["x"][b]
    z16 = np.zeros((16, 1024), np.float32)
    left = x[st - 16:st] if half == 1 else z16
    right = x[st + 2048:st + 2064] if half == 0 else z16
    x_tok = np.concatenate([left, x[st:st + 2048], right, I["ctx"][b]], 0)
    cvec = np.stack([cols(I["c"][b], 8), cols(I["c_ctx"], 8)], -1)
    hmask = np.zeros((128, 32), np.float32)
    hmask[:, 0:16] = 1.0 if half == 1 else 0.0
    hmask[:, 16:32] = 1.0 if half == 0 else 0.0
    C, S = rope_tables(64, st, 2048)
    d = {
        "ident_in": np.eye(128, dtype=np.float32),
        "x_tok": x_tok, "cvec": cvec, "hmask": hmask,
        "wmod0": I["w_mod"][0], "wmod1": I["w_mod"][1],
        "bmodc0": cols(I["b_mod"][0], 48), "bmodc1": cols(I["b_mod"][1], 48),
        "n1g0": cols(I["norm1_g"][0], 8), "n2g0": cols(I["norm2_g"][0], 8), "n1g1": cols(I["norm1_g"][1], 8),
        "conv_pw1": I["conv_w_pw1"][0], "conv_pw2": I["conv_w_pw2"][0],
        "conv_dwc": np.ascontiguousarray(I["conv_w_dw"][0].T.reshape(8, 128, 31).transpose(1, 0, 2)),
        "conv_bdwc": cols(I["conv_b_dw"][0], 8), "conv_lngc": cols(I["conv_ln_g"][0], 8),
        "conv_lnbc": cols(I["conv_ln_b"][0], 8),
        "wff1_0": I["w_ff1"][0], "wff2_0": I["w_ff2"][0],
        "gqa_wqkv": I["gqa_w_qkv"][0],
        "gqa_kgc": np.tile(I["gqa_k_norm"][0], 2).reshape(128, 1),
        "ropeC64": np.tile(C, (2, 1)), "ropeS64": np.tile(S, (2, 1)),
        "perm64": perm_matrix(64, 2),
    }
    return d


def attend_head(L, k_lhsT_fn, q_rhs, v_lhsT_fn, lcs, n, scale, ps_o, side, out_ap, ptiles, ebias=None, pending=None):
    P = L.P
    groups = [lcs[i:i + 2] for i in range(0, len(lcs), 2)]
    ng = len(groups)
    S = [L.psbig[:, 0:1024], L.psbig[:, 1024:2048]]

    def qk(gi):
        s_ = S[gi % 2]
        for j, lc in enumerate(groups[gi]):
            P.mm(s_[:, j * 512:j * 512 + n], k_lhsT_fn(lc), q_rhs)

    qk(0)
    if ng > 1:
        qk(1)
    for gi in range(ng):
        if gi == 2 and pending is not None:
            pending()
            pending = None
        g = groups[gi]
        s_ = S[gi % 2]
        pt = ptiles[gi % 2]
        if n == 512:
            w = len(g) * 512
            P.act(pt[:, 0:w], s_[:, 0:w], AF.Exp, scale=scale, bias=ebias)
        else:
            P.act(pt[:, :].rearrange("p (j c) -> p j c", j=2)[:, 0:len(g), 0:n],
                  s_.rearrange("p (j c) -> p j c", j=2)[:, 0:len(g), 0:n], AF.Exp, scale=scale, bias=ebias)
        if gi + 2 < ng:
            qk(gi + 2)
        for j, lc in enumerate(g):
            P.mm(ps_o[:, :n], v_lhsT_fn(lc), pt[:, j * 512:j * 512 + n], start=(gi == 0 and j == 0),
                 stop=(gi == ng - 1 and j == len(g) - 1))
    if pending is not None:
        pending()

    def fin():
        sl0, sl1 = (0, 64) if side == 0 else (64, 128)
        drow = 64 if side == 0 else 0
        o_sb = L.tmp()
        P.copy("dve", o_sb[:, :n], ps_o[:, :n])
        rd = L.tmp()
        P.recip(rd[drow:drow + 1, :n], o_sb[drow:drow + 1, :n])
        P.mm(ps_o[:, :n], L.ones_f[drow:drow + 1, 0:128], rd[drow:drow + 1, :n])
        P.tt("dve", out_ap, o_sb[sl0:sl1, :n], ps_o[sl0:sl1, :n], ALU.mult)
    return fin


def attend_pair(L, k_lo_fn, k_hi_fn, q_lo, q_hi, v_lo_fn, v_hi_fn, lcs, n, scale, ps_lo, ps_hi, out_lo, out_hi,
                ptiles, pending=None):
    P = L.P
    ng = len(lcs)
    S = [L.psbig[:, 0:1024], L.psbig[:, 1024:2048]]

    def qk(gi):
        s_ = S[gi % 2]
        P.mm(s_[:, 0:n], k_lo_fn(lcs[gi]), q_lo)
        P.mm(s_[:, 512:512 + n], k_hi_fn(lcs[gi]), q_hi)

    qk(0)
    if ng > 1:
        qk(1)
    for gi in range(ng):
        if gi == 2 and pending is not None:
            pending()
            pending = None
        s_ = S[gi % 2]
        pt = ptiles[gi % 2]
        if n == 512:
            P.act(pt[:, 0:1024], s_[:, 0:1024], AF.Exp, scale=scale)
        else:
            P.act(pt[:, :].rearrange("p (j c) -> p j c", j=2)[:, :, 0:n],
                  s_.rearrange("p (j c) -> p j c", j=2)[:, :, 0:n], AF.Exp, scale=scale)
        if gi + 2 < ng:
            qk(gi + 2)
        P.mm(ps_lo[:, :n], v_lo_fn(lcs[gi]), pt[:, 0:n], start=(gi == 0), stop=(gi == ng - 1))
        P.mm(ps_hi[:, :n], v_hi_fn(lcs[gi]), pt[:, 512:512 + n], start=(gi == 0), stop=(gi == ng - 1))
    if pending is not None:
        pending()

    def fin():
        for (side, ps_o, out_ap) in ((0, ps_lo, out_lo), (1, ps_hi, out_hi)):
            sl0, sl1 = (0, 64) if side == 0 else (64, 128)
            drow = 64 if side == 0 else 0
            o_sb = L.tmp()
            P.copy("dve", o_sb[:, :n], ps_o[:, :n])
            rd = L.tmp()
            P.recip(rd[drow:drow + 1, :n], o_sb[drow:drow + 1, :n])
            P.mm(ps_o[:, :n], L.ones_f[drow:drow + 1, 0:128], rd[drow:drow + 1, :n])
            P.tt("dve", out_ap, o_sb[sl0:sl1, :n], ps_o[sl0:sl1, :n], ALU.mult)
    return fin


def build_B():
    L = LB()
    P = L.P
    xT = L.sb("xT", [128, 8, NT], F32)
    xT_in = L.inp("xT_in", [128, 8, NT])
    for k in range(8):
        P.dma(L.q(), xT[:, k, :], xT_in[:, k, :])
    abf = L.sb("abf", [128, 50432], BF16)
    af = L.sb("af", [128, 2048], F32)
    L.af = af
    L.slabw = 128
    mod1 = L.load_cols("mod1_in", [128, 48, 2])
    n1g = L.load_cols("n1g1", [128, 8])
    n2g = L.load_cols("n2g1", [128, 8])
    G1 = L.make_gain("G1", mod1, 1, n1g)
    G2 = L.make_gain("G2", mod1, 4, n2g)
    K_all = abf[:, 0:8704].rearrange("p (m n) -> p m n", m=2)
    Vext = abf[:, 8704:21760].rearrange("p (c n) -> p c n", c=34)
    Wq = abf[:, 21760:29952].rearrange("p (k n) -> p k n", k=8)
    Wo = abf[:, 29952:38144].rearrange("p (k n) -> p k n", k=8)
    hblk = abf[:, 38144:42240].rearrange("p (k n) -> p k n", k=8)
    QTb = abf[:, 42240:46336].rearrange("p (k n) -> p k n", k=8)
    OTb = abf[:, 46336:50432].rearrange("p (k n) -> p k n", k=8)
    ptiles = [L.sb("ptile%d" % i, [128, 1024], BF16) for i in range(2)]
    kT_own = L.inp("kT_own", [128, 2, NT])
    kT_oth = L.inp("kT_oth", [128, 2, 2048])
    v_own = L.inp("v_own", [128, 18, 256])
    v_oth = L.inp("v_oth", [128, 16, 256])
    for m in range(2):
        L.wload(K_all[:, m, 0:NT], kT_own[:, m, :])
        L.wload(K_all[:, m, NT:4352], kT_oth[:, m, :])
    P.memset("dve", Vext[:, :, 64:128], 0.0)
    P.memset("dve", Vext[:, :, 256:320], 0.0)
    P.memset("dve", Vext[:, :, 64:65], 1.0)
    P.memset("dve", Vext[:, :, 256:257], 1.0)
    for (c0, c1, src) in ((0, 18, v_own), (18, 34, v_oth)):
        L.wload(Vext[:, c0:c1, 0:64], src[:, :, 0:64])
        L.wload(Vext[:, c0:c1, 128:256], src[:, :, 64:192])
        L.wload(Vext[:, c0:c1, 320:384], src[:, :, 192:256])
    vcol = [0, 64, 192, 256]
    w_qkv = L.inp("gqa_wqkv", [1024, 1536]).rearrange("(k p) n -> p k n", p=128)
    w_o = L.inp("gqa_wo", [1024, 1024])
    for m in range(2):
        for i in range(4):
            pb = m * 4 + i
            for side in range(2):
                h = 8 * m + 4 * side + i
                L.wload(Wq[:, :, pb * 128 + side * 64:pb * 128 + (side + 1) * 64], w_qkv[:, :, h * 64:(h + 1) * 64])
                L.wload(Wo[side * 64:(side + 1) * 64, pb, :], w_o[h * 64:(h + 1) * 64, :])
    gq = L.load_cols("gqa_qgc", [128, 1])
    permf = L.load_cols("perm64", [128, 128])
    perm = L.sb("perm_bf", [128, 128], BF16)
    P.copy("dve", perm[:], permf[:])
    blk1 = L.sb("blk64", [128, 128], BF16)
    P.memset("dve", blk1[:], 0.0)
    P.memset("dve", blk1[0:64, 0:64], 1.0)
    P.memset("dve", blk1[64:128, 64:128], 1.0)
    ropeC_d = L.inp("ropeC64", [128, 2048])
    ropeS_d = L.inp("ropeS64", [128, 2048])
    ropeC = af[:, 0:512]
    ropeS = af[:, 512:1024]
    blocks = [(tb * 512, 512, 0) for tb in range(4)] + [(2048, 256, 1)]
    hcount = 0
    for (c0, n, s) in blocks:
        L.norm_block(xT, c0, n, G1, mod1, 0, s, lambda k: hblk[:, k, :n], psb=7)
        if s == 0:
            P.dma("sp", ropeC[:, :n], ropeC_d[:, c0:c0 + n])
            P.dma("act", ropeS[:, :n], ropeS_d[:, c0:c0 + n])
        for pb in range(8):
            ps = L.ps[6]
            for k in range(8):
                P.mm(ps[:, :n], Wq[:, k, pb * 128:(pb + 1) * 128], hblk[:, k, :n], start=(k == 0), stop=(k == 7))
            if s == 0:
                L.qk_norm_rope(ps, 128, n, gq, blk1, 64, perm, ropeC[:, :n], ropeS[:, :n], QTb[:, pb, :n],
                               L.ps[7], L.ps[3])
            else:
                L.qk_norm_rope(ps, 128, n, gq, blk1, 64, None, None, None, QTb[:, pb, :n], L.ps[7], L.ps[3])
        lcs = list(range(34)) if s == 0 else [16, 17]
        pend = None
        for pb in range(8):
            m = pb // 4
            ob = 4 + 2 * (hcount % 2)
            hcount += 1
            pend = attend_pair(L,
                               lambda lc, m=m: K_all[0:64, m, lc * 128:(lc + 1) * 128],
                               lambda lc, m=m: K_all[64:128, m, lc * 128:(lc + 1) * 128],
                               QTb[0:64, pb, :n], QTb[64:128, pb, :n],
                               lambda lc, m=m: Vext[:, lc, vcol[2 * m]:vcol[2 * m] + 128],
                               lambda lc, m=m: Vext[:, lc, vcol[2 * m + 1]:vcol[2 * m + 1] + 128],
                               lcs, n, 0.125, L.ps[ob], L.ps[ob + 1], OTb[0:64, pb, :n], OTb[64:128, pb, :n],
                               ptiles, pending=pend)
        pend()
        for j in range(8):
            ps = L.ps[6 + (j % 2)]
            for pb in range(8):
                P.mm(ps[:, :n], Wo[:, pb, j * 128:(j + 1) * 128], OTb[:, pb, :n], start=(pb == 0), stop=(pb == 7))
            P.stt("dve", xT[:, j, c0:c0 + n], ps[:, :n], mod1[:, 16 + j, s:s + 1], xT[:, j, c0:c0 + n],
                  ALU.mult, ALU.add)
    hT = abf[:, 0:8 * NT].rearrange("p (k n) -> p k n", k=8)
    L.wslots = [abf[:, 18432 + i * 4096:18432 + (i + 1) * 4096] for i in range(4)]
    L.uT0 = abf[:, 34816:36864].rearrange("p (k n) -> p k n", k=4)
    L.uT1 = abf[:, 36864:38912].rearrange("p (k n) -> p k n", k=4)
    blocksF = [(c0, c0, n, s) for (c0, n, s) in blocks]
    L.ffn(xT, hT, blocksF, 1, mod1, G2)
    xT_out = L.outp("xT_out", [128, 8, NT])
    for k in range(8):
        P.dma(L.q(), xT_out[:, k, :], xT[:, k, :])
    return L


def prep_B(I, c, resA):
    b, half = c // 2, c % 2
    o = c ^ 1
    C, S = rope_tables(64, half * 2048, 2048)
    return {
        "ident_in": np.eye(128, dtype=np.float32),
        "xT_in": resA[c]["xT_out"],
        "mod1_in": resA[c]["mod_out"],
        "n1g1": cols(I["norm1_g"][1], 8), "n2g1": cols(I["norm2_g"][1], 8),
        "kT_own": resA[c]["kT_out"], "kT_oth": resA[o]["kT_out"][:, :, 0:2048],
        "v_own": resA[c]["v_out"], "v_oth": resA[o]["v_out"][:, 0:16, :],
        "gqa_wqkv": I["gqa_w_qkv"][0], "gqa_wo": I["gqa_w_o"][0],
        "gqa_qgc": np.tile(I["gqa_q_norm"][0], 2).reshape(128, 1),
        "perm64": perm_matrix(64, 2),
        "ropeC64": np.tile(C, (2, 1)), "ropeS64": np.tile(S, (2, 1)),
        "wff1_1": I["w_ff1"][1], "wff2_1": I["w_ff2"][1],
    }


NTC = 2320


def build_C():
    L = LB()
    P = L.P
    xT = L.sb("xT", [128, 8, NTC], F32)
    xT_in = L.inp("xT_in", [128, 8, NTC])
    for k in range(8):
        P.dma(L.q(), xT[:, k, :], xT_in[:, k, :])
    abf = L.sb("abf", [128, 38912], BF16)
    af = L.sb("af", [128, 7680], F32)
    L.af = af
    L.setup_c()
    mod2 = L.compute_mod(2)
    mod3 = L.compute_mod(3)
    n1g2 = L.load_cols("n1g2", [128, 8])
    n2g2 = L.load_cols("n2g2", [128, 8])
    n1g3 = L.load_cols("n1g3", [128, 8])
    G1 = L.make_gain("G1_2", mod2, 1, n1g2)
    G2 = L.make_gain("G2_2", mod2, 4, n2g2)
    G1_3 = L.make_gain("G1_3", mod3, 1, n1g3)
    pscale = L.load_cols("pool_scalec", [128, 8])
    psg = L.sb("psg", [128, 8, 2], F32)
    for s in range(2):
        P.tt("dve", psg[:, :, s], mod2[:, 16:24, s], pscale[:, :], ALU.mult)
    hmask = L.load_cols("hmask", [128, 16])
    invc = L.load_cols("invc", [128, 2, 4, 16])
    Wp = abf[:, 0:2048].rearrange("p (g c n) -> p g c n", g=4, c=2)
    pblk = abf[:, 2048:6144].rearrange("p (k n) -> p k n", k=8)
    w_pool = L.inp("pool_w", [4, 256, 256])
    for g in range(4):
        L.wload(Wp[:, g, :, :], w_pool[g].rearrange("(c p) n -> p c n", p=128))
    hE = af[:, 0:4224].rearrange("p (k n) -> p k n", k=8)
    wa = af[:, 4224:4752]
    wb = af[:, 4752:5280]
    rstd_all = af[:, 5280:7600]
    for (c0, n) in [(0, 512), (512, 512), (1024, 512), (1536, 512), (2048, 272)]:
        P.act(L.sqb[:, :, :n], xT[:, :, c0:c0 + n], AF.Square)
        ps = L.ps[6]
        for k in range(8):
            P.mm(ps[:, :n], L.ones_bf[:], L.sqb[:, k, :n], start=(k == 0), stop=(k == 7))
        L.rsqrt(rstd_all[:, c0:c0 + n], ps[:, :n], 1024.0 * EPS)
    blocksP = [(8 + tb * 512, 512, 0, tb) for tb in range(4)] + [(2064, 256, 1, 4)]
    for (c0, n, s, bi) in blocksP:
        ne = n + 16
        if s == 0:
            for k in range(8):
                t = L.tmp()
                for (a0, a1) in ((0, 512), (512, ne)):
                    P.tt(L.ve(), t[:, 0:a1 - a0], xT[:, k, c0 - 8 + a0:c0 - 8 + a1],
                         rstd_all[:, c0 - 8 + a0:c0 - 8 + a1], ALU.mult)
                    P.act(hE[:, k, a0:a1], t[:, 0:a1 - a0], AF.Identity, scale=G1[:, k, s:s + 1],
                          bias=mod2[:, k, s:s + 1])
                if bi == 0:
                    P.tt("pool", hE[:, k, 0:8], hE[:, k, 0:8], hmask[:, 0:8], ALU.mult)
                if bi == 3:
                    P.tt("pool", hE[:, k, ne - 8:ne], hE[:, k, ne - 8:ne], hmask[:, 8:16], ALU.mult)
        else:
            P.memset("pool", hE[:, :, 0:8], 0.0)
            P.memset("pool", hE[:, :, 8 + n:16 + n], 0.0)
            for k in range(8):
                t = L.tmp()
                P.tt(L.ve(), t[:, :n], xT[:, k, c0:c0 + n], rstd_all[:, c0:c0 + n], ALU.mult)
                P.act(hE[:, k, 8:8 + n], t[:, :n], AF.Identity, scale=G1[:, k, s:s + 1], bias=mod2[:, k, s:s + 1])
        for k in range(8):
            wi = k // 2
            w = 2 << wi
            eng = "dve" if k % 2 == 0 else "pool"
            cur = hE[:, k, :]
            ln_ = ne
            step = 1
            bufs = [wa, wb]
            bi_ = 0
            while step < w:
                nxt = bufs[bi_ % 2]
                bi_ += 1
                ln2 = ln_ - step
                P.tt(eng, nxt[:, 0:ln2], cur[:, 0:ln2], cur[:, step:step + ln2], ALU.add)
                cur = nxt
                ln_ = ln2
                step *= 2
            off = 8 - w // 2
            P.stt("dve", pblk[:, k, :n], cur[:, off:off + n], 1.0 / w, hE[:, k, 8:8 + n], ALU.mult, ALU.subtract)
            edges = []
            if s == 1 or bi == 0:
                edges.append((0, 0))
            if s == 1 or bi == 3:
                edges.append((n - 8, 8))
            for (t0, e0) in edges:
                t = L.tmp()
                P.tt("dve", t[:, 0:8], cur[:, off + t0:off + t0 + 8], invc[:, s, wi, e0:e0 + 8], ALU.mult)
                P.tt("dve", pblk[:, k, t0:t0 + 8], t[:, 0:8], hE[:, k, 8 + t0:16 + t0], ALU.subtract)
        for g in range(4):
            for oc in range(2):
                j = 2 * g + oc
                ps = L.ps[j % 4]
                for kc in range(2):
                    P.mm(ps[:, :n], Wp[:, g, kc, oc * 128:(oc + 1) * 128], pblk[:, 2 * g + kc, :n],
                         start=(kc == 0), stop=(kc == 1))
                P.stt("dve", xT[:, j, c0:c0 + n], ps[:, :n], psg[:, j, s:s + 1], xT[:, j, c0:c0 + n],
                      ALU.mult, ALU.add)
    hT = abf[:, 0:8 * NT].rearrange("p (k n) -> p k n", k=8)
    L.wslots = [abf[:, 18432 + i * 4096:18432 + (i + 1) * 4096] for i in range(4)]
    L.uT0 = abf[:, 34816:36864].rearrange("p (k n) -> p k n", k=4)
    L.uT1 = abf[:, 36864:38912].rearrange("p (k n) -> p k n", k=4)
    blocksF = [(8 + tb * 512, tb * 512, 512, 0) for tb in range(4)] + [(2064, 2048, 256, 1)]
    L.ffn(xT, hT, blocksF, 2, mod2, G2)
    mod_out = L.outp("mod_out", [128, 48, 2])
    P.dma("sp", mod_out, mod3[:])
    xT_out = L.outp("xT_out", [128, 8, NT])
    P.dma("sp", xT_out[:, :, 0:2048], xT[:, :, 8:2056])
    P.dma("act", xT_out[:, :, 2048:2304], xT[:, :, 2064:2320])
    for (xc0, hc0, n, s) in blocksF:
        L.norm_block(xT, xc0, n, G1_3, mod3, 0, s, lambda k: hT[:, k, hc0:hc0 + n])
    Wd = abf[:, 18432:21504].rearrange("p (k n) -> p k n", k=8)
    Wukv = abf[:, 21504:25600].rearrange("p (c n) -> p c n", c=2)
    ckvn = abf[:, 25600:26624].rearrange("p (c n) -> p c n", c=2)
    krg_bf = abf[:, 26624:27136]
    sq_r = abf[:, 27136:27648]
    sq_n = abf[:, 27648:28160]
    w_dkv = L.inp("mla_wdkv", [1024, 288]).rearrange("(k p) n -> p k n", p=128)
    w_ukv = L.inp("mla_wukv", [256, 2048]).rearrange("(c p) n -> p c n", p=128)
    P.memset("dve", Wd[:, :, 256:384], 0.0)
    L.wload(Wd[:, :, 0:256], w_dkv[:, :, 0:256])
    L.wload(Wd[:, :, 320:352], w_dkv[:, :, 256:288])
    L.wload(Wukv[:, :, :], w_ukv)
    gkvl = L.load_cols("mla_kvlgc", [128, 2])
    gk96 = L.load_cols("mla_kgc", [128, 1])
    permf = L.load_cols("perm96", [128, 128])
    perm = L.sb("perm_bf", [128, 128], BF16)
    P.copy("dve", perm[:], permf[:])
    ropeC_d = L.inp("ropeC96", [128, 2048])
    ropeS_d = L.inp("ropeS96", [128, 2048])
    ropeC = af[:, 0:512]
    ropeS = af[:, 512:1024]
    krr = af[:, 1024:1536]
    khs = [af[:, 1536:2048], af[:, 2048:2560]]
    vst = af[:, 2560:6656].rearrange("p (t n) -> p t n", t=4)
    kT3_out = L.outp("kT3_out", [96, 16, NT])
    v3_out = L.outp("v3_out", [128, 18, 1024])
    Wukv4 = Wukv.rearrange("p c (h n) -> p c h n", h=16)
    kcnt = 0
    for (xc0, hc0, n, s) in blocksF:
        if s == 0:
            P.dma("sp", ropeC[:, :n], ropeC_d[:, hc0:hc0 + n])
            P.dma("act", ropeS[:, :n], ropeS_d[:, hc0:hc0 + n])
        for cc in range(2):
            ps = L.ps[cc]
            for k in range(8):
                P.mm(ps[:, :n], Wd[:, k, cc * 128:(cc + 1) * 128], hT[:, k, hc0:hc0 + n], start=(k == 0), stop=(k == 7))
            P.act(L.sqb[:, cc, :n], ps[:, :n], AF.Square)
        pss = L.ps[2]
        for cc in range(2):
            P.mm(pss[:, :n], L.ones_bf[:], L.sqb[:, cc, :n], start=(cc == 0), stop=(cc == 1))
        rs = L.tmp()
        L.rsqrt(rs[:, :n], pss[:, :n], 256.0 * EPS)
        for cc in range(2):
            t = L.tmp()
            P.act(t[:, :n], L.ps[cc][:, :n], AF.Identity, scale=gkvl[:, cc:cc + 1])
            P.stt("dve", ckvn[:, cc, :n], t[:, :n], 16.0, rs[:, :n], ALU.mult, ALU.mult)
        psr = L.ps[3]
        for k in range(8):
            P.mm(psr[:, :n], Wd[:, k, 256:384], hT[:, k, hc0:hc0 + n], start=(k == 0), stop=(k == 7))
        P.act(sq_r[64:96, :n], psr[64:96, :n], AF.Square)
        krg = L.tmp()
        P.act(krg[64:96, :n], psr[64:96, :n], AF.Identity, scale=gk96[64:96, 0:1])
        if s == 0:
            P.copy("pool", krg_bf[64:96, :n], krg[64:96, :n])
            psw = L.ps[4]
            P.mm(psw[0:96, :n], perm[64:96, 0:96], krg_bf[64:96, :n])
            t2 = L.tmp()
            P.tt("dve", t2[64:96, :n], psw[64:96, :n], ropeS[64:96, :n], ALU.mult)
            P.tt("dve", krr[64:96, :n], krg[64:96, :n], ropeC[64:96, :n], ALU.mult)
            P.tt("pool", krr[64:96, :n], krr[64:96, :n], t2[64:96, :n], ALU.add)
        else:
            P.copy("dve", krr[64:96, :n], krg[64:96, :n])
        for h in range(16):
            psk = L.ps[5 + (h % 2)]
            for cc in range(2):
                P.mm(psk[0:64, :n], Wukv4[:, cc, h, 0:64], ckvn[:, cc, :n], start=(cc == 0), stop=(cc == 1))
            P.act(sq_n[0:64, :n], psk[0:64, :n], AF.Square)
            ps2 = L.ps[7]
            P.mm(ps2[0:96, :n], L.ones_bf[0:64, 0:96], sq_n[0:64, :n], start=True, stop=False)
            P.mm(ps2[0:96, :n], L.ones_bf[64:96, 0:96], sq_r[64:96, :n], start=False, stop=True)
            rs96 = L.tmp()
            L.rsqrt(rs96[0:96, :n], ps2[0:96, :n], 96.0 * EPS)
            t = L.tmp()
            P.act(t[0:64, :n], psk[0:64, :n], AF.Identity, scale=gk96[0:64, 0:1])
            kh = khs[kcnt % 2]
            kcnt += 1
            P.stt("dve", kh[0:64, :n], t[0:64, :n], 96.0 ** 0.5, rs96[0:64, :n], ALU.mult, ALU.mult)
            P.stt("dve", kh[64:96, :n], krr[64:96, :n], 96.0 ** 0.5, rs96[64:96, :n], ALU.mult, ALU.mult)
            P.dma(L.q(), kT3_out[:, h, hc0:hc0 + n], kh[0:96, :n])
        nt = n // 128
        for tt_ in range(nt):
            for hh in range(2):
                ps = L.ps[hh]
                for cc in range(2):
                    P.mm(ps[:, 0:512].rearrange("p (h n) -> p h n", h=8),
                         ckvn[:, cc, tt_ * 128:(tt_ + 1) * 128], Wukv4[:, cc, hh * 8:(hh + 1) * 8, 64:128],
                         start=(cc == 0), stop=(cc == 1))
                P.copy("act" if hh else "dve", vst[:, tt_, hh * 512:(hh + 1) * 512], ps[:, 0:512])
        P.dma(L.q(), v3_out[:, hc0 // 128:hc0 // 128 + nt, :], vst[:, :nt, :])
    return L


def pool_invc(half):
    t = np.zeros((128, 2, 4, 16), np.float32)
    for wi, w in enumerate((2, 4, 8, 16)):
        for seg in range(2):
            for e in range(8):
                cnt_s = (e + w // 2) - max(e - w // 2, 0)
                cnt_e = w // 2 + min(w // 2, 8 - e)
                real_s = (seg == 1) or (half == 0)
                real_e = (seg == 1) or (half == 1)
                t[:, seg, wi, e] = 1.0 / (cnt_s if real_s else w)
                t[:, seg, wi, 8 + e] = 1.0 / (cnt_e if real_e else w)
    return t


def prep_C(I, c, resB):
    b, half = c // 2, c % 2
    o = c ^ 1
    own = resB[c]["xT_out"]
    oth = resB[o]["xT_out"]
    z8 = np.zeros((128, 8, 8), np.float32)
    left = oth[:, :, 2040:2048] if half == 1 else z8
    right = oth[:, :, 0:8] if half == 0 else z8
    xT_in = np.concatenate([left, own[:, :, 0:2048], right, own[:, :, 2048:2304]], 2)
    hmask = np.zeros((128, 16), np.float32)
    hmask[:, 0:8] = 1.0 if half == 1 else 0.0
    hmask[:, 8:16] = 1.0 if half == 0 else 0.0
    C32, S32 = rope_tables(32, half * 2048, 2048)
    C = np.ones((128, 2048), np.float32)
    S = np.zeros((128, 2048), np.float32)
    C[64:96] = C32
    S[64:96] = S32
    gk = np.zeros((128, 1), np.float32)
    gk[0:96, 0] = I["mla_k_norm"][0]
    return {
        "ident_in": np.eye(128, dtype=np.float32),
        "xT_in": xT_in, "hmask": hmask, "invc": pool_invc(half),
        "cvec": np.stack([cols(I["c"][b], 8), cols(I["c_ctx"], 8)], -1),
        "wmod2": I["w_mod"][2], "bmodc2": cols(I["b_mod"][2], 48),
        "wmod3": I["w_mod"][3], "bmodc3": cols(I["b_mod"][3], 48),
        "n1g2": cols(I["norm1_g"][2], 8), "n2g2": cols(I["norm2_g"][2], 8), "n1g3": cols(I["norm1_g"][3], 8),
        "pool_scalec": cols(I["pool_scale"][0], 8), "pool_w": I["pool_w"][0],
        "wff1_2": I["w_ff1"][2], "wff2_2": I["w_ff2"][2],
        "mla_wdkv": I["mla_w_dkv"][0], "mla_wukv": I["mla_w_ukv"][0],
        "mla_kvlgc": cols(I["mla_kv_lora_norm"][0], 2), "mla_kgc": gk,
        "perm96": perm_matrix(32, 1, base=64),
        "ropeC96": C, "ropeS96": S,
    }


NL = 2048


def build_D():
    L = LB()
    P = L.P
    nc = L.nc
    xT = L.sb("xT", [128, 8, NL], F32)
    xT_in = L.inp("xT_in", [128, 8, NL])
    for k in range(8):
        P.dma(L.q(), xT[:, k, :], xT_in[:, k, :])
    abf = L.sb("abf", [128, 43008], BF16)
    af = L.sb("af", [128, 4096], F32)
    L.af = af
    mod3 = L.load_cols("mod3_in", [128, 48, 2])
    n1g = L.load_cols("n1g3", [128, 8])
    n2g = L.load_cols("n2g3", [128, 8])
    G1 = L.make_gain("G1", mod3, 1, n1g)
    G2 = L.make_gain("G2", mod3, 4, n2g)
    ebias = L.sb("ebias", [128, 1], F32)
    P.memset("dve", ebias[:], -4.0)
    hblk = abf[:, 0:4096].rearrange("p (k n) -> p k n", k=8)
    Wdq = abf[:, 4096:10240].rearrange("p (k n) -> p k n", k=8)
    Wuq = abf[:, 10240:19456].rearrange("p (c n) -> p c n", c=6)
    cqn = abf[:, 19456:22528].rearrange("p (c n) -> p c n", c=6)
    qst = [abf[:, 22528 + i * 512:23040 + i * 512] for i in range(2)]
    w_dq = L.inp("mla_wdq", [1024, 768]).rearrange("(k p) n -> p k n", p=128)
    w_uq = L.inp("mla_wuq", [768, 1536]).rearrange("(c p) n -> p c n", p=128)
    L.wload(Wdq[:, :, 0:384], w_dq[:, :, 0:384])
    L.wload(Wdq[:, :, 384:768], w_dq[:, :, 384:768])
    for s3 in range(3):
        L.wload(Wuq[:, :, s3 * 512:(s3 + 1) * 512], w_uq[:, :, s3 * 512:(s3 + 1) * 512])
    gql = L.load_cols("mla_qlgc", [128, 6])
    gq96 = L.load_cols("mla_qgc", [128, 1])
    permf = L.load_cols("perm96", [128, 128])
    perm = L.sb("perm_bf", [128, 128], BF16)
    P.copy("dve", perm[:], permf[:])
    ropeC_d = L.inp("ropeC96", [128, 2048])
    ropeS_d = L.inp("ropeS96", [128, 2048])
    ropeC = af[:, 0:512]
    ropeS = af[:, 512:1024]
    qscr = nc.dram_tensor("qscratch", [16, 96, NL], BF16, kind="Internal").ap()
    qc = 0
    for tb in range(4):
        c0, n = tb * 512, 512
        L.norm_block(xT, c0, n, G1, mod3, 0, 0, lambda k: hblk[:, k, :n], psb=7)
        P.dma("sp", ropeC[:, :n], ropeC_d[:, c0:c0 + n])
        P.dma("act", ropeS[:, :n], ropeS_d[:, c0:c0 + n])
        for cc in range(6):
            ps = L.ps[cc]
            for k in range(8):
                P.mm(ps[:, :n], Wdq[:, k, cc * 128:(cc + 1) * 128], hblk[:, k, :n], start=(k == 0), stop=(k == 7))
            P.act(L.sqb[:, cc, :n], ps[:, :n], AF.Square)
        pss = L.ps[6]
        for cc in range(6):
            P.mm(pss[:, :n], L.ones_bf[:], L.sqb[:, cc, :n], start=(cc == 0), stop=(cc == 5))
        rs = L.rstd
        L.rsqrt(rs[:, :n], pss[:, :n], 768.0 * EPS)
        for cc in range(6):
            t = L.tmp()
            P.act(t[:, :n], L.ps[cc][:, :n], AF.Identity, scale=gql[:, cc:cc + 1])
            P.stt("dve", cqn[:, cc, :n], t[:, :n], 768.0 ** 0.5, rs[:, :n], ALU.mult, ALU.mult)
        for h in range(16):
            ps = L.ps[h % 2]
            for cc in range(6):
                P.mm(ps[0:96, :n], Wuq[:, cc, h * 96:(h + 1) * 96], cqn[:, cc, :n], start=(cc == 0), stop=(cc == 5))
            qo = qst[qc % 2]
            qc += 1
            L.qk_norm_rope(ps, 96, n, gq96, L.ones_bf, 96, perm, ropeC[0:96, :n], ropeS[0:96, :n], qo[0:96, :n],
                           L.ps[2 + (h % 2)], L.ps[4 + (h % 2)])
            P.dma(L.q(), qscr[h, :, c0:c0 + n], qo[0:96, :n])
    OT = abf[:, 0:16384].rearrange("p (k n) -> p k n", k=8)
    Kh = [abf[:, 16384 + i * 4352:16384 + (i + 1) * 4352] for i in range(2)]
    Vh = [abf[:, 25088 + i * 6528:25088 + (i + 1) * 6528].rearrange("p (c n) -> p c n", c=34) for i in range(2)]
    Qh = [abf[:, 38144 + i * 2048:38144 + (i + 1) * 2048] for i in range(2)]
    ptiles = [L.sb("ptile%d" % i, [128, 1024], BF16) for i in range(2)]
    kT_own = L.inp("kT3_own", [96, 16, NT])
    kT_oth = L.inp("kT3_oth", [96, 16, NL])
    v_own = L.inp("v3_own", [128, 18, 1024])
    v_oth = L.inp("v3_oth", [128, 16, 1024])
    for i in range(2):
        P.memset("dve", Vh[i][:, :, 0:64], 0.0)
        P.memset("dve", Vh[i][:, :, 128:192], 0.0)
        P.memset("dve", Vh[i][:, :, 0:1], 1.0)
        P.memset("dve", Vh[i][:, :, 128:129], 1.0)

    def load_head(h):
        i = h % 2
        L.wload(Kh[i][0:96, 0:NT], kT_own[:, h, :])
        L.wload(Kh[i][0:96, NT:4352], kT_oth[:, h, :])
        L.wload(Vh[i][:, 0:18, 64:128], v_own[:, :, h * 64:(h + 1) * 64])
        L.wload(Vh[i][:, 18:34, 64:128], v_oth[:, :, h * 64:(h + 1) * 64])
        P.dma("sp", Qh[i][0:96, :], qscr[h, :, :])

    load_head(0)
    hcount = 0
    pend = None
    for h in range(16):
        if h + 1 < 16:
            load_head(h + 1)
        i = h % 2
        side = h % 2
        sl0, sl1 = (0, 64) if side == 0 else (64, 128)
        vc = 64 if side == 0 else 0
        for qb in range(4):
            ps_o = L.ps[4 + (hcount % 2)]
            hcount += 1
            pend = attend_head(L,
                               lambda lc, i=i: Kh[i][0:96, lc * 128:(lc + 1) * 128],
                               Qh[i][0:96, qb * 512:(qb + 1) * 512],
                               lambda lc, i=i, vc=vc: Vh[i][:, lc, vc:vc + 128],
                               list(range(34)), 512, 96.0 ** -0.5, ps_o, side,
                               OT[sl0:sl1, h // 2, qb * 512:(qb + 1) * 512], ptiles, ebias=ebias[:, 0:1],
                               pending=pend)
    pend()
    Wo = abf[:, 16384:24576].rearrange("p (k n) -> p k n", k=8)
    w_o = L.inp("mla_wo", [1024, 1024]).rearrange("(k p) n -> p k n", p=128)
    for s2 in range(2):
        L.wload(Wo[:, :, s2 * 512:(s2 + 1) * 512], w_o[:, :, s2 * 512:(s2 + 1) * 512])
    for qb in range(4):
        for j in range(8):
            ps = L.ps[6 + (j % 2)]
            for pb in range(8):
                P.mm(ps[:, :512], Wo[:, pb, j * 128:(j + 1) * 128], OT[:, pb, qb * 512:(qb + 1) * 512],
                     start=(pb == 0), stop=(pb == 7))
            P.stt("dve", xT[:, j, qb * 512:(qb + 1) * 512], ps[:, :512], mod3[:, 16 + j, 0:1],
                  xT[:, j, qb * 512:(qb + 1) * 512], ALU.mult, ALU.add)
    hT = abf[:, 0:8 * NL].rearrange("p (k n) -> p k n", k=8)
    L.wslots = [abf[:, 16384 + i * 4096:16384 + (i + 1) * 4096] for i in range(4)]
    L.uT0 = abf[:, 32768:34816].rearrange("p (k n) -> p k n", k=4)
    L.uT1 = abf[:, 34816:36864].rearrange("p (k n) -> p k n", k=4)
    blocksF = [(tb * 512, tb * 512, 512, 0) for tb in range(4)]
    L.ffn(xT, hT, blocksF, 3, mod3, G2)
    out = L.outp("out", [NL, 1024])
    ost = [af[:, i * 1024:(i + 1) * 1024] for i in range(2)]
    for r in range(16):
        st = ost[r % 2]
        for half in range(2):
            ps = L.ps[(r * 2 + half) % 4]
            for kk in range(4):
                k = half * 4 + kk
                P.transpose(ps[:, kk * 128:(kk + 1) * 128], xT[:, k, r * 128:(r + 1) * 128], L.ident[:, :])
            P.copy("act" if half else "dve", st[:, half * 512:(half + 1) * 512], ps[:, 0:512])
        P.dma(L.q(), out[r * 128:(r + 1) * 128, :], st[:, :])
    return L


def prep_D(I, c, resC):
    b, half = c // 2, c % 2
    o = c ^ 1
    C32, S32 = rope_tables(32, half * 2048, 2048)
    C = np.ones((128, 2048), np.float32)
    S = np.zeros((128, 2048), np.float32)
    C[64:96] = C32
    S[64:96] = S32
    gq = np.zeros((128, 1), np.float32)
    gq[0:96, 0] = I["mla_q_norm"][0]
    return {
        "ident_in": np.eye(128, dtype=np.float32),
        "xT_in": resC[c]["xT_out"][:, :, 0:2048],
        "mod3_in": resC[c]["mod_out"],
        "n1g3": cols(I["norm1_g"][3], 8), "n2g3": cols(I["norm2_g"][3], 8),
        "mla_wdq": I["mla_w_dq"][0], "mla_wuq": I["mla_w_uq"][0], "mla_wo": I["mla_w_o"][0],
        "mla_qlgc": cols(I["mla_q_lora_norm"][0], 6), "mla_qgc": gq,
        "perm96": perm_matrix(32, 1, base=64),
        "ropeC96": C, "ropeS96": S,
        "kT3_own": resC[c]["kT3_out"], "kT3_oth": resC[o]["kT3_out"][:, :, 0:2048],
        "v3_own": resC[c]["v3_out"], "v3_oth": resC[o]["v3_out"][:, 0:16, :],
        "wff1_3": I["w_ff1"][3], "wff2_3": I["w_ff2"][3],
    }


def kernel(**inputs):
    I = {k: np.asarray(v) for k, v in inputs.items()}
    resA = run_launch("A", build_A, [prep_A(I, c) for c in range(CORES)])
    resB = run_launch("B", build_B, [prep_B(I, c, resA) for c in range(CORES)])
    resC = run_launch("C", build_C, [prep_C(I, c, resB) for c in range(CORES)])
    resD = run_launch("D", build_D, [prep_D(I, c, resC) for c in range(CORES)])
    out = np.empty((4, 4096, 1024), np.float32)
    for c in range(CORES):
        out[c // 2, (c % 2) * 2048:(c % 2 + 1) * 2048, :] = resD[c]["out"]
    return out
```

```python
from concourse.bass_utils import run_bass_kernel_spmd

import numpy as np
from contextlib import ExitStack
import concourse.bass as bass
import concourse.mybir as mybir

F32 = mybir.dt.float32
BF16 = mybir.dt.bfloat16
AF = mybir.ActivationFunctionType
ALU = mybir.AluOpType

ENGS = ["pe", "act", "dve", "pool", "sp"]
N_DMA_SEMS = 6


def _region(ap):
    t = ap.tensor
    dims = ap.ap
    off = int(ap.offset)
    cls = type(t).__name__
    if cls.startswith("DRam"):
        ext = 1
        for s, c in dims:
            ext += (c - 1) * abs(s)
        return (t.name, 0, 1, off, off + ext)
    pstep, pcnt = dims[0]
    p0 = off // pstep if pstep > 0 else 0
    lo = off - p0 * pstep
    ext = 1
    for s, c in dims[1:]:
        ext += (c - 1) * abs(s)
    return (t.name, p0, p0 + pcnt, lo, lo + ext)


def _overlap(a, b):
    return a[1] < b[2] and b[1] < a[2] and a[3] < b[4] and b[3] < a[4]


class Op:
    __slots__ = ("eng", "emit", "reads", "writes", "is_dma", "idx", "waits",
                 "signal", "dsem", "dwait_prev", "clock", "dma_inc")


class Prog:
    def __init__(self, nc, same_engine_sync=True):
        self.nc = nc
        self.ops = {e: [] for e in ENGS}
        self.same_engine_sync = same_engine_sync
        self.track = {}
        self.know = {e: {} for e in ENGS}
        self.dma_rr = {e: 0 for e in ENGS}
        self.dma_last = {e: [None] * N_DMA_SEMS for e in ENGS}
        self.dma_count = {e: [0] * N_DMA_SEMS for e in ENGS}
        self.nops = 0
        self.notrack = set()

    def _add(self, eng, emit, reads, writes, is_dma=False, dma_inc=16):
        op = Op()
        op.eng = eng
        op.emit = emit
        op.is_dma = is_dma
        op.dma_inc = dma_inc
        op.idx = len(self.ops[eng])
        op.waits = {}
        op.signal = False
        op.dsem = None
        rr = [_region(a) for a in reads if a.tensor.name not in self.notrack]
        wr = [_region(a) for a in writes]
        deps = []
        for r in rr:
            for rec in self.track.get(r[0], ()):
                if rec[1] is not None and _overlap(rec[0], r):
                    deps.append(rec[1])
        for w in wr:
            for rec in self.track.get(w[0], ()):
                if _overlap(rec[0], w):
                    if rec[1] is not None and not (rec[1][0] == "E" and rec[1][1] == eng and not is_dma):
                        deps.append(rec[1])
                    for tok in rec[2].values():
                        if not (tok[0] == "E" and tok[1] == eng and not is_dma):
                            deps.append(tok)
        know = self.know[eng]
        if is_dma:
            q = self.dma_rr[eng]
            self.dma_rr[eng] = (q + 1) % N_DMA_SEMS
            prev = self.dma_last[eng][q]
            if prev is not None:
                deps.append(prev)
            self.dma_count[eng][q] += 1
            op.dsem = q
            mytok = ("D", eng, q, self.dma_count[eng][q], None)
            self.dma_last[eng][q] = mytok
        else:
            mytok = ("E", eng, op.idx, None)
        for tok in deps:
            if tok[0] == "E":
                _, e2, i2, _ = tok
                if e2 == eng:
                    if eng == "pe" or not self.same_engine_sync:
                        continue
                key = ("E", e2)
                val = i2
            else:
                _, e2, q2, c2, _ = tok
                key = ("D", e2, q2)
                val = c2
            if know.get(key, -1) >= val:
                continue
            if op.waits.get(key, -1) < val:
                op.waits[key] = val
        for key, val in op.waits.items():
            know[key] = max(know.get(key, -1), val)
            if key[0] == "E":
                src = self.ops[key[1]][val]
                src.signal = True
                for k2, v2 in src.clock.items():
                    if know.get(k2, -1) < v2:
                        know[k2] = v2
        op.clock = dict(know)
        skey = eng if not is_dma else ("dma", eng, op.dsem)
        for r in rr:
            lst = self.track.setdefault(r[0], [])
            for rec in lst:
                if rec[0] == r:
                    rec[2][skey] = mytok
                    break
            else:
                lst.append([r, None, {skey: mytok}])
        for w in wr:
            lst = self.track.setdefault(w[0], [])
            newl = []
            found = False
            for rec in lst:
                r0 = rec[0]
                if r0 == w:
                    rec[1] = mytok
                    rec[2] = {}
                    newl.append(rec)
                    found = True
                elif (w[1] <= r0[1] and r0[2] <= w[2] and w[3] <= r0[3] and r0[4] <= w[4]):
                    continue
                else:
                    newl.append(rec)
            if not found:
                newl.append([w, mytok, {}])
            self.track[w[0]] = newl
        self.ops[eng].append(op)
        self.nops += 1
        return op

    def mm(self, out, lhsT, rhs, start=True, stop=True, **kw):
        rd = [lhsT, rhs] + ([] if start else [out])
        return self._add("pe", lambda e: e.matmul(out, lhsT, rhs, start=start, stop=stop, **kw), rd, [out])

    def transpose(self, out, in_, ident):
        return self._add("pe", lambda e: e.transpose(out, in_, ident), [in_, ident], [out])

    def act(self, out, in_, func, bias=None, scale=None, accum_out=None, eng="act"):
        kw = {}
        rd = [in_]
        if bias is not None:
            kw["bias"] = bias
            if not isinstance(bias, (int, float)):
                rd.append(bias)
        if scale is not None:
            kw["scale"] = scale
            if not isinstance(scale, (int, float)):
                rd.append(scale)
        wr = [out]
        if accum_out is not None:
            kw["accum_out"] = accum_out
            wr.append(accum_out)
        return self._add("act", lambda e: e.activation(out, in_, func, **kw), rd, wr)

    def tt(self, eng, out, a, b, op):
        return self._add(eng, lambda e: e.tensor_tensor(out, a, b, op), [a, b], [out])

    def ts(self, eng, out, a, s1, s2, op0, op1=None, accum_out=None):
        rd = [a]
        if not isinstance(s1, (int, float)):
            rd.append(s1)
        if s2 is not None and not isinstance(s2, (int, float)):
            rd.append(s2)
        wr = [out]
        kw = {}
        if accum_out is not None:
            kw["accum_out"] = accum_out
            wr.append(accum_out)
        if op1 is None:
            return self._add(eng, lambda e: e.tensor_scalar(out, a, s1, None, op0, **kw), rd, wr)
        return self._add(eng, lambda e: e.tensor_scalar(out, a, s1, s2, op0, op1, **kw), rd, wr)

    def stt(self, eng, out, a, s, b, op0, op1):
        rd = [a, b]
        if not isinstance(s, (int, float)):
            rd.append(s)
        return self._add(eng, lambda e: e.scalar_tensor_tensor(out, a, s, b, op0, op1), rd, [out])

    def copy(self, eng, out, in_):
        if eng == "act":
            return self._add("act", lambda e: e.copy(out, in_), [in_], [out])
        return self._add(eng, lambda e: e.tensor_copy(out, in_), [in_], [out])

    def memset(self, eng, out, val):
        return self._add(eng, lambda e: e.memset(out, val), [], [out])

    def recip(self, out, in_):
        return self._add("dve", lambda e: e.reciprocal(out, in_), [in_], [out])

    def dma(self, q, out, in_, **kw):
        return self._add(q, lambda e: e.dma_start(out, in_, **kw), [in_], [out], is_dma=True)

    def custom(self, eng, emit, reads, writes, is_dma=False, dma_inc=16):
        return self._add(eng, emit, reads, writes, is_dma=is_dma, dma_inc=dma_inc)

    def emit(self, final_wait_all_dma=True):
        nc = self.nc
        with ExitStack() as st:
            esem = {e: st.enter_context(nc.semaphore("se_" + e)) for e in ENGS}
            dsem = {e: [st.enter_context(nc.semaphore("sd_%s%d" % (e, i))) for i in range(N_DMA_SEMS)]
                    for e in ENGS}
            sigcount = {}
            for e in ENGS:
                c = 0
                arr = []
                for op in self.ops[e]:
                    if (not op.is_dma) and op.signal:
                        c += 1
                    arr.append(c)
                sigcount[e] = arr
            block = st.enter_context(nc.Block())
            engobj = {"pe": block.tensor, "act": block.scalar, "dve": block.vector,
                      "pool": block.gpsimd, "sp": block.sync}

            def make(e):
                def body(eng):
                    for op in self.ops[e]:
                        for key, val in op.waits.items():
                            if key[0] == "E":
                                eng.wait_ge(esem[key[1]], sigcount[key[1]][val])
                            else:
                                eng.wait_ge(dsem[key[1]][key[2]], val * 16)
                        ins = op.emit(eng)
                        if op.is_dma:
                            ins.then_inc(dsem[e][op.dsem], op.dma_inc)
                        elif op.signal:
                            ins.then_inc(esem[e], 1)
                    for q in range(N_DMA_SEMS):
                        if self.dma_count[e][q] > 0:
                            eng.wait_ge(dsem[e][q], self.dma_count[e][q] * 16)
                return body

            for e in ENGS:
                engobj[e](make(e))

EPS = 1e-6
NT = 2304
CORES = 8
GRID_W = 64


def chunks(total, step=512):
    out = []
    c = 0
    while c < total:
        out.append((c, min(step, total - c)))
        c += step
    return out


class LB:
    def __init__(self):
        self.nc = bass.Bass("TRN2", target_bir_lowering=False)
        self.P = Prog(self.nc)
        self.psbig = self.nc.alloc_psum_tensor("psbig", [128, 4096], F32)
        self.ps = [self.psbig[:, i * 512:(i + 1) * 512] for i in range(8)]
        self.in_names = []
        self.out_names = []
        self.dq = 0
        P = self.P
        self.ident = self.sb("ident", [128, 128], F32)
        P.dma("sp", self.ident[:], self.inp("ident_in", [128, 128]))
        self.ones_bf = self.sb("ones_bf", [128, 128], BF16)
        P.memset("dve", self.ones_bf[:], 1.0)
        self.ones_f = self.sb("ones_f", [128, 128], F32)
        P.memset("dve", self.ones_f[:], 1.0)
        self.sqb = self.sb("sqb", [128, 8, 512], BF16)
        self.rstd = self.sb("rstd", [128, 512], F32)
        self.tmpf = [self.sb("tmpf%d" % i, [128, 512], F32) for i in range(4)]
        self.tmpi = 0
        self.rr = 0

    def inp(self, name, shape, dt=F32):
        t = self.nc.dram_tensor(name, list(shape), dt, kind="ExternalInput")
        self.P.notrack.add(name)
        self.in_names.append(name)
        return t.ap()

    def outp(self, name, shape, dt=F32):
        t = self.nc.dram_tensor(name, list(shape), dt, kind="ExternalOutput")
        self.out_names.append(name)
        return t.ap()

    def sb(self, name, shape, dt):
        return self.nc.alloc_sbuf_tensor(name, list(shape), dt)

    def q(self):
        self.dq ^= 1
        return "sp" if self.dq else "act"

    def tmp(self):
        self.tmpi = (self.tmpi + 1) % 4
        return self.tmpf[self.tmpi]

    def ve(self):
        self.rr ^= 1
        return "dve" if self.rr else "pool"

    def cconst(self, val):
        if not hasattr(self, "_cc"):
            self._cc = {}
        if val not in self._cc:
            t = self.sb("cc%d" % len(self._cc), [128, 1], F32)
            self.P.memset("dve", t[:], float(val))
            self._cc[val] = t
        return self._cc[val]

    def rsqrt(self, out, in_ps, addc):
        shp = list(out.shape)
        p0 = out.base_partition()
        cb = self.cconst(float(addc))
        self.P.act(out, in_ps, AF.Ln, bias=cb[p0:p0 + shp[0], 0:1])
        self.P.act(out, out, AF.Exp, scale=-0.5)

    def load_cols(self, name, shape):
        t = self.sb(name + "_sb", shape, F32)
        self.P.dma(self.q(), t[:], self.inp(name, shape))
        return t

    def setup_c(self):
        P = self.P
        cv = self.load_cols("cvec", [128, 8, 2])
        self.scT = self.sb("scT", [128, 8, 2], F32)
        P.act(self.scT[:], cv[:], AF.Silu)
        sw = self.slabw = getattr(self, "slabw", 256)
        self.wslab = [self.af[:, i * 8 * sw:(i + 1) * 8 * sw].rearrange("p (k n) -> p k n", k=8) for i in range(2)]

    def compute_mod(self, i):
        P = self.P
        wm = self.inp("wmod%d" % i, [1024, 6144]).rearrange("(k p) n -> p k n", p=128)
        bm = self.load_cols("bmodc%d" % i, [128, 48])
        mod = self.sb("mod%d" % i, [128, 48, 2], F32)
        psm = self.ps[7]
        sw = self.slabw
        for s in range(6144 // sw):
            slab = self.wslab[s % 2]
            P.dma(self.q(), slab[:], wm[:, :, s * sw:(s + 1) * sw])
            for jj in range(sw // 128):
                j = s * (sw // 128) + jj
                for k in range(8):
                    P.mm(psm[:, j * 2:(j + 1) * 2], slab[:, k, jj * 128:(jj + 1) * 128], self.scT[:, k, :],
                         start=(k == 0), stop=(k == 7))
        psv = psm[:, 0:96].rearrange("p (j s) -> p j s", s=2)
        for s in range(2):
            P.tt("dve", mod[:, :, s], psv[:, :, s], bm[:, :], ALU.add)
        return mod

    def make_gain(self, name, mod, m, ng):
        P = self.P
        G = self.sb(name, [128, 8, 2], F32)
        for s in range(2):
            P.ts("dve", G[:, :, s], mod[:, m * 8:(m + 1) * 8, s], 1.0, 32.0, ALU.add, ALU.mult)
            P.tt("dve", G[:, :, s], G[:, :, s], ng[:, :], ALU.mult)
        return G

    def norm_block(self, xT, c0, n, G, mod, msh, s, out_fn, psb=6):
        P = self.P
        P.act(self.sqb[:, :, :n], xT[:, :, c0:c0 + n], AF.Square)
        ps = self.ps[psb]
        for k in range(8):
            P.mm(ps[:, :n], self.ones_bf[:], self.sqb[:, k, :n], start=(k == 0), stop=(k == 7))
        self.rsqrt(self.rstd[:, :n], ps[:, :n], 1024.0 * EPS)
        for k in range(8):
            t = self.tmp()
            P.tt(self.ve(), t[:, :n], xT[:, k, c0:c0 + n], self.rstd[:, :n], ALU.mult)
            P.act(out_fn(k), t[:, :n], AF.Identity, scale=G[:, k, s:s + 1],
                  bias=mod[:, msh * 8 + k, s:s + 1])

    def wload(self, dst, src):
        self.P.dma("pool", dst, src)

    def setup_wslots(self, n=4):
        self.wslots = [self.sb("wslot%d" % i, [128, 4096], BF16) for i in range(n)]

    def ffn(self, xT, hT, blocks, i, mod, G2):
        P = self.P
        w1 = self.inp("wff1_%d" % i, [1024, 4096]).rearrange("(k p) n -> p k n", p=128)
        w2 = self.inp("wff2_%d" % i, [4096, 1024])
        for (xc0, hc0, n, s) in blocks:
            self.norm_block(xT, xc0, n, G2, mod, 3, s, lambda k: hT[:, k, hc0:hc0 + n])
        uT = [self.uT0, self.uT1]

        def issue(g):
            a = self.wslots[(g % 2) * 2][:, :].rearrange("p (k n) -> p k n", k=8)
            b = self.wslots[(g % 2) * 2 + 1][:, :].rearrange("p (k n) -> p k n", k=4)
            self.wload(a, w1[:, :, g * 512:(g + 1) * 512])
            self.wload(b, w2[g * 512:(g + 1) * 512, :].rearrange("(k p) n -> p k n", p=128))
            return a, b

        nxt = issue(0)
        cnt = 0
        for g in range(8):
            W1g, W2g = nxt
            if g + 1 < 8:
                nxt = issue(g + 1)
            for (xc0, hc0, n, s) in blocks:
                u = uT[cnt % 2]
                cnt += 1
                for fc in range(4):
                    ps = self.ps[fc]
                    for k in range(8):
                        P.mm(ps[:, :n], W1g[:, k, fc * 128:(fc + 1) * 128], hT[:, k, hc0:hc0 + n],
                             start=(k == 0), stop=(k == 7))
                    t = self.tmp()
                    P.act(t[:, :n], ps[:, :n], AF.Relu)
                    P.tt("pool", u[:, fc, :n], t[:, :n], t[:, :n], ALU.mult)
                for j in range(8):
                    ps = self.ps[4 + (j % 3)]
                    for fc in range(4):
                        P.mm(ps[:, :n], W2g[:, fc, j * 128:(j + 1) * 128], u[:, fc, :n],
                             start=(fc == 0), stop=(fc == 3))
                    P.stt("dve", xT[:, j, xc0:xc0 + n], ps[:, :n], mod[:, 40 + j, s:s + 1],
                          xT[:, j, xc0:xc0 + n], ALU.mult, ALU.add)

    def qk_norm_rope(self, ps_raw, rows, n, gcol, blk_ones, hd, perm, C, S, out_ap, ps_ss, ps_sw):
        P = self.P
        qg = self.tmp()
        P.act(qg[:rows, :n], ps_raw[:rows, :n], AF.Identity, scale=gcol[:rows, 0:1])
        sq = self.sqb[:, 0, :]
        P.act(sq[:rows, :n], ps_raw[:rows, :n], AF.Square)
        P.mm(ps_ss[:rows, :n], blk_ones[:rows, :rows], sq[:rows, :n])
        rs = self.tmp()
        self.rsqrt(rs[:rows, :n], ps_ss[:rows, :n], float(hd) * EPS)
        if C is None:
            P.stt("dve", out_ap, qg[:rows, :n], float(hd) ** 0.5, rs[:rows, :n], ALU.mult, ALU.mult)
            return
        qb = self.sqb[:, 1, :]
        P.copy("pool", qb[:rows, :n], qg[:rows, :n])
        P.mm(ps_sw[:rows, :n], perm[:rows, :rows], qb[:rows, :n])
        t1 = self.tmp()
        P.tt("dve", t1[:rows, :n], qg[:rows, :n], C, ALU.mult)
        t2 = self.tmp()
        P.tt("dve", t2[:rows, :n], ps_sw[:rows, :n], S, ALU.mult)
        P.tt("pool", t1[:rows, :n], t1[:rows, :n], t2[:rows, :n], ALU.add)
        P.stt("dve", out_ap, t1[:rows, :n], float(hd) ** 0.5, rs[:rows, :n], ALU.mult, ALU.mult)

    def finish(self):
        self.P.emit()
        return self.nc


NTA = 2336
UW = 2368


def build_A():
    L = LB()
    P = L.P
    nc = L.nc
    x_tok = L.inp("x_tok", [NTA, 1024])
    xT = L.sb("xT", [128, 8, NTA], F32)
    abf = L.sb("abf", [128, 39424], BF16)
    af = L.sb("af", [128, 4096], F32)
    L.af = af
    vblk = af[:, :].rearrange("p (k n) -> p k n", k=8)
    xst = [af[:, i * 1024:(i + 1) * 1024] for i in range(2)]
    for r in range(19):
        rows = 128 if r < 18 else 32
        st = xst[r % 2]
        P.dma(L.q(), st[:rows, :], x_tok[r * 128:r * 128 + rows, :])
        for half in range(2):
            ps = L.ps[(r * 2 + half) % 4]
            for kk in range(4):
                k = half * 4 + kk
                P.transpose(ps[:, kk * 128:kk * 128 + rows], st[:rows, k * 128:(k + 1) * 128], L.ident[:rows, :rows])
            src = ps[:, 0:512].rearrange("p (a b) -> p a b", a=4)[:, :, :rows]
            dst = xT[:, half * 4:(half + 1) * 4, r * 128:r * 128 + rows]
            P.copy("act" if half else "dve", dst, src)
    L.setup_c()
    mod0 = L.compute_mod(0)
    mod1 = L.compute_mod(1)
    n1g0 = L.load_cols("n1g0", [128, 8])
    n2g0 = L.load_cols("n2g0", [128, 8])
    n1g1 = L.load_cols("n1g1", [128, 8])
    G1_0 = L.make_gain("G1_0", mod0, 1, n1g0)
    G2_0 = L.make_gain("G2_0", mod0, 4, n2g0)
    G1_1 = L.make_gain("G1_1", mod1, 1, n1g1)
    pw1 = abf[:, 0:16384].rearrange("p (k n) -> p k n", k=8)
    hblk = [abf[:, 16384:20480].rearrange("p (k n) -> p k n", k=8) for i in range(2)]
    U = abf[:, 20480:20480 + 8 * UW].rearrange("p (k n) -> p k n", k=8)
    w_pw1 = L.inp("conv_pw1", [1024, 2048]).rearrange("(k p) n -> p k n", p=128)
    for s in range(4):
        L.wload(pw1[:, :, s * 512:(s + 1) * 512], w_pw1[:, :, s * 512:(s + 1) * 512])
    hmask = L.load_cols("hmask", [128, 32])
    wdw = L.load_cols("conv_dwc", [128, 8, 31])
    bdw = L.load_cols("conv_bdwc", [128, 8])
    lng = L.load_cols("conv_lngc", [128, 8])
    lnb = L.load_cols("conv_lnbc", [128, 8])
    lng32 = L.sb("lng32", [128, 8], F32)
    P.ts("dve", lng32[:], lng[:], 32.0, None, ALU.mult)
    ident_bf = L.sb("ident_bf", [128, 128], BF16)
    P.copy("dve", ident_bf[:], L.ident[:])
    P.memset("pool", U[:, :, 2080:2096], 0.0)
    P.memset("pool", U[:, :, 2352:2368], 0.0)
    blocksA = [(0, 512, 0, 0), (512, 512, 512, 0), (1024, 512, 1024, 0), (1536, 512, 1536, 0),
               (2048, 32, 2048, 0), (2080, 256, 2096, 1)]
    for bi, (c0, n, uc0, s) in enumerate(blocksA):
        hb = hblk[bi % 2]
        L.norm_block(xT, c0, n, G1_0, mod0, 0, s, lambda k: hb[:, k, :n])
        for j in range(8):
            psa = L.ps[(j % 2) * 2]
            psg = L.ps[(j % 2) * 2 + 1]
            for k in range(8):
                P.mm(psa[:, :n], pw1[:, k, j * 128:(j + 1) * 128], hb[:, k, :n], start=(k == 0), stop=(k == 7))
            for k in range(8):
                P.mm(psg[:, :n], pw1[:, k, 1024 + j * 128:1024 + (j + 1) * 128], hb[:, k, :n],
                     start=(k == 0), stop=(k == 7))
            sg = L.tmp()
            P.act(sg[:, :n], psg[:, :n], AF.Sigmoid)
            P.tt("dve", U[:, j, uc0:uc0 + n], psa[:, :n], sg[:, :n], ALU.mult)
    for j in range(8):
        P.tt("pool", U[:, j, 0:16], U[:, j, 0:16], hmask[:, 0:16], ALU.mult)
        P.tt("pool", U[:, j, 2064:2080], U[:, j, 2064:2080], hmask[:, 16:32], ALU.mult)
    pw2 = abf[:, 0:8192].rearrange("p (k n) -> p k n", k=8)
    w_pw2 = L.inp("conv_pw2", [1024, 1024]).rearrange("(k p) n -> p k n", p=128)
    for s in range(2):
        L.wload(pw2[:, :, s * 512:(s + 1) * 512], w_pw2[:, :, s * 512:(s + 1) * 512])
    zblk = abf[:, 8192:12288].rearrange("p (k n) -> p k n", k=8)
    diag = [abf[:, 12288 + i * 3968:12288 + (i + 1) * 3968].rearrange("p (t n) -> p t n", t=31) for i in range(2)]
    blocksC = [(16 + tb * 512, 512, 16 + tb * 512, 0) for tb in range(4)] + [(2080, 256, 2096, 1)]
    dcount = 0
    for (xc0, n, uc0, s) in blocksC:
        for j in range(8):
            dg = diag[dcount % 2]
            dcount += 1
            for tap in range(31):
                P.ts("dve", dg[:, tap, :], ident_bf[:], wdw[:, j, tap:tap + 1], None, ALU.mult)
            ps = L.ps[j % 4]
            for tap in range(31):
                P.mm(ps[:, :n], dg[:, tap, :], U[:, j, uc0 - 15 + tap:uc0 - 15 + tap + n],
                     start=(tap == 0), stop=(tap == 30))
            P.act(vblk[:, j, :n], ps[:, :n], AF.Identity, bias=bdw[:, j:j + 1])
        psm = L.ps[4]
        for j in range(8):
            P.mm(psm[:, :n], L.ones_f[:], vblk[:, j, :n], start=(j == 0), stop=(j == 7))
        mu = L.tmp()
        P.ts("dve", mu[:, :n], psm[:, :n], 1.0 / 1024.0, None, ALU.mult)
        for j in range(8):
            P.tt(L.ve(), vblk[:, j, :n], vblk[:, j, :n], mu[:, :n], ALU.subtract)
        P.act(L.sqb[:, :, :n], vblk[:, :, :n], AF.Square)
        psv = L.ps[5]
        for j in range(8):
            P.mm(psv[:, :n], L.ones_bf[:], L.sqb[:, j, :n], start=(j == 0), stop=(j == 7))
        L.rsqrt(L.rstd[:, :n], psv[:, :n], 1024.0 * EPS)
        for j in range(8):
            t = L.tmp()
            P.tt(L.ve(), t[:, :n], vblk[:, j, :n], L.rstd[:, :n], ALU.mult)
            P.act(zblk[:, j, :n], t[:, :n], AF.Silu, scale=lng32[:, j:j + 1], bias=lnb[:, j:j + 1])
        for j in range(8):
            ps = L.ps[6 + (j % 2)]
            for k in range(8):
                P.mm(ps[:, :n], pw2[:, k, j * 128:(j + 1) * 128], zblk[:, k, :n], start=(k == 0), stop=(k == 7))
            P.stt("dve", xT[:, j, xc0:xc0 + n], ps[:, :n], mod0[:, 16 + j, s:s + 1],
                  xT[:, j, xc0:xc0 + n], ALU.mult, ALU.add)
    hT = abf[:, 0:8 * NT].rearrange("p (k n) -> p k n", k=8)
    L.wslots = [abf[:, 18432 + i * 4096:18432 + (i + 1) * 4096] for i in range(4)]
    L.uT0 = abf[:, 34816:36864].rearrange("p (k n) -> p k n", k=4)
    L.uT1 = abf[:, 36864:38912].rearrange("p (k n) -> p k n", k=4)
    blocksF = [(16 + tb * 512, tb * 512, 512, 0) for tb in range(4)] + [(2080, 2048, 256, 1)]
    L.ffn(xT, hT, blocksF, 0, mod0, G2_0)
    for (xc0, hc0, n, s) in blocksF:
        L.norm_block(xT, xc0, n, G1_1, mod1, 0, s, lambda k: hT[:, k, hc0:hc0 + n])
    wkv = L.wslots[0][:, :].rearrange("p (k n) -> p k n", k=8)
    w_qkv = L.inp("gqa_wqkv", [1024, 1536]).rearrange("(k p) n -> p k n", p=128)
    L.wload(wkv[:, :, :], w_qkv[:, :, 1024:1536])
    gk = L.load_cols("gqa_kgc", [128, 1])
    ropeC_d = L.inp("ropeC64", [128, 2048])
    ropeS_d = L.inp("ropeS64", [128, 2048])
    ropeC = af[:, 3072:3584]
    ropeS = af[:, 3584:4096]
    permf = L.load_cols("perm64", [128, 128])
    perm = L.sb("perm_bf", [128, 128], BF16)
    P.copy("dve", perm[:], permf[:])
    blk1 = L.sb("blk64", [128, 128], BF16)
    P.memset("dve", blk1[:], 0.0)
    P.memset("dve", blk1[0:64, 0:64], 1.0)
    P.memset("dve", blk1[64:128, 64:128], 1.0)
    kT_out = L.outp("kT_out", [128, 2, NT])
    v_out = L.outp("v_out", [128, 18, 256])
    kst = [af[:, i * 512:(i + 1) * 512] for i in range(2)]
    vst = [af[:, 1024 + i * 1024:2048 + i * 1024].rearrange("p (k n) -> p k n", k=4) for i in range(2)]
    cnt = 0
    for (xc0, hc0, n, s) in blocksF:
        if s == 0:
            P.dma("sp", ropeC[:, :n], ropeC_d[:, hc0:hc0 + n])
            P.dma("act", ropeS[:, :n], ropeS_d[:, hc0:hc0 + n])
        for m in range(2):
            ps = L.ps[m]
            for k in range(8):
                P.mm(ps[:, :n], wkv[:, k, m * 128:(m + 1) * 128], hT[:, k, hc0:hc0 + n], start=(k == 0), stop=(k == 7))
            ko = kst[cnt % 2]
            cnt += 1
            if s == 0:
                L.qk_norm_rope(ps, 128, n, gk, blk1, 64, perm, ropeC[:, :n], ropeS[:, :n],
                               ko[:, :n], L.ps[2], L.ps[3])
            else:
                L.qk_norm_rope(ps, 128, n, gk, blk1, 64, None, None, None, ko[:, :n], L.ps[2], L.ps[3])
            P.dma(L.q(), kT_out[:, m, hc0:hc0 + n], ko[:, :n])
        vs = vst[(hc0 // 512) % 2]
        nt = n // 128
        for tt_ in range(nt):
            ps = L.ps[4 + (tt_ % 2)]
            for k in range(8):
                P.mm(ps[:, 0:256], hT[:, k, hc0 + tt_ * 128:hc0 + (tt_ + 1) * 128], wkv[:, k, 256:512],
                     start=(k == 0), stop=(k == 7))
            P.copy("act", vs[:, tt_, :], ps[:, 0:256])
        P.dma(L.q(), v_out[:, hc0 // 128:hc0 // 128 + nt, :], vs[:, :nt, :])
    mod_out = L.outp("mod_out", [128, 48, 2])
    P.dma("sp", mod_out, mod1[:])
    xT_out = L.outp("xT_out", [128, 8, NT])
    P.dma("sp", xT_out[:, :, 0:2048], xT[:, :, 16:2064])
    P.dma("act", xT_out[:, :, 2048:2304], xT[:, :, 2080:2336])
    return L


def cols(v, k):
    return np.ascontiguousarray(np.asarray(v, np.float32).reshape(k, 128).T)


def rope_tables(rot, pos0, n):
    q = rot // 4
    inv = (10000.0 ** (-np.arange(q, dtype=np.float32) / q)).astype(np.float32)
    t = np.arange(pos0, pos0 + n)
    row = (t // GRID_W).astype(np.float32)
    col = (t % GRID_W).astype(np.float32)
    ar = (inv[:, None] * row[None, :]).astype(np.float32)
    ac = (inv[:, None] * col[None, :]).astype(np.float32)
    C = np.concatenate([np.cos(ar), np.cos(ar), np.cos(ac), np.cos(ac)], 0).astype(np.float32)
    S = np.concatenate([-np.sin(ar), np.sin(ar), -np.sin(ac), np.sin(ac)], 0).astype(np.float32)
    return C, S


def perm_matrix(rot, reps, base=0, size=128):
    Pm = np.zeros((size, size), np.float32)
    q = rot // 4
    for r in range(reps):
        for i in range(rot):
            g = i // (2 * q)
            j = i % (2 * q)
            pj = j + q if j < q else j - q
            Pm[base + r * rot + g * 2 * q + pj, base + r * rot + i] = 1.0
    return Pm


_PROG_CACHE = {}


def get_prog(name, builder):
    if name not in _PROG_CACHE:
        L = builder()
        L.finish()
        _PROG_CACHE[name] = L
    return _PROG_CACHE[name]


def run_launch(name, builder, in_maps):
    L = get_prog(name, builder)
    maps = [{k: np.ascontiguousarray(m[k], dtype=np.float32) for k in L.in_names} for m in in_maps]
    res = run_bass_kernel_spmd(L.nc, maps, core_ids=list(range(CORES)))
    return res.results


def prep_A(I, c):
    b, half = c // 2, c % 2
    st = half * 2048
    x = I["x"][b]
    z16 = np.zeros((16, 1024), np.float32)
    left = x[st - 16:st] if half == 1 else z16
    right = x[st + 2048:st + 2064] if half == 0 else z16
    x_tok = np.concatenate([left, x[st:st + 2048], right, I["ctx"][b]], 0)
    cvec = np.stack([cols(I["c"][b], 8), cols(I["c_ctx"], 8)], -1)
    hmask = np.zeros((128, 32), np.float32)
    hmask[:, 0:16] = 1.0 if half == 1 else 0.0
    hmask[:, 16:32] = 1.0 if half == 0 else 0.0
    C, S = rope_tables(64, st, 2048)
    d = {
        "ident_in": np.eye(128, dtype=np.float32),
        "x_tok": x_tok, "cvec": cvec, "hmask": hmask,
        "wmod0": I["w_mod"][0], "wmod1": I["w_mod"][1],
        "bmodc0": cols(I["b_mod"][0], 48), "bmodc1": cols(I["b_mod"][1], 48),
        "n1g0": cols(I["norm1_g"][0], 8), "n2g0": cols(I["norm2_g"][0], 8), "n1g1": cols(I["norm1_g"][1], 8),
        "conv_pw1": I["conv_w_pw1"][0], "conv_pw2": I["conv_w_pw2"][0],
        "conv_dwc": np.ascontiguousarray(I["conv_w_dw"][0].T.reshape(8, 128, 31).transpose(1, 0, 2)),
        "conv_bdwc": cols(I["conv_b_dw"][0], 8), "conv_lngc": cols(I["conv_ln_g"][0], 8),
        "conv_lnbc": cols(I["conv_ln_b"][0], 8),
        "wff1_0": I["w_ff1"][0], "wff2_0": I["w_ff2"][0],
        "gqa_wqkv": I["gqa_w_qkv"][0],
        "gqa_kgc": np.tile(I["gqa_k_norm"][0], 2).reshape(128, 1),
        "ropeC64": np.tile(C, (2, 1)), "ropeS64": np.tile(S, (2, 1)),
        "perm64": perm_matrix(64, 2),
    }
    return d


def attend_head(L, k_lhsT_fn, q_rhs, v_lhsT_fn, lcs, n, scale, ps_o, side, out_ap, ptiles, ebias=None, pending=None):
    P = L.P
    groups = [lcs[i:i + 2] for i in range(0, len(lcs), 2)]
    ng = len(groups)
    S = [L.psbig[:, 0:1024], L.psbig[:, 1024:2048]]

    def qk(gi):
        s_ = S[gi % 2]
        for j, lc in enumerate(groups[gi]):
            P.mm(s_[:, j * 512:j * 512 + n], k_lhsT_fn(lc), q_rhs)

    qk(0)
    if ng > 1:
        qk(1)
    for gi in range(ng):
        if gi == 6 and pending is not None:
            pending()
            pending = None
        g = groups[gi]
        s_ = S[gi % 2]
        pt = ptiles[gi % 2]
        if n == 512:
            w = len(g) * 512
            P.act(pt[:, 0:w], s_[:, 0:w], AF.Exp, scale=scale, bias=ebias)
        else:
            P.act(pt[:, :].rearrange("p (j c) -> p j c", j=2)[:, 0:len(g), 0:n],
                  s_.rearrange("p (j c) -> p j c", j=2)[:, 0:len(g), 0:n], AF.Exp, scale=scale, bias=ebias)
        if gi + 2 < ng:
            qk(gi + 2)
        for j, lc in enumerate(g):
            P.mm(ps_o[:, :n], v_lhsT_fn(lc), pt[:, j * 512:j * 512 + n], start=(gi == 0 and j == 0),
                 stop=(gi == ng - 1 and j == len(g) - 1))
    if pending is not None:
        pending()
    sl0, sl1 = (0, 64) if side == 0 else (64, 128)
    drow = 64 if side == 0 else 0
    o_sb = L.tmp()
    P.copy("dve", o_sb[:, :n], ps_o[:, :n])
    rd = L.tmp()
    P.recip(rd[drow:drow + 1, :n], o_sb[drow:drow + 1, :n])

    def fin():
        P.mm(ps_o[:, :n], L.ones_f[drow:drow + 1, 0:128], rd[drow:drow + 1, :n])
        P.tt("dve", out_ap, o_sb[sl0:sl1, :n], ps_o[sl0:sl1, :n], ALU.mult)
    return fin


def attend_pair(L, k_lo_fn, k_hi_fn, q_lo, q_hi, v_lo_fn, v_hi_fn, lcs, n, scale, ps_lo, ps_hi, out_lo, out_hi,
                ptiles, pending=None):
    P = L.P
    ng = len(lcs)
    S = [L.psbig[:, 0:1024], L.psbig[:, 1024:2048]]

    def qk(gi):
        s_ = S[gi % 2]
        P.mm(s_[:, 0:n], k_lo_fn(lcs[gi]), q_lo)
        P.mm(s_[:, 512:512 + n], k_hi_fn(lcs[gi]), q_hi)

    qk(0)
    if ng > 1:
        qk(1)
    for gi in range(ng):
        if gi == 10 and pending is not None:
            pending()
            pending = None
        s_ = S[gi % 2]
        pt = ptiles[gi % 2]
        if n == 512:
            P.act(pt[:, 0:1024], s_[:, 0:1024], AF.Exp, scale=scale)
        else:
            P.act(pt[:, :].rearrange("p (j c) -> p j c", j=2)[:, :, 0:n],
                  s_.rearrange("p (j c) -> p j c", j=2)[:, :, 0:n], AF.Exp, scale=scale)
        if gi + 2 < ng:
            qk(gi + 2)
        P.mm(ps_lo[:, :n], v_lo_fn(lcs[gi]), pt[:, 0:n], start=(gi == 0), stop=(gi == ng - 1))
        P.mm(ps_hi[:, :n], v_hi_fn(lcs[gi]), pt[:, 512:512 + n], start=(gi == 0), stop=(gi == ng - 1))
    if pending is not None:
        pending()
    st = []
    for (side, ps_o, out_ap) in ((0, ps_lo, out_lo), (1, ps_hi, out_hi)):
        drow = 64 if side == 0 else 0
        o_sb = L.tmp()
        P.copy("dve", o_sb[:, :n], ps_o[:, :n])
        st.append((side, ps_o, out_ap, o_sb))
    for (side, ps_o, out_ap, o_sb) in list(st):
        drow = 64 if side == 0 else 0
        rd = L.tmp()
        P.recip(rd[drow:drow + 1, :n], o_sb[drow:drow + 1, :n])
        st.append(rd)

    def fin():
        for idx in range(2):
            side, ps_o, out_ap, o_sb = st[idx]
            rd = st[2 + idx]
            sl0, sl1 = (0, 64) if side == 0 else (64, 128)
            drow = 64 if side == 0 else 0
            P.mm(ps_o[:, :n], L.ones_f[drow:drow + 1, 0:128], rd[drow:drow + 1, :n])
            P.tt("dve", out_ap, o_sb[sl0:sl1, :n], ps_o[sl0:sl1, :n], ALU.mult)
    return fin


def build_B():
    L = LB()
    P = L.P
    xT = L.sb("xT", [128, 8, NT], F32)
    xT_in = L.inp("xT_in", [128, 8, NT])
    for k in range(8):
        P.dma(L.q(), xT[:, k, :], xT_in[:, k, :])
    abf = L.sb("abf", [128, 50432], BF16)
    af = L.sb("af", [128, 2048], F32)
    L.af = af
    L.slabw = 128
    mod1 = L.load_cols("mod1_in", [128, 48, 2])
    n1g = L.load_cols("n1g1", [128, 8])
    n2g = L.load_cols("n2g1", [128, 8])
    G1 = L.make_gain("G1", mod1, 1, n1g)
    G2 = L.make_gain("G2", mod1, 4, n2g)
    K_all = abf[:, 0:8704].rearrange("p (m n) -> p m n", m=2)
    Vext = abf[:, 8704:21760].rearrange("p (c n) -> p c n", c=34)
    Wq = abf[:, 21760:29952].rearrange("p (k n) -> p k n", k=8)
    Wo = abf[:, 29952:38144].rearrange("p (k n) -> p k n", k=8)
    hblk = abf[:, 38144:42240].rearrange("p (k n) -> p k n", k=8)
    QTb = abf[:, 42240:46336].rearrange("p (k n) -> p k n", k=8)
    OTb = abf[:, 46336:50432].rearrange("p (k n) -> p k n", k=8)
    ptiles = [L.sb("ptile%d" % i, [128, 1024], BF16) for i in range(2)]
    kT_own = L.inp("kT_own", [128, 2, NT])
    kT_oth = L.inp("kT_oth", [128, 2, 2048])
    v_own = L.inp("v_own", [128, 18, 256])
    v_oth = L.inp("v_oth", [128, 16, 256])
    for m in range(2):
        L.wload(K_all[:, m, 0:NT], kT_own[:, m, :])
        L.wload(K_all[:, m, NT:4352], kT_oth[:, m, :])
    P.memset("dve", Vext[:, :, 64:128], 0.0)
    P.memset("dve", Vext[:, :, 256:320], 0.0)
    P.memset("dve", Vext[:, :, 64:65], 1.0)
    P.memset("dve", Vext[:, :, 256:257], 1.0)
    for (c0, c1, src) in ((0, 18, v_own), (18, 34, v_oth)):
        L.wload(Vext[:, c0:c1, 0:64], src[:, :, 0:64])
        L.wload(Vext[:, c0:c1, 128:256], src[:, :, 64:192])
        L.wload(Vext[:, c0:c1, 320:384], src[:, :, 192:256])
    vcol = [0, 64, 192, 256]
    w_qkv = L.inp("gqa_wqkv", [1024, 1536]).rearrange("(k p) n -> p k n", p=128)
    w_o = L.inp("gqa_wo", [1024, 1024])
    for m in range(2):
        for i in range(4):
            pb = m * 4 + i
            for side in range(2):
                h = 8 * m + 4 * side + i
                L.wload(Wq[:, :, pb * 128 + side * 64:pb * 128 + (side + 1) * 64], w_qkv[:, :, h * 64:(h + 1) * 64])
                L.wload(Wo[side * 64:(side + 1) * 64, pb, :], w_o[h * 64:(h + 1) * 64, :])
    gq = L.load_cols("gqa_qgc", [128, 1])
    permf = L.load_cols("perm64", [128, 128])
    perm = L.sb("perm_bf", [128, 128], BF16)
    P.copy("dve", perm[:], permf[:])
    blk1 = L.sb("blk64", [128, 128], BF16)
    P.memset("dve", blk1[:], 0.0)
    P.memset("dve", blk1[0:64, 0:64], 1.0)
    P.memset("dve", blk1[64:128, 64:128], 1.0)
    ropeC_d = L.inp("ropeC64", [128, 2048])
    ropeS_d = L.inp("ropeS64", [128, 2048])
    ropeC = af[:, 0:512]
    ropeS = af[:, 512:1024]
    blocks = [(tb * 512, 512, 0) for tb in range(4)] + [(2048, 256, 1)]
    hcount = 0
    for (c0, n, s) in blocks:
        L.norm_block(xT, c0, n, G1, mod1, 0, s, lambda k: hblk[:, k, :n], psb=7)
        if s == 0:
            P.dma("sp", ropeC[:, :n], ropeC_d[:, c0:c0 + n])
            P.dma("act", ropeS[:, :n], ropeS_d[:, c0:c0 + n])
        for pb in range(8):
            ps = L.ps[6]
            for k in range(8):
                P.mm(ps[:, :n], Wq[:, k, pb * 128:(pb + 1) * 128], hblk[:, k, :n], start=(k == 0), stop=(k == 7))
            if s == 0:
                L.qk_norm_rope(ps, 128, n, gq, blk1, 64, perm, ropeC[:, :n], ropeS[:, :n], QTb[:, pb, :n],
                               L.ps[7], L.ps[3])
            else:
                L.qk_norm_rope(ps, 128, n, gq, blk1, 64, None, None, None, QTb[:, pb, :n], L.ps[7], L.ps[3])
        lcs = list(range(34)) if s == 0 else [16, 17]
        pend = None
        for pb in range(8):
            m = pb // 4
            ob = 4 + 2 * (hcount % 2)
            hcount += 1
            pend = attend_pair(L,
                               lambda lc, m=m: K_all[0:64, m, lc * 128:(lc + 1) * 128],
                               lambda lc, m=m: K_all[64:128, m, lc * 128:(lc + 1) * 128],
                               QTb[0:64, pb, :n], QTb[64:128, pb, :n],
                               lambda lc, m=m: Vext[:, lc, vcol[2 * m]:vcol[2 * m] + 128],
                               lambda lc, m=m: Vext[:, lc, vcol[2 * m + 1]:vcol[2 * m + 1] + 128],
                               lcs, n, 0.125, L.ps[ob], L.ps[ob + 1], OTb[0:64, pb, :n], OTb[64:128, pb, :n],
                               ptiles, pending=pend)
        pend()
        for j in range(8):
            ps = L.ps[6 + (j % 2)]
            for pb in range(8):
                P.mm(ps[:, :n], Wo[:, pb, j * 128:(j + 1) * 128], OTb[:, pb, :n], start=(pb == 0), stop=(pb == 7))
            P.stt("dve", xT[:, j, c0:c0 + n], ps[:, :n], mod1[:, 16 + j, s:s + 1], xT[:, j, c0:c0 + n],
                  ALU.mult, ALU.add)
    hT = abf[:, 0:8 * NT].rearrange("p (k n) -> p k n", k=8)
    L.wslots = [abf[:, 18432 + i * 4096:18432 + (i + 1) * 4096] for i in range(4)]
    L.uT0 = abf[:, 34816:36864].rearrange("p (k n) -> p k n", k=4)
    L.uT1 = abf[:, 36864:38912].rearrange("p (k n) -> p k n", k=4)
    blocksF = [(c0, c0, n, s) for (c0, n, s) in blocks]
    L.ffn(xT, hT, blocksF, 1, mod1, G2)
    xT_out = L.outp("xT_out", [128, 8, NT])
    for k in range(8):
        P.dma(L.q(), xT_out[:, k, :], xT[:, k, :])
    return L


def prep_B(I, c, resA):
    b, half = c // 2, c % 2
    o = c ^ 1
    C, S = rope_tables(64, half * 2048, 2048)
    return {
        "ident_in": np.eye(128, dtype=np.float32),
        "xT_in": resA[c]["xT_out"],
        "mod1_in": resA[c]["mod_out"],
        "n1g1": cols(I["norm1_g"][1], 8), "n2g1": cols(I["norm2_g"][1], 8),
        "kT_own": resA[c]["kT_out"], "kT_oth": resA[o]["kT_out"][:, :, 0:2048],
        "v_own": resA[c]["v_out"], "v_oth": resA[o]["v_out"][:, 0:16, :],
        "gqa_wqkv": I["gqa_w_qkv"][0], "gqa_wo": I["gqa_w_o"][0],
        "gqa_qgc": np.tile(I["gqa_q_norm"][0], 2).reshape(128, 1),
        "perm64": perm_matrix(64, 2),
        "ropeC64": np.tile(C, (2, 1)), "ropeS64": np.tile(S, (2, 1)),
        "wff1_1": I["w_ff1"][1], "wff2_1": I["w_ff2"][1],
    }


NTC = 2320


def build_C():
    L = LB()
    P = L.P
    xT = L.sb("xT", [128, 8, NTC], F32)
    xT_in = L.inp("xT_in", [128, 8, NTC])
    for k in range(8):
        P.dma(L.q(), xT[:, k, :], xT_in[:, k, :])
    abf = L.sb("abf", [128, 38912], BF16)
    af = L.sb("af", [128, 7680], F32)
    L.af = af
    L.setup_c()
    mod2 = L.compute_mod(2)
    mod3 = L.compute_mod(3)
    n1g2 = L.load_cols("n1g2", [128, 8])
    n2g2 = L.load_cols("n2g2", [128, 8])
    n1g3 = L.load_cols("n1g3", [128, 8])
    G1 = L.make_gain("G1_2", mod2, 1, n1g2)
    G2 = L.make_gain("G2_2", mod2, 4, n2g2)
    G1_3 = L.make_gain("G1_3", mod3, 1, n1g3)
    pscale = L.load_cols("pool_scalec", [128, 8])
    psg = L.sb("psg", [128, 8, 2], F32)
    for s in range(2):
        P.tt("dve", psg[:, :, s], mod2[:, 16:24, s], pscale[:, :], ALU.mult)
    hmask = L.load_cols("hmask", [128, 16])
    invc = L.load_cols("invc", [128, 2, 4, 16])
    Wp = abf[:, 0:2048].rearrange("p (g c n) -> p g c n", g=4, c=2)
    pblk = abf[:, 2048:6144].rearrange("p (k n) -> p k n", k=8)
    w_pool = L.inp("pool_w", [4, 256, 256])
    for g in range(4):
        L.wload(Wp[:, g, :, :], w_pool[g].rearrange("(c p) n -> p c n", p=128))
    hE = af[:, 0:4224].rearrange("p (k n) -> p k n", k=8)
    wa = af[:, 4224:4752]
    wb = af[:, 4752:5280]
    rstd_all = af[:, 5280:7600]
    for (c0, n) in [(0, 512), (512, 512), (1024, 512), (1536, 512), (2048, 272)]:
        P.act(L.sqb[:, :, :n], xT[:, :, c0:c0 + n], AF.Square)
        ps = L.ps[6]
        for k in range(8):
            P.mm(ps[:, :n], L.ones_bf[:], L.sqb[:, k, :n], start=(k == 0), stop=(k == 7))
        L.rsqrt(rstd_all[:, c0:c0 + n], ps[:, :n], 1024.0 * EPS)
    blocksP = [(8 + tb * 512, 512, 0, tb) for tb in range(4)] + [(2064, 256, 1, 4)]
    for (c0, n, s, bi) in blocksP:
        ne = n + 16
        if s == 0:
            for k in range(8):
                t = L.tmp()
                for (a0, a1) in ((0, 512), (512, ne)):
                    P.tt(L.ve(), t[:, 0:a1 - a0], xT[:, k, c0 - 8 + a0:c0 - 8 + a1],
                         rstd_all[:, c0 - 8 + a0:c0 - 8 + a1], ALU.mult)
                    P.act(hE[:, k, a0:a1], t[:, 0:a1 - a0], AF.Identity, scale=G1[:, k, s:s + 1],
                          bias=mod2[:, k, s:s + 1])
                if bi == 0:
                    P.tt("pool", hE[:, k, 0:8], hE[:, k, 0:8], hmask[:, 0:8], ALU.mult)
                if bi == 3:
                    P.tt("pool", hE[:, k, ne - 8:ne], hE[:, k, ne - 8:ne], hmask[:, 8:16], ALU.mult)
        else:
            P.memset("pool", hE[:, :, 0:8], 0.0)
            P.memset("pool", hE[:, :, 8 + n:16 + n], 0.0)
            for k in range(8):
                t = L.tmp()
                P.tt(L.ve(), t[:, :n], xT[:, k, c0:c0 + n], rstd_all[:, c0:c0 + n], ALU.mult)
                P.act(hE[:, k, 8:8 + n], t[:, :n], AF.Identity, scale=G1[:, k, s:s + 1], bias=mod2[:, k, s:s + 1])
        for k in range(8):
            wi = k // 2
            w = 2 << wi
            eng = "dve" if k % 2 == 0 else "pool"
            cur = hE[:, k, :]
            ln_ = ne
            step = 1
            bufs = [wa, wb]
            bi_ = 0
            while step < w:
                nxt = bufs[bi_ % 2]
                bi_ += 1
                ln2 = ln_ - step
                P.tt(eng, nxt[:, 0:ln2], cur[:, 0:ln2], cur[:, step:step + ln2], ALU.add)
                cur = nxt
                ln_ = ln2
                step *= 2
            off = 8 - w // 2
            P.stt("dve", pblk[:, k, :n], cur[:, off:off + n], 1.0 / w, hE[:, k, 8:8 + n], ALU.mult, ALU.subtract)
            edges = []
            if s == 1 or bi == 0:
                edges.append((0, 0))
            if s == 1 or bi == 3:
                edges.append((n - 8, 8))
            for (t0, e0) in edges:
                t = L.tmp()
                P.tt("dve", t[:, 0:8], cur[:, off + t0:off + t0 + 8], invc[:, s, wi, e0:e0 + 8], ALU.mult)
                P.tt("dve", pblk[:, k, t0:t0 + 8], t[:, 0:8], hE[:, k, 8 + t0:16 + t0], ALU.subtract)
        for g in range(4):
            for oc in range(2):
                j = 2 * g + oc
                ps = L.ps[j % 4]
                for kc in range(2):
                    P.mm(ps[:, :n], Wp[:, g, kc, oc * 128:(oc + 1) * 128], pblk[:, 2 * g + kc, :n],
                         start=(kc == 0), stop=(kc == 1))
                P.stt("dve", xT[:, j, c0:c0 + n], ps[:, :n], psg[:, j, s:s + 1], xT[:, j, c0:c0 + n],
                      ALU.mult, ALU.add)
    hT = abf[:, 0:8 * NT].rearrange("p (k n) -> p k n", k=8)
    L.wslots = [abf[:, 18432 + i * 4096:18432 + (i + 1) * 4096] for i in range(4)]
    L.uT0 = abf[:, 34816:36864].rearrange("p (k n) -> p k n", k=4)
    L.uT1 = abf[:, 36864:38912].rearrange("p (k n) -> p k n", k=4)
    blocksF = [(8 + tb * 512, tb * 512, 512, 0) for tb in range(4)] + [(2064, 2048, 256, 1)]
    L.ffn(xT, hT, blocksF, 2, mod2, G2)
    mod_out = L.outp("mod_out", [128, 48, 2])
    P.dma("sp", mod_out, mod3[:])
    xT_out = L.outp("xT_out", [128, 8, NT])
    P.dma("sp", xT_out[:, :, 0:2048], xT[:, :, 8:2056])
    P.dma("act", xT_out[:, :, 2048:2304], xT[:, :, 2064:2320])
    for (xc0, hc0, n, s) in blocksF:
        L.norm_block(xT, xc0, n, G1_3, mod3, 0, s, lambda k: hT[:, k, hc0:hc0 + n])
    Wd = abf[:, 18432:21504].rearrange("p (k n) -> p k n", k=8)
    Wukv = abf[:, 21504:25600].rearrange("p (c n) -> p c n", c=2)
    ckvn = abf[:, 25600:26624].rearrange("p (c n) -> p c n", c=2)
    krg_bf = abf[:, 26624:27136]
    sq_r = abf[:, 27136:27648]
    sq_n = abf[:, 27648:28160]
    w_dkv = L.inp("mla_wdkv", [1024, 288]).rearrange("(k p) n -> p k n", p=128)
    w_ukv = L.inp("mla_wukv", [256, 2048]).rearrange("(c p) n -> p c n", p=128)
    P.memset("dve", Wd[:, :, 256:384], 0.0)
    L.wload(Wd[:, :, 0:256], w_dkv[:, :, 0:256])
    L.wload(Wd[:, :, 320:352], w_dkv[:, :, 256:288])
    L.wload(Wukv[:, :, :], w_ukv)
    gkvl = L.load_cols("mla_kvlgc", [128, 2])
    gk96 = L.load_cols("mla_kgc", [128, 1])
    permf = L.load_cols("perm96", [128, 128])
    perm = L.sb("perm_bf", [128, 128], BF16)
    P.copy("dve", perm[:], permf[:])
    ropeC_d = L.inp("ropeC96", [128, 2048])
    ropeS_d = L.inp("ropeS96", [128, 2048])
    ropeC = af[:, 0:512]
    ropeS = af[:, 512:1024]
    krr = af[:, 1024:1536]
    khs = [af[:, 1536:2048], af[:, 2048:2560]]
    vst = af[:, 2560:6656].rearrange("p (t n) -> p t n", t=4)
    kT3_out = L.outp("kT3_out", [96, 16, NT])
    v3_out = L.outp("v3_out", [128, 18, 1024])
    Wukv4 = Wukv.rearrange("p c (h n) -> p c h n", h=16)
    kcnt = 0
    for (xc0, hc0, n, s) in blocksF:
        if s == 0:
            P.dma("sp", ropeC[:, :n], ropeC_d[:, hc0:hc0 + n])
            P.dma("act", ropeS[:, :n], ropeS_d[:, hc0:hc0 + n])
        for cc in range(2):
            ps = L.ps[cc]
            for k in range(8):
                P.mm(ps[:, :n], Wd[:, k, cc * 128:(cc + 1) * 128], hT[:, k, hc0:hc0 + n], start=(k == 0), stop=(k == 7))
            P.act(L.sqb[:, cc, :n], ps[:, :n], AF.Square)
        pss = L.ps[2]
        for cc in range(2):
            P.mm(pss[:, :n], L.ones_bf[:], L.sqb[:, cc, :n], start=(cc == 0), stop=(cc == 1))
        rs = L.tmp()
        L.rsqrt(rs[:, :n], pss[:, :n], 256.0 * EPS)
        for cc in range(2):
            t = L.tmp()
            P.act(t[:, :n], L.ps[cc][:, :n], AF.Identity, scale=gkvl[:, cc:cc + 1])
            P.stt("dve", ckvn[:, cc, :n], t[:, :n], 16.0, rs[:, :n], ALU.mult, ALU.mult)
        psr = L.ps[3]
        for k in range(8):
            P.mm(psr[:, :n], Wd[:, k, 256:384], hT[:, k, hc0:hc0 + n], start=(k == 0), stop=(k == 7))
        P.act(sq_r[64:96, :n], psr[64:96, :n], AF.Square)
        krg = L.tmp()
        P.act(krg[64:96, :n], psr[64:96, :n], AF.Identity, scale=gk96[64:96, 0:1])
        if s == 0:
            P.copy("pool", krg_bf[64:96, :n], krg[64:96, :n])
            psw = L.ps[4]
            P.mm(psw[0:96, :n], perm[64:96, 0:96], krg_bf[64:96, :n])
            t2 = L.tmp()
            P.tt("dve", t2[64:96, :n], psw[64:96, :n], ropeS[64:96, :n], ALU.mult)
            P.tt("dve", krr[64:96, :n], krg[64:96, :n], ropeC[64:96, :n], ALU.mult)
            P.tt("pool", krr[64:96, :n], krr[64:96, :n], t2[64:96, :n], ALU.add)
        else:
            P.copy("dve", krr[64:96, :n], krg[64:96, :n])
        for h in range(16):
            psk = L.ps[5 + (h % 2)]
            for cc in range(2):
                P.mm(psk[0:64, :n], Wukv4[:, cc, h, 0:64], ckvn[:, cc, :n], start=(cc == 0), stop=(cc == 1))
            P.act(sq_n[0:64, :n], psk[0:64, :n], AF.Square)
            ps2 = L.ps[7]
            P.mm(ps2[0:96, :n], L.ones_bf[0:64, 0:96], sq_n[0:64, :n], start=True, stop=False)
            P.mm(ps2[0:96, :n], L.ones_bf[64:96, 0:96], sq_r[64:96, :n], start=False, stop=True)
            rs96 = L.tmp()
            L.rsqrt(rs96[0:96, :n], ps2[0:96, :n], 96.0 * EPS)
            t = L.tmp()
            P.act(t[0:64, :n], psk[0:64, :n], AF.Identity, scale=gk96[0:64, 0:1])
            kh = khs[kcnt % 2]
            kcnt += 1
            P.stt("dve", kh[0:64, :n], t[0:64, :n], 96.0 ** 0.5, rs96[0:64, :n], ALU.mult, ALU.mult)
            P.stt("dve", kh[64:96, :n], krr[64:96, :n], 96.0 ** 0.5, rs96[64:96, :n], ALU.mult, ALU.mult)
            P.dma(L.q(), kT3_out[:, h, hc0:hc0 + n], kh[0:96, :n])
        nt = n // 128
        for tt_ in range(nt):
            for hh in range(2):
                ps = L.ps[hh]
                for cc in range(2):
                    P.mm(ps[:, 0:512].rearrange("p (h n) -> p h n", h=8),
                         ckvn[:, cc, tt_ * 128:(tt_ + 1) * 128], Wukv4[:, cc, hh * 8:(hh + 1) * 8, 64:128],
                         start=(cc == 0), stop=(cc == 1))
                P.copy("act" if hh else "dve", vst[:, tt_, hh * 512:(hh + 1) * 512], ps[:, 0:512])
        P.dma(L.q(), v3_out[:, hc0 // 128:hc0 // 128 + nt, :], vst[:, :nt, :])
    return L


def pool_invc(half):
    t = np.zeros((128, 2, 4, 16), np.float32)
    for wi, w in enumerate((2, 4, 8, 16)):
        for seg in range(2):
            for e in range(8):
                cnt_s = (e + w // 2) - max(e - w // 2, 0)
                cnt_e = w // 2 + min(w // 2, 8 - e)
                real_s = (seg == 1) or (half == 0)
                real_e = (seg == 1) or (half == 1)
                t[:, seg, wi, e] = 1.0 / (cnt_s if real_s else w)
                t[:, seg, wi, 8 + e] = 1.0 / (cnt_e if real_e else w)
    return t


def prep_C(I, c, resB):
    b, half = c // 2, c % 2
    o = c ^ 1
    own = resB[c]["xT_out"]
    oth = resB[o]["xT_out"]
    z8 = np.zeros((128, 8, 8), np.float32)
    left = oth[:, :, 2040:2048] if half == 1 else z8
    right = oth[:, :, 0:8] if half == 0 else z8
    xT_in = np.concatenate([left, own[:, :, 0:2048], right, own[:, :, 2048:2304]], 2)
    hmask = np.zeros((128, 16), np.float32)
    hmask[:, 0:8] = 1.0 if half == 1 else 0.0
    hmask[:, 8:16] = 1.0 if half == 0 else 0.0
    C32, S32 = rope_tables(32, half * 2048, 2048)
    C = np.ones((128, 2048), np.float32)
    S = np.zeros((128, 2048), np.float32)
    C[64:96] = C32
    S[64:96] = S32
    gk = np.zeros((128, 1), np.float32)
    gk[0:96, 0] = I["mla_k_norm"][0]
    return {
        "ident_in": np.eye(128, dtype=np.float32),
        "xT_in": xT_in, "hmask": hmask, "invc": pool_invc(half),
        "cvec": np.stack([cols(I["c"][b], 8), cols(I["c_ctx"], 8)], -1),
        "wmod2": I["w_mod"][2], "bmodc2": cols(I["b_mod"][2], 48),
        "wmod3": I["w_mod"][3], "bmodc3": cols(I["b_mod"][3], 48),
        "n1g2": cols(I["norm1_g"][2], 8), "n2g2": cols(I["norm2_g"][2], 8), "n1g3": cols(I["norm1_g"][3], 8),
        "pool_scalec": cols(I["pool_scale"][0], 8), "pool_w": I["pool_w"][0],
        "wff1_2": I["w_ff1"][2], "wff2_2": I["w_ff2"][2],
        "mla_wdkv": I["mla_w_dkv"][0], "mla_wukv": I["mla_w_ukv"][0],
        "mla_kvlgc": cols(I["mla_kv_lora_norm"][0], 2), "mla_kgc": gk,
        "perm96": perm_matrix(32, 1, base=64),
        "ropeC96": C, "ropeS96": S,
    }


NL = 2048


def build_D():
    L = LB()
    P = L.P
    nc = L.nc
    xT = L.sb("xT", [128, 8, NL], F32)
    xT_in = L.inp("xT_in", [128, 8, NL])
    for k in range(8):
        P.dma(L.q(), xT[:, k, :], xT_in[:, k, :])
    abf = L.sb("abf", [128, 43008], BF16)
    af = L.sb("af", [128, 4096], F32)
    L.af = af
    mod3 = L.load_cols("mod3_in", [128, 48, 2])
    n1g = L.load_cols("n1g3", [128, 8])
    n2g = L.load_cols("n2g3", [128, 8])
    G1 = L.make_gain("G1", mod3, 1, n1g)
    G2 = L.make_gain("G2", mod3, 4, n2g)
    ebias = L.sb("ebias", [128, 1], F32)
    P.memset("dve", ebias[:], -4.0)
    hblk = abf[:, 0:4096].rearrange("p (k n) -> p k n", k=8)
    Wdq = abf[:, 4096:10240].rearrange("p (k n) -> p k n", k=8)
    Wuq = abf[:, 10240:19456].rearrange("p (c n) -> p c n", c=6)
    cqn = abf[:, 19456:22528].rearrange("p (c n) -> p c n", c=6)
    qst = [abf[:, 22528 + i * 512:23040 + i * 512] for i in range(2)]
    w_dq = L.inp("mla_wdq", [1024, 768]).rearrange("(k p) n -> p k n", p=128)
    w_uq = L.inp("mla_wuq", [768, 1536]).rearrange("(c p) n -> p c n", p=128)
    L.wload(Wdq[:, :, 0:384], w_dq[:, :, 0:384])
    L.wload(Wdq[:, :, 384:768], w_dq[:, :, 384:768])
    for s3 in range(3):
        L.wload(Wuq[:, :, s3 * 512:(s3 + 1) * 512], w_uq[:, :, s3 * 512:(s3 + 1) * 512])
    gql = L.load_cols("mla_qlgc", [128, 6])
    gq96 = L.load_cols("mla_qgc", [128, 1])
    permf = L.load_cols("perm96", [128, 128])
    perm = L.sb("perm_bf", [128, 128], BF16)
    P.copy("dve", perm[:], permf[:])
    ropeC_d = L.inp("ropeC96", [128, 2048])
    ropeS_d = L.inp("ropeS96", [128, 2048])
    ropeC = af[:, 0:512]
    ropeS = af[:, 512:1024]
    qscr = nc.dram_tensor("qscratch", [16, 96, NL], BF16, kind="Internal").ap()
    qc = 0
    for tb in range(4):
        c0, n = tb * 512, 512
        L.norm_block(xT, c0, n, G1, mod3, 0, 0, lambda k: hblk[:, k, :n], psb=7)
        P.dma("sp", ropeC[:, :n], ropeC_d[:, c0:c0 + n])
        P.dma("act", ropeS[:, :n], ropeS_d[:, c0:c0 + n])
        for cc in range(6):
            ps = L.ps[cc]
            for k in range(8):
                P.mm(ps[:, :n], Wdq[:, k, cc * 128:(cc + 1) * 128], hblk[:, k, :n], start=(k == 0), stop=(k == 7))
            P.act(L.sqb[:, cc, :n], ps[:, :n], AF.Square)
        pss = L.ps[6]
        for cc in range(6):
            P.mm(pss[:, :n], L.ones_bf[:], L.sqb[:, cc, :n], start=(cc == 0), stop=(cc == 5))
        rs = L.rstd
        L.rsqrt(rs[:, :n], pss[:, :n], 768.0 * EPS)
        for cc in range(6):
            t = L.tmp()
            P.act(t[:, :n], L.ps[cc][:, :n], AF.Identity, scale=gql[:, cc:cc + 1])
            P.stt("dve", cqn[:, cc, :n], t[:, :n], 768.0 ** 0.5, rs[:, :n], ALU.mult, ALU.mult)
        for h in range(16):
            ps = L.ps[h % 2]
            for cc in range(6):
                P.mm(ps[0:96, :n], Wuq[:, cc, h * 96:(h + 1) * 96], cqn[:, cc, :n], start=(cc == 0), stop=(cc == 5))
            qo = qst[qc % 2]
            qc += 1
            L.qk_norm_rope(ps, 96, n, gq96, L.ones_bf, 96, perm, ropeC[0:96, :n], ropeS[0:96, :n], qo[0:96, :n],
                           L.ps[2 + (h % 2)], L.ps[4 + (h % 2)])
            P.dma(L.q(), qscr[h, :, c0:c0 + n], qo[0:96, :n])
    OT = abf[:, 0:16384].rearrange("p (k n) -> p k n", k=8)
    Kh = [abf[:, 16384 + i * 4352:16384 + (i + 1) * 4352] for i in range(2)]
    Vh = [abf[:, 25088 + i * 6528:25088 + (i + 1) * 6528].rearrange("p (c n) -> p c n", c=34) for i in range(2)]
    Qh = [abf[:, 38144 + i * 2048:38144 + (i + 1) * 2048] for i in range(2)]
    ptiles = [L.sb("ptile%d" % i, [128, 1024], BF16) for i in range(2)]
    kT_own = L.inp("kT3_own", [96, 16, NT])
    kT_oth = L.inp("kT3_oth", [96, 16, NL])
    v_own = L.inp("v3_own", [128, 18, 1024])
    v_oth = L.inp("v3_oth", [128, 16, 1024])
    for i in range(2):
        P.memset("dve", Vh[i][:, :, 0:64], 0.0)
        P.memset("dve", Vh[i][:, :, 128:192], 0.0)
        P.memset("dve", Vh[i][:, :, 0:1], 1.0)
        P.memset("dve", Vh[i][:, :, 128:129], 1.0)

    def load_head(h):
        i = h % 2
        L.wload(Kh[i][0:96, 0:NT], kT_own[:, h, :])
        L.wload(Kh[i][0:96, NT:4352], kT_oth[:, h, :])
        L.wload(Vh[i][:, 0:18, 64:128], v_own[:, :, h * 64:(h + 1) * 64])
        L.wload(Vh[i][:, 18:34, 64:128], v_oth[:, :, h * 64:(h + 1) * 64])
        P.dma("sp", Qh[i][0:96, :], qscr[h, :, :])

    load_head(0)
    hcount = 0
    pend = None
    for h in range(16):
        if h + 1 < 16:
            load_head(h + 1)
        i = h % 2
        side = h % 2
        sl0, sl1 = (0, 64) if side == 0 else (64, 128)
        vc = 64 if side == 0 else 0
        for qb in range(4):
            ps_o = L.ps[4 + (hcount % 2)]
            hcount += 1
            pend = attend_head(L,
                               lambda lc, i=i: Kh[i][0:96, lc * 128:(lc + 1) * 128],
                               Qh[i][0:96, qb * 512:(qb + 1) * 512],
                               lambda lc, i=i, vc=vc: Vh[i][:, lc, vc:vc + 128],
                               list(range(34)), 512, 96.0 ** -0.5, ps_o, side,
                               OT[sl0:sl1, h // 2, qb * 512:(qb + 1) * 512], ptiles, ebias=ebias[:, 0:1],
                               pending=pend)
    pend()
    Wo = abf[:, 16384:24576].rearrange("p (k n) -> p k n", k=8)
    w_o = L.inp("mla_wo", [1024, 1024]).rearrange("(k p) n -> p k n", p=128)
    for s2 in range(2):
        L.wload(Wo[:, :, s2 * 512:(s2 + 1) * 512], w_o[:, :, s2 * 512:(s2 + 1) * 512])
    for qb in range(4):
        for j in range(8):
            ps = L.ps[6 + (j % 2)]
            for pb in range(8):
                P.mm(ps[:, :512], Wo[:, pb, j * 128:(j + 1) * 128], OT[:, pb, qb * 512:(qb + 1) * 512],
                     start=(pb == 0), stop=(pb == 7))
            P.stt("dve", xT[:, j, qb * 512:(qb + 1) * 512], ps[:, :512], mod3[:, 16 + j, 0:1],
                  xT[:, j, qb * 512:(qb + 1) * 512], ALU.mult, ALU.add)
    hT = abf[:, 0:8 * NL].rearrange("p (k n) -> p k n", k=8)
    L.wslots = [abf[:, 16384 + i * 4096:16384 + (i + 1) * 4096] for i in range(4)]
    L.uT0 = abf[:, 32768:34816].rearrange("p (k n) -> p k n", k=4)
    L.uT1 = abf[:, 34816:36864].rearrange("p (k n) -> p k n", k=4)
    blocksF = [(tb * 512, tb * 512, 512, 0) for tb in range(4)]
    L.ffn(xT, hT, blocksF, 3, mod3, G2)
    out = L.outp("out", [NL, 1024])
    ost = [af[:, i * 1024:(i + 1) * 1024] for i in range(2)]
    for r in range(16):
        st = ost[r % 2]
        for half in range(2):
            ps = L.ps[(r * 2 + half) % 4]
            for kk in range(4):
                k = half * 4 + kk
                P.transpose(ps[:, kk * 128:(kk + 1) * 128], xT[:, k, r * 128:(r + 1) * 128], L.ident[:, :])
            P.copy("act" if half else "dve", st[:, half * 512:(half + 1) * 512], ps[:, 0:512])
        P.dma(L.q(), out[r * 128:(r + 1) * 128, :], st[:, :])
    return L


def prep_D(I, c, resC):
    b, half = c // 2, c % 2
    o = c ^ 1
    C32, S32 = rope_tables(32, half * 2048, 2048)
    C = np.ones((128, 2048), np.float32)
    S = np.zeros((128, 2048), np.float32)
    C[64:96] = C32
    S[64:96] = S32
    gq = np.zeros((128, 1), np.float32)
    gq[0:96, 0] = I["mla_q_norm"][0]
    return {
        "ident_in": np.eye(128, dtype=np.float32),
        "xT_in": resC[c]["xT_out"][:, :, 0:2048],
        "mod3_in": resC[c]["mod_out"],
        "n1g3": cols(I["norm1_g"][3], 8), "n2g3": cols(I["norm2_g"][3], 8),
        "mla_wdq": I["mla_w_dq"][0], "mla_wuq": I["mla_w_uq"][0], "mla_wo": I["mla_w_o"][0],
        "mla_qlgc": cols(I["mla_q_lora_norm"][0], 6), "mla_qgc": gq,
        "perm96": perm_matrix(32, 1, base=64),
        "ropeC96": C, "ropeS96": S,
        "kT3_own": resC[c]["kT3_out"], "kT3_oth": resC[o]["kT3_out"][:, :, 0:2048],
        "v3_own": resC[c]["v3_out"], "v3_oth": resC[o]["v3_out"][:, 0:16, :],
        "wff1_3": I["w_ff1"][3], "wff2_3": I["w_ff2"][3],
    }


def kernel(**inputs):
    I = {k: np.asarray(v) for k, v in inputs.items()}
    resA = run_launch("A", build_A, [prep_A(I, c) for c in range(CORES)])
    resB = run_launch("B", build_B, [prep_B(I, c, resA) for c in range(CORES)])
    resC = run_launch("C", build_C, [prep_C(I, c, resB) for c in range(CORES)])
    resD = run_launch("D", build_D, [prep_D(I, c, resC) for c in range(CORES)])
    out = np.empty((4, 4096, 1024), np.float32)
    for c in range(CORES):
        out[c // 2, (c % 2) * 2048:(c % 2 + 1) * 2048, :] = resD[c]["out"]
    return out
```

```python
from concourse.bass_utils import run_bass_kernel_spmd

import numpy as np
from contextlib import ExitStack
import concourse.bass as bass
import concourse.mybir as mybir

F32 = mybir.dt.float32
BF16 = mybir.dt.bfloat16
AF = mybir.ActivationFunctionType
ALU = mybir.AluOpType

ENGS = ["pe", "act", "dve", "pool", "sp"]
N_DMA_SEMS = 6


def _region(ap):
    t = ap.tensor
    dims = ap.ap
    off = int(ap.offset)
    cls = type(t).__name__
    if cls.startswith("DRam"):
        ext = 1
        for s, c in dims:
            ext += (c - 1) * abs(s)
        return (t.name, 0, 1, off, off + ext)
    pstep, pcnt = dims[0]
    p0 = off // pstep if pstep > 0 else 0
    lo = off - p0 * pstep
    ext = 1
    for s, c in dims[1:]:
        ext += (c - 1) * abs(s)
    return (t.name, p0, p0 + pcnt, lo, lo + ext)


def _overlap(a, b):
    return a[1] < b[2] and b[1] < a[2] and a[3] < b[4] and b[3] < a[4]


class Op:
    __slots__ = ("eng", "emit", "reads", "writes", "is_dma", "idx", "waits",
                 "signal", "dsem", "dwait_prev", "clock", "dma_inc")


class Prog:
    def __init__(self, nc, same_engine_sync=True):
        self.nc = nc
        self.ops = {e: [] for e in ENGS}
        self.same_engine_sync = same_engine_sync
        self.track = {}
        self.know = {e: {} for e in ENGS}
        self.dma_rr = {e: 0 for e in ENGS}
        self.dma_last = {e: [None] * N_DMA_SEMS for e in ENGS}
        self.dma_count = {e: [0] * N_DMA_SEMS for e in ENGS}
        self.nops = 0
        self.notrack = set()

    def _add(self, eng, emit, reads, writes, is_dma=False, dma_inc=16):
        op = Op()
        op.eng = eng
        op.emit = emit
        op.is_dma = is_dma
        op.dma_inc = dma_inc
        op.idx = len(self.ops[eng])
        op.waits = {}
        op.signal = False
        op.dsem = None
        rr = [_region(a) for a in reads if a.tensor.name not in self.notrack]
        wr = [_region(a) for a in writes]
        deps = []
        for r in rr:
            for rec in self.track.get(r[0], ()):
                if rec[1] is not None and _overlap(rec[0], r):
                    deps.append(rec[1])
        for w in wr:
            for rec in self.track.get(w[0], ()):
                if _overlap(rec[0], w):
                    if rec[1] is not None and not (rec[1][0] == "E" and rec[1][1] == eng and not is_dma):
                        deps.append(rec[1])
                    for tok in rec[2].values():
                        if not (tok[0] == "E" and tok[1] == eng and not is_dma):
                            deps.append(tok)
        know = self.know[eng]
        if is_dma:
            q = self.dma_rr[eng]
            self.dma_rr[eng] = (q + 1) % N_DMA_SEMS
            prev = self.dma_last[eng][q]
            if prev is not None:
                deps.append(prev)
            self.dma_count[eng][q] += 1
            op.dsem = q
            mytok = ("D", eng, q, self.dma_count[eng][q], None)
            self.dma_last[eng][q] = mytok
        else:
            mytok = ("E", eng, op.idx, None)
        for tok in deps:
            if tok[0] == "E":
                _, e2, i2, _ = tok
                if e2 == eng:
                    if eng == "pe" or not self.same_engine_sync:
                        continue
                key = ("E", e2)
                val = i2
            else:
                _, e2, q2, c2, _ = tok
                key = ("D", e2, q2)
                val = c2
            if know.get(key, -1) >= val:
                continue
            if op.waits.get(key, -1) < val:
                op.waits[key] = val
        for key, val in op.waits.items():
            know[key] = max(know.get(key, -1), val)
            if key[0] == "E":
                src = self.ops[key[1]][val]
                src.signal = True
                for k2, v2 in src.clock.items():
                    if know.get(k2, -1) < v2:
                        know[k2] = v2
        op.clock = dict(know)
        skey = eng if not is_dma else ("dma", eng, op.dsem)
        for r in rr:
            lst = self.track.setdefault(r[0], [])
            for rec in lst:
                if rec[0] == r:
                    rec[2][skey] = mytok
                    break
            else:
                lst.append([r, None, {skey: mytok}])
        for w in wr:
            lst = self.track.setdefault(w[0], [])
            newl = []
            found = False
            for rec in lst:
                r0 = rec[0]
                if r0 == w:
                    rec[1] = mytok
                    rec[2] = {}
                    newl.append(rec)
                    found = True
                elif (w[1] <= r0[1] and r0[2] <= w[2] and w[3] <= r0[3] and r0[4] <= w[4]):
                    continue
                else:
                    newl.append(rec)
            if not found:
                newl.append([w, mytok, {}])
            self.track[w[0]] = newl
        self.ops[eng].append(op)
        self.nops += 1
        return op

    def mm(self, out, lhsT, rhs, start=True, stop=True, **kw):
        rd = [lhsT, rhs] + ([] if start else [out])
        return self._add("pe", lambda e: e.matmul(out, lhsT, rhs, start=start, stop=stop, **kw), rd, [out])

    def transpose(self, out, in_, ident):
        return self._add("pe", lambda e: e.transpose(out, in_, ident), [in_, ident], [out])

    def act(self, out, in_, func, bias=None, scale=None, accum_out=None, eng="act"):
        kw = {}
        rd = [in_]
        if bias is not None:
            kw["bias"] = bias
            if not isinstance(bias, (int, float)):
                rd.append(bias)
        if scale is not None:
            kw["scale"] = scale
            if not isinstance(scale, (int, float)):
                rd.append(scale)
        wr = [out]
        if accum_out is not None:
            kw["accum_out"] = accum_out
            wr.append(accum_out)
        return self._add("act", lambda e: e.activation(out, in_, func, **kw), rd, wr)

    def tt(self, eng, out, a, b, op):
        return self._add(eng, lambda e: e.tensor_tensor(out, a, b, op), [a, b], [out])

    def ts(self, eng, out, a, s1, s2, op0, op1=None, accum_out=None):
        rd = [a]
        if not isinstance(s1, (int, float)):
            rd.append(s1)
        if s2 is not None and not isinstance(s2, (int, float)):
            rd.append(s2)
        wr = [out]
        kw = {}
        if accum_out is not None:
            kw["accum_out"] = accum_out
            wr.append(accum_out)
        if op1 is None:
            return self._add(eng, lambda e: e.tensor_scalar(out, a, s1, None, op0, **kw), rd, wr)
        return self._add(eng, lambda e: e.tensor_scalar(out, a, s1, s2, op0, op1, **kw), rd, wr)

    def stt(self, eng, out, a, s, b, op0, op1):
        rd = [a, b]
        if not isinstance(s, (int, float)):
            rd.append(s)
        return self._add(eng, lambda e: e.scalar_tensor_tensor(out, a, s, b, op0, op1), rd, [out])

    def copy(self, eng, out, in_):
        if eng == "act":
            return self._add("act", lambda e: e.copy(out, in_), [in_], [out])
        return self._add(eng, lambda e: e.tensor_copy(out, in_), [in_], [out])

    def memset(self, eng, out, val):
        return self._add(eng, lambda e: e.memset(out, val), [], [out])

    def recip(self, out, in_):
        return self._add("dve", lambda e: e.reciprocal(out, in_), [in_], [out])

    def dma(self, q, out, in_, **kw):
        return self._add(q, lambda e: e.dma_start(out, in_, **kw), [in_], [out], is_dma=True)

    def custom(self, eng, emit, reads, writes, is_dma=False, dma_inc=16):
        return self._add(eng, emit, reads, writes, is_dma=is_dma, dma_inc=dma_inc)

    def emit(self, final_wait_all_dma=True):
        nc = self.nc
        with ExitStack() as st:
            esem = {e: st.enter_context(nc.semaphore("se_" + e)) for e in ENGS}
            dsem = {e: [st.enter_context(nc.semaphore("sd_%s%d" % (e, i))) for i in range(N_DMA_SEMS)]
                    for e in ENGS}
            sigcount = {}
            for e in ENGS:
                c = 0
                arr = []
                for op in self.ops[e]:
                    if (not op.is_dma) and op.signal:
                        c += 1
                    arr.append(c)
                sigcount[e] = arr
            block = st.enter_context(nc.Block())
            engobj = {"pe": block.tensor, "act": block.scalar, "dve": block.vector,
                      "pool": block.gpsimd, "sp": block.sync}

            def make(e):
                def body(eng):
                    for op in self.ops[e]:
                        for key, val in op.waits.items():
                            if key[0] == "E":
                                eng.wait_ge(esem[key[1]], sigcount[key[1]][val])
                            else:
                                eng.wait_ge(dsem[key[1]][key[2]], val * 16)
                        ins = op.emit(eng)
                        if op.is_dma:
                            ins.then_inc(dsem[e][op.dsem], op.dma_inc)
                        elif op.signal:
                            ins.then_inc(esem[e], 1)
                    for q in range(N_DMA_SEMS):
                        if self.dma_count[e][q] > 0:
                            eng.wait_ge(dsem[e][q], self.dma_count[e][q] * 16)
                return body

            for e in ENGS:
                engobj[e](make(e))

EPS = 1e-6
NT = 2304
CORES = 8
GRID_W = 64


def chunks(total, step=512):
    out = []
    c = 0
    while c < total:
        out.append((c, min(step, total - c)))
        c += step
    return out


class LB:
    def __init__(self):
        self.nc = bass.Bass("TRN2", target_bir_lowering=False)
        self.P = Prog(self.nc)
        self.psbig = self.nc.alloc_psum_tensor("psbig", [128, 4096], F32)
        self.ps = [self.psbig[:, i * 512:(i + 1) * 512] for i in range(8)]
        self.in_names = []
        self.out_names = []
        self.dq = 0
        P = self.P
        self.ident = self.sb("ident", [128, 128], F32)
        P.dma("sp", self.ident[:], self.inp("ident_in", [128, 128]))
        self.ones_bf = self.sb("ones_bf", [128, 128], BF16)
        P.memset("dve", self.ones_bf[:], 1.0)
        self.ones_f = self.sb("ones_f", [128, 128], F32)
        P.memset("dve", self.ones_f[:], 1.0)
        self.sqb = self.sb("sqb", [128, 8, 512], BF16)
        self.rstd = self.sb("rstd", [128, 512], F32)
        self.ntmp = getattr(type(self), "NTMP", 4)
        self.tmpf = [self.sb("tmpf%d" % i, [128, 512], F32) for i in range(self.ntmp)]
        self.tmpi = 0
        self.rr = 0

    def inp(self, name, shape, dt=F32):
        t = self.nc.dram_tensor(name, list(shape), dt, kind="ExternalInput")
        self.P.notrack.add(name)
        self.in_names.append(name)
        return t.ap()

    def outp(self, name, shape, dt=F32):
        t = self.nc.dram_tensor(name, list(shape), dt, kind="ExternalOutput")
        self.out_names.append(name)
        return t.ap()

    def sb(self, name, shape, dt):
        return self.nc.alloc_sbuf_tensor(name, list(shape), dt)

    def q(self):
        self.dq ^= 1
        return "sp" if self.dq else "act"

    def tmp(self):
        self.tmpi = (self.tmpi + 1) % self.ntmp
        return self.tmpf[self.tmpi]

    def ve(self):
        self.rr ^= 1
        return "dve" if self.rr else "pool"

    def cconst(self, val):
        if not hasattr(self, "_cc"):
            self._cc = {}
        if val not in self._cc:
            t = self.sb("cc%d" % len(self._cc), [128, 1], F32)
            self.P.memset("dve", t[:], float(val))
            self._cc[val] = t
        return self._cc[val]

    def rsqrt(self, out, in_ps, addc):
        shp = list(out.shape)
        p0 = out.base_partition()
        cb = self.cconst(float(addc))
        self.P.act(out, in_ps, AF.Ln, bias=cb[p0:p0 + shp[0], 0:1])
        self.P.act(out, out, AF.Exp, scale=-0.5)

    def load_cols(self, name, shape):
        t = self.sb(name + "_sb", shape, F32)
        self.P.dma(self.q(), t[:], self.inp(name, shape))
        return t

    def setup_c(self):
        P = self.P
        cv = self.load_cols("cvec", [128, 8, 2])
        self.scT = self.sb("scT", [128, 8, 2], F32)
        P.act(self.scT[:], cv[:], AF.Silu)
        sw = self.slabw = getattr(self, "slabw", 256)
        self.wslab = [self.af[:, i * 8 * sw:(i + 1) * 8 * sw].rearrange("p (k n) -> p k n", k=8) for i in range(2)]

    def compute_mod(self, i):
        P = self.P
        wm = self.inp("wmod%d" % i, [1024, 6144]).rearrange("(k p) n -> p k n", p=128)
        bm = self.load_cols("bmodc%d" % i, [128, 48])
        mod = self.sb("mod%d" % i, [128, 48, 2], F32)
        psm = self.ps[7]
        sw = self.slabw
        for s in range(6144 // sw):
            slab = self.wslab[s % 2]
            P.dma(self.q(), slab[:], wm[:, :, s * sw:(s + 1) * sw])
            for jj in range(sw // 128):
                j = s * (sw // 128) + jj
                for k in range(8):
                    P.mm(psm[:, j * 2:(j + 1) * 2], slab[:, k, jj * 128:(jj + 1) * 128], self.scT[:, k, :],
                         start=(k == 0), stop=(k == 7))
        psv = psm[:, 0:96].rearrange("p (j s) -> p j s", s=2)
        for s in range(2):
            P.tt("dve", mod[:, :, s], psv[:, :, s], bm[:, :], ALU.add)
        return mod

    def make_gain(self, name, mod, m, ng):
        P = self.P
        G = self.sb(name, [128, 8, 2], F32)
        for s in range(2):
            P.ts("dve", G[:, :, s], mod[:, m * 8:(m + 1) * 8, s], 1.0, 32.0, ALU.add, ALU.mult)
            P.tt("dve", G[:, :, s], G[:, :, s], ng[:, :], ALU.mult)
        return G

    def norm_block(self, xT, c0, n, G, mod, msh, s, out_fn, psb=6):
        P = self.P
        P.act(self.sqb[:, :, :n], xT[:, :, c0:c0 + n], AF.Square)
        ps = self.ps[psb]
        for k in range(8):
            P.mm(ps[:, :n], self.ones_bf[:], self.sqb[:, k, :n], start=(k == 0), stop=(k == 7))
        self.rsqrt(self.rstd[:, :n], ps[:, :n], 1024.0 * EPS)
        for k in range(8):
            t = self.tmp()
            P.tt(self.ve(), t[:, :n], xT[:, k, c0:c0 + n], self.rstd[:, :n], ALU.mult)
            P.act(out_fn(k), t[:, :n], AF.Identity, scale=G[:, k, s:s + 1],
                  bias=mod[:, msh * 8 + k, s:s + 1])

    def wload(self, dst, src):
        self.P.dma("pool", dst, src)

    def setup_wslots(self, n=4):
        self.wslots = [self.sb("wslot%d" % i, [128, 4096], BF16) for i in range(n)]

    def ffn(self, xT, hT, blocks, i, mod, G2):
        P = self.P
        w1 = self.inp("wff1_%d" % i, [1024, 4096]).rearrange("(k p) n -> p k n", p=128)
        w2 = self.inp("wff2_%d" % i, [4096, 1024])
        for (xc0, hc0, n, s) in blocks:
            self.norm_block(xT, xc0, n, G2, mod, 3, s, lambda k: hT[:, k, hc0:hc0 + n])
        uT = [self.uT0, self.uT1]

        def issue(g):
            a = self.wslots[(g % 2) * 2][:, :].rearrange("p (k n) -> p k n", k=8)
            b = self.wslots[(g % 2) * 2 + 1][:, :].rearrange("p (k n) -> p k n", k=4)
            self.wload(a, w1[:, :, g * 512:(g + 1) * 512])
            self.wload(b, w2[g * 512:(g + 1) * 512, :].rearrange("(k p) n -> p k n", p=128))
            return a, b

        nxt = issue(0)
        cnt = 0
        for g in range(8):
            W1g, W2g = nxt
            if g + 1 < 8:
                nxt = issue(g + 1)
            for (xc0, hc0, n, s) in blocks:
                u = uT[cnt % 2]
                cnt += 1
                for fc in range(4):
                    ps = self.ps[fc]
                    for k in range(8):
                        P.mm(ps[:, :n], W1g[:, k, fc * 128:(fc + 1) * 128], hT[:, k, hc0:hc0 + n],
                             start=(k == 0), stop=(k == 7))
                    t = self.tmp()
                    P.act(t[:, :n], ps[:, :n], AF.Relu)
                    P.tt("pool", u[:, fc, :n], t[:, :n], t[:, :n], ALU.mult)
                for j in range(8):
                    ps = self.ps[4 + (j % 3)]
                    for fc in range(4):
                        P.mm(ps[:, :n], W2g[:, fc, j * 128:(j + 1) * 128], u[:, fc, :n],
                             start=(fc == 0), stop=(fc == 3))
                    P.stt("dve", xT[:, j, xc0:xc0 + n], ps[:, :n], mod[:, 40 + j, s:s + 1],
                          xT[:, j, xc0:xc0 + n], ALU.mult, ALU.add)

    def qk_norm_rope(self, ps_raw, rows, n, gcol, blk_ones, hd, perm, C, S, out_ap, ps_ss, ps_sw, defer=False):
        P = self.P
        self.sqi = (getattr(self, "sqi", 0) + 2) % 8
        qg = self.tmp()
        P.act(qg[:rows, :n], ps_raw[:rows, :n], AF.Identity, scale=gcol[:rows, 0:1])
        sq = self.sqb[:, self.sqi, :]
        P.act(sq[:rows, :n], ps_raw[:rows, :n], AF.Square)
        P.mm(ps_ss[:rows, :n], blk_ones[:rows, :rows], sq[:rows, :n])
        if C is not None:
            qb = self.sqb[:, self.sqi + 1, :]
            P.copy("pool", qb[:rows, :n], qg[:rows, :n])
            P.mm(ps_sw[:rows, :n], perm[:rows, :rows], qb[:rows, :n])

        def stage_b():
            rs = self.tmp()
            self.rsqrt(rs[:rows, :n], ps_ss[:rows, :n], float(hd) * EPS)
            if C is None:
                P.stt("dve", out_ap, qg[:rows, :n], float(hd) ** 0.5, rs[:rows, :n], ALU.mult, ALU.mult)
                return
            t1 = self.tmp()
            P.tt("dve", t1[:rows, :n], qg[:rows, :n], C, ALU.mult)
            t2 = self.tmp()
            P.tt("dve", t2[:rows, :n], ps_sw[:rows, :n], S, ALU.mult)
            P.tt("pool", t1[:rows, :n], t1[:rows, :n], t2[:rows, :n], ALU.add)
            P.stt("dve", out_ap, t1[:rows, :n], float(hd) ** 0.5, rs[:rows, :n], ALU.mult, ALU.mult)

        if defer:
            return stage_b
        stage_b()

    def finish(self):
        self.P.emit()
        return self.nc


NTA = 2336
UW = 2368


def build_A():
    L = LB()
    P = L.P
    nc = L.nc
    x_tok = L.inp("x_tok", [NTA, 1024])
    xT = L.sb("xT", [128, 8, NTA], F32)
    abf = L.sb("abf", [128, 39424], BF16)
    af = L.sb("af", [128, 4096], F32)
    L.af = af
    vblk = af[:, :].rearrange("p (k n) -> p k n", k=8)
    xst = [af[:, i * 1024:(i + 1) * 1024] for i in range(2)]
    for r in range(19):
        rows = 128 if r < 18 else 32
        st = xst[r % 2]
        P.dma(L.q(), st[:rows, :], x_tok[r * 128:r * 128 + rows, :])
        for half in range(2):
            ps = L.ps[(r * 2 + half) % 4]
            for kk in range(4):
                k = half * 4 + kk
                P.transpose(ps[:, kk * 128:kk * 128 + rows], st[:rows, k * 128:(k + 1) * 128], L.ident[:rows, :rows])
            src = ps[:, 0:512].rearrange("p (a b) -> p a b", a=4)[:, :, :rows]
            dst = xT[:, half * 4:(half + 1) * 4, r * 128:r * 128 + rows]
            P.copy("act" if half else "dve", dst, src)
    L.setup_c()
    mod0 = L.compute_mod(0)
    mod1 = L.compute_mod(1)
    n1g0 = L.load_cols("n1g0", [128, 8])
    n2g0 = L.load_cols("n2g0", [128, 8])
    n1g1 = L.load_cols("n1g1", [128, 8])
    G1_0 = L.make_gain("G1_0", mod0, 1, n1g0)
    G2_0 = L.make_gain("G2_0", mod0, 4, n2g0)
    G1_1 = L.make_gain("G1_1", mod1, 1, n1g1)
    pw1 = abf[:, 0:16384].rearrange("p (k n) -> p k n", k=8)
    hblk = [abf[:, 16384:20480].rearrange("p (k n) -> p k n", k=8) for i in range(2)]
    U = abf[:, 20480:20480 + 8 * UW].rearrange("p (k n) -> p k n", k=8)
    w_pw1 = L.inp("conv_pw1", [1024, 2048]).rearrange("(k p) n -> p k n", p=128)
    for s in range(4):
        L.wload(pw1[:, :, s * 512:(s + 1) * 512], w_pw1[:, :, s * 512:(s + 1) * 512])
    hmask = L.load_cols("hmask", [128, 32])
    wdw = L.load_cols("conv_dwc", [128, 8, 31])
    bdw = L.load_cols("conv_bdwc", [128, 8])
    lng = L.load_cols("conv_lngc", [128, 8])
    lnb = L.load_cols("conv_lnbc", [128, 8])
    lng32 = L.sb("lng32", [128, 8], F32)
    P.ts("dve", lng32[:], lng[:], 32.0, None, ALU.mult)
    ident_bf = L.sb("ident_bf", [128, 128], BF16)
    P.copy("dve", ident_bf[:], L.ident[:])
    P.memset("pool", U[:, :, 2080:2096], 0.0)
    P.memset("pool", U[:, :, 2352:2368], 0.0)
    blocksA = [(0, 512, 0, 0), (512, 512, 512, 0), (1024, 512, 1024, 0), (1536, 512, 1536, 0),
               (2048, 32, 2048, 0), (2080, 256, 2096, 1)]
    for bi, (c0, n, uc0, s) in enumerate(blocksA):
        hb = hblk[bi % 2]
        L.norm_block(xT, c0, n, G1_0, mod0, 0, s, lambda k: hb[:, k, :n])
        for j in range(8):
            psa = L.ps[(j % 2) * 2]
            psg = L.ps[(j % 2) * 2 + 1]
            for k in range(8):
                P.mm(psa[:, :n], pw1[:, k, j * 128:(j + 1) * 128], hb[:, k, :n], start=(k == 0), stop=(k == 7))
            for k in range(8):
                P.mm(psg[:, :n], pw1[:, k, 1024 + j * 128:1024 + (j + 1) * 128], hb[:, k, :n],
                     start=(k == 0), stop=(k == 7))
            sg = L.tmp()
            P.act(sg[:, :n], psg[:, :n], AF.Sigmoid)
            P.tt("dve", U[:, j, uc0:uc0 + n], psa[:, :n], sg[:, :n], ALU.mult)
    for j in range(8):
        P.tt("pool", U[:, j, 0:16], U[:, j, 0:16], hmask[:, 0:16], ALU.mult)
        P.tt("pool", U[:, j, 2064:2080], U[:, j, 2064:2080], hmask[:, 16:32], ALU.mult)
    pw2 = abf[:, 0:8192].rearrange("p (k n) -> p k n", k=8)
    w_pw2 = L.inp("conv_pw2", [1024, 1024]).rearrange("(k p) n -> p k n", p=128)
    for s in range(2):
        L.wload(pw2[:, :, s * 512:(s + 1) * 512], w_pw2[:, :, s * 512:(s + 1) * 512])
    zblk = abf[:, 8192:12288].rearrange("p (k n) -> p k n", k=8)
    diag = [abf[:, 12288 + i * 3968:12288 + (i + 1) * 3968].rearrange("p (t n) -> p t n", t=31) for i in range(2)]
    blocksC = [(16 + tb * 512, 512, 16 + tb * 512, 0) for tb in range(4)] + [(2080, 256, 2096, 1)]
    dcount = 0
    for (xc0, n, uc0, s) in blocksC:
        for j in range(8):
            dg = diag[dcount % 2]
            dcount += 1
            for tap in range(31):
                P.ts("dve", dg[:, tap, :], ident_bf[:], wdw[:, j, tap:tap + 1], None, ALU.mult)
            ps = L.ps[j % 4]
            for tap in range(31):
                P.mm(ps[:, :n], dg[:, tap, :], U[:, j, uc0 - 15 + tap:uc0 - 15 + tap + n],
                     start=(tap == 0), stop=(tap == 30))
            P.act(vblk[:, j, :n], ps[:, :n], AF.Identity, bias=bdw[:, j:j + 1])
        psm = L.ps[4]
        for j in range(8):
            P.mm(psm[:, :n], L.ones_f[:], vblk[:, j, :n], start=(j == 0), stop=(j == 7))
        mu = L.tmp()
        P.ts("dve", mu[:, :n], psm[:, :n], 1.0 / 1024.0, None, ALU.mult)
        for j in range(8):
            P.tt(L.ve(), vblk[:, j, :n], vblk[:, j, :n], mu[:, :n], ALU.subtract)
        P.act(L.sqb[:, :, :n], vblk[:, :, :n], AF.Square)
        psv = L.ps[5]
        for j in range(8):
            P.mm(psv[:, :n], L.ones_bf[:], L.sqb[:, j, :n], start=(j == 0), stop=(j == 7))
        L.rsqrt(L.rstd[:, :n], psv[:, :n], 1024.0 * EPS)
        for j in range(8):
            t = L.tmp()
            P.tt(L.ve(), t[:, :n], vblk[:, j, :n], L.rstd[:, :n], ALU.mult)
            P.act(zblk[:, j, :n], t[:, :n], AF.Silu, scale=lng32[:, j:j + 1], bias=lnb[:, j:j + 1])
        for j in range(8):
            ps = L.ps[6 + (j % 2)]
            for k in range(8):
                P.mm(ps[:, :n], pw2[:, k, j * 128:(j + 1) * 128], zblk[:, k, :n], start=(k == 0), stop=(k == 7))
            P.stt("dve", xT[:, j, xc0:xc0 + n], ps[:, :n], mod0[:, 16 + j, s:s + 1],
                  xT[:, j, xc0:xc0 + n], ALU.mult, ALU.add)
    hT = abf[:, 0:8 * NT].rearrange("p (k n) -> p k n", k=8)
    L.wslots = [abf[:, 18432 + i * 4096:18432 + (i + 1) * 4096] for i in range(4)]
    L.uT0 = abf[:, 34816:36864].rearrange("p (k n) -> p k n", k=4)
    L.uT1 = abf[:, 36864:38912].rearrange("p (k n) -> p k n", k=4)
    blocksF = [(16 + tb * 512, tb * 512, 512, 0) for tb in range(4)] + [(2080, 2048, 256, 1)]
    L.ffn(xT, hT, blocksF, 0, mod0, G2_0)
    for (xc0, hc0, n, s) in blocksF:
        L.norm_block(xT, xc0, n, G1_1, mod1, 0, s, lambda k: hT[:, k, hc0:hc0 + n])
    wkv = L.wslots[0][:, :].rearrange("p (k n) -> p k n", k=8)
    w_qkv = L.inp("gqa_wqkv", [1024, 1536]).rearrange("(k p) n -> p k n", p=128)
    L.wload(wkv[:, :, :], w_qkv[:, :, 1024:1536])
    gk = L.load_cols("gqa_kgc", [128, 1])
    ropeC_d = L.inp("ropeC64", [128, 2048])
    ropeS_d = L.inp("ropeS64", [128, 2048])
    ropeC = af[:, 3072:3584]
    ropeS = af[:, 3584:4096]
    permf = L.load_cols("perm64", [128, 128])
    perm = L.sb("perm_bf", [128, 128], BF16)
    P.copy("dve", perm[:], permf[:])
    blk1 = L.sb("blk64", [128, 128], BF16)
    P.memset("dve", blk1[:], 0.0)
    P.memset("dve", blk1[0:64, 0:64], 1.0)
    P.memset("dve", blk1[64:128, 64:128], 1.0)
    kT_out = L.outp("kT_out", [128, 2, NT])
    v_out = L.outp("v_out", [128, 18, 256])
    kst = [af[:, i * 512:(i + 1) * 512] for i in range(2)]
    vst = [af[:, 1024 + i * 1024:2048 + i * 1024].rearrange("p (k n) -> p k n", k=4) for i in range(2)]
    cnt = 0
    for (xc0, hc0, n, s) in blocksF:
        if s == 0:
            P.dma("sp", ropeC[:, :n], ropeC_d[:, hc0:hc0 + n])
            P.dma("act", ropeS[:, :n], ropeS_d[:, hc0:hc0 + n])
        for m in range(2):
            ps = L.ps[m]
            for k in range(8):
                P.mm(ps[:, :n], wkv[:, k, m * 128:(m + 1) * 128], hT[:, k, hc0:hc0 + n], start=(k == 0), stop=(k == 7))
            ko = kst[cnt % 2]
            cnt += 1
            if s == 0:
                L.qk_norm_rope(ps, 128, n, gk, blk1, 64, perm, ropeC[:, :n], ropeS[:, :n],
                               ko[:, :n], L.ps[2], L.ps[3])
            else:
                L.qk_norm_rope(ps, 128, n, gk, blk1, 64, None, None, None, ko[:, :n], L.ps[2], L.ps[3])
            P.dma(L.q(), kT_out[:, m, hc0:hc0 + n], ko[:, :n])
        vs = vst[(hc0 // 512) % 2]
        nt = n // 128
        for tt_ in range(nt):
            ps = L.ps[4 + (tt_ % 2)]
            for k in range(8):
                P.mm(ps[:, 0:256], hT[:, k, hc0 + tt_ * 128:hc0 + (tt_ + 1) * 128], wkv[:, k, 256:512],
                     start=(k == 0), stop=(k == 7))
            P.copy("act", vs[:, tt_, :], ps[:, 0:256])
        P.dma(L.q(), v_out[:, hc0 // 128:hc0 // 128 + nt, :], vs[:, :nt, :])
    mod_out = L.outp("mod_out", [128, 48, 2])
    P.dma("sp", mod_out, mod1[:])
    xT_out = L.outp("xT_out", [128, 8, NT])
    P.dma("sp", xT_out[:, :, 0:2048], xT[:, :, 16:2064])
    P.dma("act", xT_out[:, :, 2048:2304], xT[:, :, 2080:2336])
    return L


def cols(v, k):
    return np.ascontiguousarray(np.asarray(v, np.float32).reshape(k, 128).T)


def rope_tables(rot, pos0, n):
    q = rot // 4
    inv = (10000.0 ** (-np.arange(q, dtype=np.float32) / q)).astype(np.float32)
    t = np.arange(pos0, pos0 + n)
    row = (t // GRID_W).astype(np.float32)
    col = (t % GRID_W).astype(np.float32)
    ar = (inv[:, None] * row[None, :]).astype(np.float32)
    ac = (inv[:, None] * col[None, :]).astype(np.float32)
    C = np.concatenate([np.cos(ar), np.cos(ar), np.cos(ac), np.cos(ac)], 0).astype(np.float32)
    S = np.concatenate([-np.sin(ar), np.sin(ar), -np.sin(ac), np.sin(ac)], 0).astype(np.float32)
    return C, S


def perm_matrix(rot, reps, base=0, size=128):
    Pm = np.zeros((size, size), np.float32)
    q = rot // 4
    for r in range(reps):
        for i in range(rot):
            g = i // (2 * q)
            j = i % (2 * q)
            pj = j + q if j < q else j - q
            Pm[base + r * rot + g * 2 * q + pj, base + r * rot + i] = 1.0
    return Pm


_PROG_CACHE = {}


def get_prog(name, builder):
    if name not in _PROG_CACHE:
        L = builder()
        L.finish()
        _PROG_CACHE[name] = L
    return _PROG_CACHE[name]


def run_launch(name, builder, in_maps):
    L = get_prog(name, builder)
    maps = [{k: np.ascontiguousarray(m[k], dtype=np.float32) for k in L.in_names} for m in in_maps]
    res = run_bass_kernel_spmd(L.nc, maps, core_ids=list(range(CORES)))
    return res.results


def prep_A(I, c):
    b, half = c // 2, c % 2
    st = half * 2048
    x = I["x"][b]
    z16 = np.zeros((16, 1024), np.float32)
    left = x[st - 16:st] if half == 1 else z16
    right = x[st + 2048:st + 2064] if half == 0 else z16
    x_tok = np.concatenate([left, x[st:st + 2048], right, I["ctx"][b]], 0)
    cvec = np.stack([cols(I["c"][b], 8), cols(I["c_ctx"], 8)], -1)
    hmask = np.zeros((128, 32), np.float32)
    hmask[:, 0:16] = 1.0 if half == 1 else 0.0
    hmask[:, 16:32] = 1.0 if half == 0 else 0.0
    C, S = rope_tables(64, st, 2048)
    d = {
        "ident_in": np.eye(128, dtype=np.float32),
        "x_tok": x_tok, "cvec": cvec, "hmask": hmask,
        "wmod0": I["w_mod"][0], "wmod1": I["w_mod"][1],
        "bmodc0": cols(I["b_mod"][0], 48), "bmodc1": cols(I["b_mod"][1], 48),
        "n1g0": cols(I["norm1_g"][0], 8), "n2g0": cols(I["norm2_g"][0], 8), "n1g1": cols(I["norm1_g"][1], 8),
        "conv_pw1": I["conv_w_pw1"][0], "conv_pw2": I["conv_w_pw2"][0],
        "conv_dwc": np.ascontiguousarray(I["conv_w_dw"][0].T.reshape(8, 128, 31).transpose(1, 0, 2)),
        "conv_bdwc": cols(I["conv_b_dw"][0], 8), "conv_lngc": cols(I["conv_ln_g"][0], 8),
        "conv_lnbc": cols(I["conv_ln_b"][0], 8),
        "wff1_0": I["w_ff1"][0], "wff2_0": I["w_ff2"][0],
        "gqa_wqkv": I["gqa_w_qkv"][0],
        "gqa_kgc": np.tile(I["gqa_k_norm"][0], 2).reshape(128, 1),
        "ropeC64": np.tile(C, (2, 1)), "ropeS64": np.tile(S, (2, 1)),
        "perm64": perm_matrix(64, 2),
    }
    return d


def attend_head(L, k_lhsT_fn, q_rhs, v_lhsT_fn, lcs, n, scale, ps_o, side, out_ap, ptiles, ebias=None, pending=None):
    P = L.P
    groups = [lcs[i:i + 2] for i in range(0, len(lcs), 2)]
    ng = len(groups)
    S = [L.psbig[:, 0:1024], L.psbig[:, 1024:2048]]

    def qk(gi):
        s_ = S[gi % 2]
        for j, lc in enumerate(groups[gi]):
            P.mm(s_[:, j * 512:j * 512 + n], k_lhsT_fn(lc), q_rhs)

    qk(0)
    if ng > 1:
        qk(1)
    for gi in range(ng):
        if gi == 6 and pending is not None:
            pending()
            pending = None
        g = groups[gi]
        s_ = S[gi % 2]
        pt = ptiles[gi % 2]
        if n == 512:
            w = len(g) * 512
            P.act(pt[:, 0:w], s_[:, 0:w], AF.Exp, scale=scale, bias=ebias)
        else:
            P.act(pt[:, :].rearrange("p (j c) -> p j c", j=2)[:, 0:len(g), 0:n],
                  s_.rearrange("p (j c) -> p j c", j=2)[:, 0:len(g), 0:n], AF.Exp, scale=scale, bias=ebias)
        if gi + 2 < ng:
            qk(gi + 2)
        for j, lc in enumerate(g):
            P.mm(ps_o[:, :n], v_lhsT_fn(lc), pt[:, j * 512:j * 512 + n], start=(gi == 0 and j == 0),
                 stop=(gi == ng - 1 and j == len(g) - 1))
    if pending is not None:
        pending()
    sl0, sl1 = (0, 64) if side == 0 else (64, 128)
    drow = 64 if side == 0 else 0
    o_sb = L.tmp()
    P.copy("dve", o_sb[:, :n], ps_o[:, :n])
    rd = L.tmp()
    P.recip(rd[drow:drow + 1, :n], o_sb[drow:drow + 1, :n])

    def fin():
        P.mm(ps_o[:, :n], L.ones_f[drow:drow + 1, 0:128], rd[drow:drow + 1, :n])
        P.tt("dve", out_ap, o_sb[sl0:sl1, :n], ps_o[sl0:sl1, :n], ALU.mult)
    return fin


def attend_pair(L, k_lo_fn, k_hi_fn, q_lo, q_hi, v_lo_fn, v_hi_fn, lcs, n, scale, ps_lo, ps_hi, out_lo, out_hi,
                ptiles, pending=None):
    P = L.P
    ng = len(lcs)
    S = [L.psbig[:, 0:1024], L.psbig[:, 1024:2048]]

    def qk(gi):
        s_ = S[gi % 2]
        P.mm(s_[:, 0:n], k_lo_fn(lcs[gi]), q_lo)
        P.mm(s_[:, 512:512 + n], k_hi_fn(lcs[gi]), q_hi)

    qk(0)
    if ng > 1:
        qk(1)
    for gi in range(ng):
        if gi == 10 and pending is not None:
            pending()
            pending = None
        s_ = S[gi % 2]
        pt = ptiles[gi % 2]
        if n == 512:
            P.act(pt[:, 0:1024], s_[:, 0:1024], AF.Exp, scale=scale)
        else:
            P.act(pt[:, :].rearrange("p (j c) -> p j c", j=2)[:, :, 0:n],
                  s_.rearrange("p (j c) -> p j c", j=2)[:, :, 0:n], AF.Exp, scale=scale)
        if gi + 2 < ng:
            qk(gi + 2)
        P.mm(ps_lo[:, :n], v_lo_fn(lcs[gi]), pt[:, 0:n], start=(gi == 0), stop=(gi == ng - 1))
        P.mm(ps_hi[:, :n], v_hi_fn(lcs[gi]), pt[:, 512:512 + n], start=(gi == 0), stop=(gi == ng - 1))
    if pending is not None:
        pending()
    st = []
    for (side, ps_o, out_ap) in ((0, ps_lo, out_lo), (1, ps_hi, out_hi)):
        drow = 64 if side == 0 else 0
        o_sb = L.tmp()
        P.copy("dve", o_sb[:, :n], ps_o[:, :n])
        st.append((side, ps_o, out_ap, o_sb))
    for (side, ps_o, out_ap, o_sb) in list(st):
        drow = 64 if side == 0 else 0
        rd = L.tmp()
        P.recip(rd[drow:drow + 1, :n], o_sb[drow:drow + 1, :n])
        st.append(rd)

    def fin():
        for idx in range(2):
            side, ps_o, out_ap, o_sb = st[idx]
            rd = st[2 + idx]
            sl0, sl1 = (0, 64) if side == 0 else (64, 128)
            drow = 64 if side == 0 else 0
            P.mm(ps_o[:, :n], L.ones_f[drow:drow + 1, 0:128], rd[drow:drow + 1, :n])
            P.tt("dve", out_ap, o_sb[sl0:sl1, :n], ps_o[sl0:sl1, :n], ALU.mult)
    return fin


def build_B():
    L = LB()
    P = L.P
    xT = L.sb("xT", [128, 8, NT], F32)
    xT_in = L.inp("xT_in", [128, 8, NT])
    for k in range(8):
        P.dma(L.q(), xT[:, k, :], xT_in[:, k, :])
    abf = L.sb("abf", [128, 50432], BF16)
    af = L.sb("af", [128, 2048], F32)
    L.af = af
    L.slabw = 128
    mod1 = L.load_cols("mod1_in", [128, 48, 2])
    n1g = L.load_cols("n1g1", [128, 8])
    n2g = L.load_cols("n2g1", [128, 8])
    G1 = L.make_gain("G1", mod1, 1, n1g)
    G2 = L.make_gain("G2", mod1, 4, n2g)
    K_all = abf[:, 0:8704].rearrange("p (m n) -> p m n", m=2)
    Vext = abf[:, 8704:21760].rearrange("p (c n) -> p c n", c=34)
    Wq = abf[:, 21760:29952].rearrange("p (k n) -> p k n", k=8)
    Wo = abf[:, 29952:38144].rearrange("p (k n) -> p k n", k=8)
    hblk = abf[:, 38144:42240].rearrange("p (k n) -> p k n", k=8)
    QTb = abf[:, 42240:46336].rearrange("p (k n) -> p k n", k=8)
    OTb = abf[:, 46336:50432].rearrange("p (k n) -> p k n", k=8)
    ptiles = [L.sb("ptile%d" % i, [128, 1024], BF16) for i in range(2)]
    kT_own = L.inp("kT_own", [128, 2, NT])
    kT_oth = L.inp("kT_oth", [128, 2, 2048])
    v_own = L.inp("v_own", [128, 18, 256])
    v_oth = L.inp("v_oth", [128, 16, 256])
    for m in range(2):
        L.wload(K_all[:, m, 0:NT], kT_own[:, m, :])
        L.wload(K_all[:, m, NT:4352], kT_oth[:, m, :])
    P.memset("dve", Vext[:, :, 64:128], 0.0)
    P.memset("dve", Vext[:, :, 256:320], 0.0)
    P.memset("dve", Vext[:, :, 64:65], 1.0)
    P.memset("dve", Vext[:, :, 256:257], 1.0)
    for (c0, c1, src) in ((0, 18, v_own), (18, 34, v_oth)):
        L.wload(Vext[:, c0:c1, 0:64], src[:, :, 0:64])
        L.wload(Vext[:, c0:c1, 128:256], src[:, :, 64:192])
        L.wload(Vext[:, c0:c1, 320:384], src[:, :, 192:256])
    vcol = [0, 64, 192, 256]
    w_qkv = L.inp("gqa_wqkv", [1024, 1536]).rearrange("(k p) n -> p k n", p=128)
    w_o = L.inp("gqa_wo", [1024, 1024])
    for m in range(2):
        for i in range(4):
            pb = m * 4 + i
            for side in range(2):
                h = 8 * m + 4 * side + i
                L.wload(Wq[:, :, pb * 128 + side * 64:pb * 128 + (side + 1) * 64], w_qkv[:, :, h * 64:(h + 1) * 64])
                L.wload(Wo[side * 64:(side + 1) * 64, pb, :], w_o[h * 64:(h + 1) * 64, :])
    gq = L.load_cols("gqa_qgc", [128, 1])
    permf = L.load_cols("perm64", [128, 128])
    perm = L.sb("perm_bf", [128, 128], BF16)
    P.copy("dve", perm[:], permf[:])
    blk1 = L.sb("blk64", [128, 128], BF16)
    P.memset("dve", blk1[:], 0.0)
    P.memset("dve", blk1[0:64, 0:64], 1.0)
    P.memset("dve", blk1[64:128, 64:128], 1.0)
    ropeC_d = L.inp("ropeC64", [128, 2048])
    ropeS_d = L.inp("ropeS64", [128, 2048])
    ropeC = af[:, 0:512]
    ropeS = af[:, 512:1024]
    blocks = [(tb * 512, 512, 0) for tb in range(4)] + [(2048, 256, 1)]
    hcount = 0
    for (c0, n, s) in blocks:
        L.norm_block(xT, c0, n, G1, mod1, 0, s, lambda k: hblk[:, k, :n], psb=7)
        if s == 0:
            P.dma("sp", ropeC[:, :n], ropeC_d[:, c0:c0 + n])
            P.dma("act", ropeS[:, :n], ropeS_d[:, c0:c0 + n])
        for pb in range(8):
            ps = L.ps[6]
            for k in range(8):
                P.mm(ps[:, :n], Wq[:, k, pb * 128:(pb + 1) * 128], hblk[:, k, :n], start=(k == 0), stop=(k == 7))
            if s == 0:
                L.qk_norm_rope(ps, 128, n, gq, blk1, 64, perm, ropeC[:, :n], ropeS[:, :n], QTb[:, pb, :n],
                               L.ps[7], L.ps[3])
            else:
                L.qk_norm_rope(ps, 128, n, gq, blk1, 64, None, None, None, QTb[:, pb, :n], L.ps[7], L.ps[3])
        lcs = list(range(34)) if s == 0 else [16, 17]
        pend = None
        for pb in range(8):
            m = pb // 4
            ob = 4 + 2 * (hcount % 2)
            hcount += 1
            pend = attend_pair(L,
                               lambda lc, m=m: K_all[0:64, m, lc * 128:(lc + 1) * 128],
                               lambda lc, m=m: K_all[64:128, m, lc * 128:(lc + 1) * 128],
                               QTb[0:64, pb, :n], QTb[64:128, pb, :n],
                               lambda lc, m=m: Vext[:, lc, vcol[2 * m]:vcol[2 * m] + 128],
                               lambda lc, m=m: Vext[:, lc, vcol[2 * m + 1]:vcol[2 * m + 1] + 128],
                               lcs, n, 0.125, L.ps[ob], L.ps[ob + 1], OTb[0:64, pb, :n], OTb[64:128, pb, :n],
                               ptiles, pending=pend)
        pend()
        for j in range(8):
            ps = L.ps[6 + (j % 2)]
            for pb in range(8):
                P.mm(ps[:, :n], Wo[:, pb, j * 128:(j + 1) * 128], OTb[:, pb, :n], start=(pb == 0), stop=(pb == 7))
            P.stt("dve", xT[:, j, c0:c0 + n], ps[:, :n], mod1[:, 16 + j, s:s + 1], xT[:, j, c0:c0 + n],
                  ALU.mult, ALU.add)
    hT = abf[:, 0:8 * NT].rearrange("p (k n) -> p k n", k=8)
    L.wslots = [abf[:, 18432 + i * 4096:18432 + (i + 1) * 4096] for i in range(4)]
    L.uT0 = abf[:, 34816:36864].rearrange("p (k n) -> p k n", k=4)
    L.uT1 = abf[:, 36864:38912].rearrange("p (k n) -> p k n", k=4)
    blocksF = [(c0, c0, n, s) for (c0, n, s) in blocks]
    L.ffn(xT, hT, blocksF, 1, mod1, G2)
    xT_out = L.outp("xT_out", [128, 8, NT])
    for k in range(8):
        P.dma(L.q(), xT_out[:, k, :], xT[:, k, :])
    return L


def prep_B(I, c, resA):
    b, half = c // 2, c % 2
    o = c ^ 1
    C, S = rope_tables(64, half * 2048, 2048)
    return {
        "ident_in": np.eye(128, dtype=np.float32),
        "xT_in": resA[c]["xT_out"],
        "mod1_in": resA[c]["mod_out"],
        "n1g1": cols(I["norm1_g"][1], 8), "n2g1": cols(I["norm2_g"][1], 8),
        "kT_own": resA[c]["kT_out"], "kT_oth": resA[o]["kT_out"][:, :, 0:2048],
        "v_own": resA[c]["v_out"], "v_oth": resA[o]["v_out"][:, 0:16, :],
        "gqa_wqkv": I["gqa_w_qkv"][0], "gqa_wo": I["gqa_w_o"][0],
        "gqa_qgc": np.tile(I["gqa_q_norm"][0], 2).reshape(128, 1),
        "perm64": perm_matrix(64, 2),
        "ropeC64": np.tile(C, (2, 1)), "ropeS64": np.tile(S, (2, 1)),
        "wff1_1": I["w_ff1"][1], "wff2_1": I["w_ff2"][1],
    }


NTC = 2320


def build_C():
    LB.NTMP = 6
    L = LB()
    LB.NTMP = 4
    P = L.P
    xT = L.sb("xT", [128, 8, NTC], F32)
    xT_in = L.inp("xT_in", [128, 8, NTC])
    for k in range(8):
        P.dma(L.q(), xT[:, k, :], xT_in[:, k, :])
    abf = L.sb("abf", [128, 38912], BF16)
    af = L.sb("af", [128, 7680], F32)
    L.af = af
    L.setup_c()
    mod2 = L.compute_mod(2)
    mod3 = L.compute_mod(3)
    n1g2 = L.load_cols("n1g2", [128, 8])
    n2g2 = L.load_cols("n2g2", [128, 8])
    n1g3 = L.load_cols("n1g3", [128, 8])
    G1 = L.make_gain("G1_2", mod2, 1, n1g2)
    G2 = L.make_gain("G2_2", mod2, 4, n2g2)
    G1_3 = L.make_gain("G1_3", mod3, 1, n1g3)
    pscale = L.load_cols("pool_scalec", [128, 8])
    psg = L.sb("psg", [128, 8, 2], F32)
    for s in range(2):
        P.tt("dve", psg[:, :, s], mod2[:, 16:24, s], pscale[:, :], ALU.mult)
    hmask = L.load_cols("hmask", [128, 16])
    invc = L.load_cols("invc", [128, 2, 4, 16])
    Wp = abf[:, 0:2048].rearrange("p (g c n) -> p g c n", g=4, c=2)
    pblk = abf[:, 2048:6144].rearrange("p (k n) -> p k n", k=8)
    w_pool = L.inp("pool_w", [4, 256, 256])
    for g in range(4):
        L.wload(Wp[:, g, :, :], w_pool[g].rearrange("(c p) n -> p c n", p=128))
    hE = af[:, 0:4224].rearrange("p (k n) -> p k n", k=8)
    wa = af[:, 4224:4752]
    wb = af[:, 4752:5280]
    rstd_all = af[:, 5280:7600]
    for (c0, n) in [(0, 512), (512, 512), (1024, 512), (1536, 512), (2048, 272)]:
        P.act(L.sqb[:, :, :n], xT[:, :, c0:c0 + n], AF.Square)
        ps = L.ps[6]
        for k in range(8):
            P.mm(ps[:, :n], L.ones_bf[:], L.sqb[:, k, :n], start=(k == 0), stop=(k == 7))
        L.rsqrt(rstd_all[:, c0:c0 + n], ps[:, :n], 1024.0 * EPS)
    blocksP = [(8 + tb * 512, 512, 0, tb) for tb in range(4)] + [(2064, 256, 1, 4)]
    for (c0, n, s, bi) in blocksP:
        ne = n + 16
        if s == 0:
            for k in range(8):
                t = L.tmp()
                for (a0, a1) in ((0, 512), (512, ne)):
                    P.tt(L.ve(), t[:, 0:a1 - a0], xT[:, k, c0 - 8 + a0:c0 - 8 + a1],
                         rstd_all[:, c0 - 8 + a0:c0 - 8 + a1], ALU.mult)
                    P.act(hE[:, k, a0:a1], t[:, 0:a1 - a0], AF.Identity, scale=G1[:, k, s:s + 1],
                          bias=mod2[:, k, s:s + 1])
                if bi == 0:
                    P.tt("pool", hE[:, k, 0:8], hE[:, k, 0:8], hmask[:, 0:8], ALU.mult)
                if bi == 3:
                    P.tt("pool", hE[:, k, ne - 8:ne], hE[:, k, ne - 8:ne], hmask[:, 8:16], ALU.mult)
        else:
            P.memset("pool", hE[:, :, 0:8], 0.0)
            P.memset("pool", hE[:, :, 8 + n:16 + n], 0.0)
            for k in range(8):
                t = L.tmp()
                P.tt(L.ve(), t[:, :n], xT[:, k, c0:c0 + n], rstd_all[:, c0:c0 + n], ALU.mult)
                P.act(hE[:, k, 8:8 + n], t[:, :n], AF.Identity, scale=G1[:, k, s:s + 1], bias=mod2[:, k, s:s + 1])
        for k in range(8):
            wi = k // 2
            w = 2 << wi
            eng = "dve" if k % 2 == 0 else "pool"
            cur = hE[:, k, :]
            ln_ = ne
            step = 1
            bufs = [wa, wb]
            bi_ = 0
            while step < w:
                nxt = bufs[bi_ % 2]
                bi_ += 1
                ln2 = ln_ - step
                P.tt(eng, nxt[:, 0:ln2], cur[:, 0:ln2], cur[:, step:step + ln2], ALU.add)
                cur = nxt
                ln_ = ln2
                step *= 2
            off = 8 - w // 2
            P.stt("dve", pblk[:, k, :n], cur[:, off:off + n], 1.0 / w, hE[:, k, 8:8 + n], ALU.mult, ALU.subtract)
            edges = []
            if s == 1 or bi == 0:
                edges.append((0, 0))
            if s == 1 or bi == 3:
                edges.append((n - 8, 8))
            for (t0, e0) in edges:
                t = L.tmp()
                P.tt("dve", t[:, 0:8], cur[:, off + t0:off + t0 + 8], invc[:, s, wi, e0:e0 + 8], ALU.mult)
                P.tt("dve", pblk[:, k, t0:t0 + 8], t[:, 0:8], hE[:, k, 8 + t0:16 + t0], ALU.subtract)
        for g in range(4):
            for oc in range(2):
                j = 2 * g + oc
                ps = L.ps[j % 4]
                for kc in range(2):
                    P.mm(ps[:, :n], Wp[:, g, kc, oc * 128:(oc + 1) * 128], pblk[:, 2 * g + kc, :n],
                         start=(kc == 0), stop=(kc == 1))
                P.stt("dve", xT[:, j, c0:c0 + n], ps[:, :n], psg[:, j, s:s + 1], xT[:, j, c0:c0 + n],
                      ALU.mult, ALU.add)
    hT = abf[:, 0:8 * NT].rearrange("p (k n) -> p k n", k=8)
    L.wslots = [abf[:, 18432 + i * 4096:18432 + (i + 1) * 4096] for i in range(4)]
    L.uT0 = abf[:, 34816:36864].rearrange("p (k n) -> p k n", k=4)
    L.uT1 = abf[:, 36864:38912].rearrange("p (k n) -> p k n", k=4)
    blocksF = [(8 + tb * 512, tb * 512, 512, 0) for tb in range(4)] + [(2064, 2048, 256, 1)]
    L.ffn(xT, hT, blocksF, 2, mod2, G2)
    mod_out = L.outp("mod_out", [128, 48, 2])
    P.dma("sp", mod_out, mod3[:])
    xT_out = L.outp("xT_out", [128, 8, NT])
    P.dma("sp", xT_out[:, :, 0:2048], xT[:, :, 8:2056])
    P.dma("act", xT_out[:, :, 2048:2304], xT[:, :, 2064:2320])
    for (xc0, hc0, n, s) in blocksF:
        L.norm_block(xT, xc0, n, G1_3, mod3, 0, s, lambda k: hT[:, k, hc0:hc0 + n])
    Wd = abf[:, 18432:21504].rearrange("p (k n) -> p k n", k=8)
    Wukv = abf[:, 21504:25600].rearrange("p (c n) -> p c n", c=2)
    ckvn = abf[:, 25600:26624].rearrange("p (c n) -> p c n", c=2)
    krg_bf = abf[:, 26624:27136]
    sq_r = abf[:, 27136:27648]
    sq_n = abf[:, 27648:28160]
    sq_n2 = [abf[:, 27648:28160], abf[:, 28160:28672]]
    w_dkv = L.inp("mla_wdkv", [1024, 288]).rearrange("(k p) n -> p k n", p=128)
    w_ukv = L.inp("mla_wukv", [256, 2048]).rearrange("(c p) n -> p c n", p=128)
    P.memset("dve", Wd[:, :, 256:384], 0.0)
    L.wload(Wd[:, :, 0:256], w_dkv[:, :, 0:256])
    L.wload(Wd[:, :, 320:352], w_dkv[:, :, 256:288])
    L.wload(Wukv[:, :, :], w_ukv)
    gkvl = L.load_cols("mla_kvlgc", [128, 2])
    gk96 = L.load_cols("mla_kgc", [128, 1])
    permf = L.load_cols("perm96", [128, 128])
    perm = L.sb("perm_bf", [128, 128], BF16)
    P.copy("dve", perm[:], permf[:])
    ropeC_d = L.inp("ropeC96", [128, 2048])
    ropeS_d = L.inp("ropeS96", [128, 2048])
    ropeC = af[:, 0:512]
    ropeS = af[:, 512:1024]
    krr = af[:, 1024:1536]
    khs = [af[:, 1536:2048], af[:, 2048:2560]]
    vst = af[:, 2560:6656].rearrange("p (t n) -> p t n", t=4)
    kT3_out = L.outp("kT3_out", [96, 16, NT])
    v3_out = L.outp("v3_out", [128, 18, 1024])
    Wukv4 = Wukv.rearrange("p c (h n) -> p c h n", h=16)
    kcnt = 0
    for (xc0, hc0, n, s) in blocksF:
        if s == 0:
            P.dma("sp", ropeC[:, :n], ropeC_d[:, hc0:hc0 + n])
            P.dma("act", ropeS[:, :n], ropeS_d[:, hc0:hc0 + n])
        for cc in range(2):
            ps = L.ps[cc]
            for k in range(8):
                P.mm(ps[:, :n], Wd[:, k, cc * 128:(cc + 1) * 128], hT[:, k, hc0:hc0 + n], start=(k == 0), stop=(k == 7))
            P.act(L.sqb[:, cc, :n], ps[:, :n], AF.Square)
        pss = L.ps[2]
        for cc in range(2):
            P.mm(pss[:, :n], L.ones_bf[:], L.sqb[:, cc, :n], start=(cc == 0), stop=(cc == 1))
        rs = L.tmp()
        L.rsqrt(rs[:, :n], pss[:, :n], 256.0 * EPS)
        for cc in range(2):
            t = L.tmp()
            P.act(t[:, :n], L.ps[cc][:, :n], AF.Identity, scale=gkvl[:, cc:cc + 1])
            P.stt("dve", ckvn[:, cc, :n], t[:, :n], 16.0, rs[:, :n], ALU.mult, ALU.mult)
        psr = L.ps[3]
        for k in range(8):
            P.mm(psr[:, :n], Wd[:, k, 256:384], hT[:, k, hc0:hc0 + n], start=(k == 0), stop=(k == 7))
        P.act(sq_r[64:96, :n], psr[64:96, :n], AF.Square)
        krg = L.tmp()
        P.act(krg[64:96, :n], psr[64:96, :n], AF.Identity, scale=gk96[64:96, 0:1])
        if s == 0:
            P.copy("pool", krg_bf[64:96, :n], krg[64:96, :n])
            psw = L.ps[4]
            P.mm(psw[0:96, :n], perm[64:96, 0:96], krg_bf[64:96, :n])
            t2 = L.tmp()
            P.tt("dve", t2[64:96, :n], psw[64:96, :n], ropeS[64:96, :n], ALU.mult)
            P.tt("dve", krr[64:96, :n], krg[64:96, :n], ropeC[64:96, :n], ALU.mult)
            P.tt("pool", krr[64:96, :n], krr[64:96, :n], t2[64:96, :n], ALU.add)
        else:
            P.copy("dve", krr[64:96, :n], krg[64:96, :n])
        pendk = None
        for h in range(16):
            psk = L.ps[5 + (h % 2)]
            for cc in range(2):
                P.mm(psk[0:64, :n], Wukv4[:, cc, h, 0:64], ckvn[:, cc, :n], start=(cc == 0), stop=(cc == 1))
            sqn = sq_n2[h % 2]
            P.act(sqn[0:64, :n], psk[0:64, :n], AF.Square)
            ps2 = L.ps[7] if h % 2 == 0 else L.ps[4]
            P.mm(ps2[0:96, :n], L.ones_bf[0:64, 0:96], sqn[0:64, :n], start=True, stop=False)
            P.mm(ps2[0:96, :n], L.ones_bf[64:96, 0:96], sq_r[64:96, :n], start=False, stop=True)
            t = L.tmp()
            P.act(t[0:64, :n], psk[0:64, :n], AF.Identity, scale=gk96[0:64, 0:1])
            if pendk is not None:
                pendk()

            def pendk(ps2=ps2, t=t, h=h):
                nonlocal kcnt
                rs96 = L.tmp()
                L.rsqrt(rs96[0:96, :n], ps2[0:96, :n], 96.0 * EPS)
                kh = khs[kcnt % 2]
                kcnt += 1
                P.stt("dve", kh[0:64, :n], t[0:64, :n], 96.0 ** 0.5, rs96[0:64, :n], ALU.mult, ALU.mult)
                P.stt("dve", kh[64:96, :n], krr[64:96, :n], 96.0 ** 0.5, rs96[64:96, :n], ALU.mult, ALU.mult)
                P.dma(L.q(), kT3_out[:, h, hc0:hc0 + n], kh[0:96, :n])
        pendk()
        nt = n // 128
        for tt_ in range(nt):
            for hh in range(2):
                ps = L.ps[hh]
                for cc in range(2):
                    P.mm(ps[:, 0:512].rearrange("p (h n) -> p h n", h=8),
                         ckvn[:, cc, tt_ * 128:(tt_ + 1) * 128], Wukv4[:, cc, hh * 8:(hh + 1) * 8, 64:128],
                         start=(cc == 0), stop=(cc == 1))
                P.copy("act" if hh else "dve", vst[:, tt_, hh * 512:(hh + 1) * 512], ps[:, 0:512])
        P.dma(L.q(), v3_out[:, hc0 // 128:hc0 // 128 + nt, :], vst[:, :nt, :])
    return L


def pool_invc(half):
    t = np.zeros((128, 2, 4, 16), np.float32)
    for wi, w in enumerate((2, 4, 8, 16)):
        for seg in range(2):
            for e in range(8):
                cnt_s = (e + w // 2) - max(e - w // 2, 0)
                cnt_e = w // 2 + min(w // 2, 8 - e)
                real_s = (seg == 1) or (half == 0)
                real_e = (seg == 1) or (half == 1)
                t[:, seg, wi, e] = 1.0 / (cnt_s if real_s else w)
                t[:, seg, wi, 8 + e] = 1.0 / (cnt_e if real_e else w)
    return t


def prep_C(I, c, resB):
    b, half = c // 2, c % 2
    o = c ^ 1
    own = resB[c]["xT_out"]
    oth = resB[o]["xT_out"]
    z8 = np.zeros((128, 8, 8), np.float32)
    left = oth[:, :, 2040:2048] if half == 1 else z8
    right = oth[:, :, 0:8] if half == 0 else z8
    xT_in = np.concatenate([left, own[:, :, 0:2048], right, own[:, :, 2048:2304]], 2)
    hmask = np.zeros((128, 16), np.float32)
    hmask[:, 0:8] = 1.0 if half == 1 else 0.0
    hmask[:, 8:16] = 1.0 if half == 0 else 0.0
    C32, S32 = rope_tables(32, half * 2048, 2048)
    C = np.ones((128, 2048), np.float32)
    S = np.zeros((128, 2048), np.float32)
    C[64:96] = C32
    S[64:96] = S32
    gk = np.zeros((128, 1), np.float32)
    gk[0:96, 0] = I["mla_k_norm"][0]
    return {
        "ident_in": np.eye(128, dtype=np.float32),
        "xT_in": xT_in, "hmask": hmask, "invc": pool_invc(half),
        "cvec": np.stack([cols(I["c"][b], 8), cols(I["c_ctx"], 8)], -1),
        "wmod2": I["w_mod"][2], "bmodc2": cols(I["b_mod"][2], 48),
        "wmod3": I["w_mod"][3], "bmodc3": cols(I["b_mod"][3], 48),
        "n1g2": cols(I["norm1_g"][2], 8), "n2g2": cols(I["norm2_g"][2], 8), "n1g3": cols(I["norm1_g"][3], 8),
        "pool_scalec": cols(I["pool_scale"][0], 8), "pool_w": I["pool_w"][0],
        "wff1_2": I["w_ff1"][2], "wff2_2": I["w_ff2"][2],
        "mla_wdkv": I["mla_w_dkv"][0], "mla_wukv": I["mla_w_ukv"][0],
        "mla_kvlgc": cols(I["mla_kv_lora_norm"][0], 2), "mla_kgc": gk,
        "perm96": perm_matrix(32, 1, base=64),
        "ropeC96": C, "ropeS96": S,
    }


NL = 2048


def build_D():
    LB.NTMP = 8
    L = LB()
    LB.NTMP = 4
    P = L.P
    nc = L.nc
    xT = L.sb("xT", [128, 8, NL], F32)
    xT_in = L.inp("xT_in", [128, 8, NL])
    for k in range(8):
        P.dma(L.q(), xT[:, k, :], xT_in[:, k, :])
    abf = L.sb("abf", [128, 43008], BF16)
    af = L.sb("af", [128, 4096], F32)
    L.af = af
    mod3 = L.load_cols("mod3_in", [128, 48, 2])
    n1g = L.load_cols("n1g3", [128, 8])
    n2g = L.load_cols("n2g3", [128, 8])
    G1 = L.make_gain("G1", mod3, 1, n1g)
    G2 = L.make_gain("G2", mod3, 4, n2g)
    ebias = L.sb("ebias", [128, 1], F32)
    P.memset("dve", ebias[:], -4.0)
    hblk = abf[:, 0:4096].rearrange("p (k n) -> p k n", k=8)
    Wdq = abf[:, 4096:10240].rearrange("p (k n) -> p k n", k=8)
    Wuq = abf[:, 10240:19456].rearrange("p (c n) -> p c n", c=6)
    cqn = abf[:, 19456:22528].rearrange("p (c n) -> p c n", c=6)
    qst = [abf[:, 22528 + i * 512:23040 + i * 512] for i in range(2)]
    w_dq = L.inp("mla_wdq", [1024, 768]).rearrange("(k p) n -> p k n", p=128)
    w_uq = L.inp("mla_wuq", [768, 1536]).rearrange("(c p) n -> p c n", p=128)
    L.wload(Wdq[:, :, 0:384], w_dq[:, :, 0:384])
    L.wload(Wdq[:, :, 384:768], w_dq[:, :, 384:768])
    for s3 in range(3):
        L.wload(Wuq[:, :, s3 * 512:(s3 + 1) * 512], w_uq[:, :, s3 * 512:(s3 + 1) * 512])
    gql = L.load_cols("mla_qlgc", [128, 6])
    gq96 = L.load_cols("mla_qgc", [128, 1])
    permf = L.load_cols("perm96", [128, 128])
    perm = L.sb("perm_bf", [128, 128], BF16)
    P.copy("dve", perm[:], permf[:])
    ropeC_d = L.inp("ropeC96", [128, 2048])
    ropeS_d = L.inp("ropeS96", [128, 2048])
    ropeC = af[:, 0:512]
    ropeS = af[:, 512:1024]
    qscr = nc.dram_tensor("qscratch", [16, 96, NL], BF16, kind="Internal").ap()
    qc = 0
    for tb in range(4):
        c0, n = tb * 512, 512
        L.norm_block(xT, c0, n, G1, mod3, 0, 0, lambda k: hblk[:, k, :n], psb=7)
        P.dma("sp", ropeC[:, :n], ropeC_d[:, c0:c0 + n])
        P.dma("act", ropeS[:, :n], ropeS_d[:, c0:c0 + n])
        for cc in range(6):
            ps = L.ps[cc]
            for k in range(8):
                P.mm(ps[:, :n], Wdq[:, k, cc * 128:(cc + 1) * 128], hblk[:, k, :n], start=(k == 0), stop=(k == 7))
            P.act(L.sqb[:, cc, :n], ps[:, :n], AF.Square)
        pss = L.ps[6]
        for cc in range(6):
            P.mm(pss[:, :n], L.ones_bf[:], L.sqb[:, cc, :n], start=(cc == 0), stop=(cc == 5))
        rs = L.rstd
        L.rsqrt(rs[:, :n], pss[:, :n], 768.0 * EPS)
        for cc in range(6):
            t = L.tmp()
            P.act(t[:, :n], L.ps[cc][:, :n], AF.Identity, scale=gql[:, cc:cc + 1])
            P.stt("dve", cqn[:, cc, :n], t[:, :n], 768.0 ** 0.5, rs[:, :n], ALU.mult, ALU.mult)
        pendq = None
        for h in range(16):
            ps = L.ps[h % 2]
            for cc in range(6):
                P.mm(ps[0:96, :n], Wuq[:, cc, h * 96:(h + 1) * 96], cqn[:, cc, :n], start=(cc == 0), stop=(cc == 5))
            qo = qst[qc % 2]
            qc += 1
            stb = L.qk_norm_rope(ps, 96, n, gq96, L.ones_bf, 96, perm, ropeC[0:96, :n], ropeS[0:96, :n],
                                 qo[0:96, :n], L.ps[2 + (h % 2)], L.ps[4 + (h % 2)], defer=True)
            if pendq is not None:
                pendq()

            def pendq(stb=stb, qo=qo, h=h):
                stb()
                P.dma(L.q(), qscr[h, :, c0:c0 + n], qo[0:96, :n])
        pendq()
    OT = abf[:, 0:16384].rearrange("p (k n) -> p k n", k=8)
    Kh = [abf[:, 16384 + i * 4352:16384 + (i + 1) * 4352] for i in range(2)]
    Vh = [abf[:, 25088 + i * 6528:25088 + (i + 1) * 6528].rearrange("p (c n) -> p c n", c=34) for i in range(2)]
    Qh = [abf[:, 38144 + i * 2048:38144 + (i + 1) * 2048] for i in range(2)]
    ptiles = [L.sb("ptile%d" % i, [128, 1024], BF16) for i in range(2)]
    kT_own = L.inp("kT3_own", [96, 16, NT])
    kT_oth = L.inp("kT3_oth", [96, 16, NL])
    v_own = L.inp("v3_own", [128, 18, 1024])
    v_oth = L.inp("v3_oth", [128, 16, 1024])
    for i in range(2):
        P.memset("dve", Vh[i][:, :, 0:64], 0.0)
        P.memset("dve", Vh[i][:, :, 128:192], 0.0)
        P.memset("dve", Vh[i][:, :, 0:1], 1.0)
        P.memset("dve", Vh[i][:, :, 128:129], 1.0)

    def load_head(h):
        i = h % 2
        L.wload(Kh[i][0:96, 0:NT], kT_own[:, h, :])
        L.wload(Kh[i][0:96, NT:4352], kT_oth[:, h, :])
        L.wload(Vh[i][:, 0:18, 64:128], v_own[:, :, h * 64:(h + 1) * 64])
        L.wload(Vh[i][:, 18:34, 64:128], v_oth[:, :, h * 64:(h + 1) * 64])
        P.dma("sp", Qh[i][0:96, :], qscr[h, :, :])

    load_head(0)
    hcount = 0
    pend = None
    for h in range(16):
        if h + 1 < 16:
            load_head(h + 1)
        i = h % 2
        side = h % 2
        sl0, sl1 = (0, 64) if side == 0 else (64, 128)
        vc = 64 if side == 0 else 0
        for qb in range(4):
            ps_o = L.ps[4 + (hcount % 2)]
            hcount += 1
            pend = attend_head(L,
                               lambda lc, i=i: Kh[i][0:96, lc * 128:(lc + 1) * 128],
                               Qh[i][0:96, qb * 512:(qb + 1) * 512],
                               lambda lc, i=i, vc=vc: Vh[i][:, lc, vc:vc + 128],
                               list(range(34)), 512, 96.0 ** -0.5, ps_o, side,
                               OT[sl0:sl1, h // 2, qb * 512:(qb + 1) * 512], ptiles, ebias=ebias[:, 0:1],
                               pending=pend)
    pend()
    Wo = abf[:, 16384:24576].rearrange("p (k n) -> p k n", k=8)
    w_o = L.inp("mla_wo", [1024, 1024]).rearrange("(k p) n -> p k n", p=128)
    for s2 in range(2):
        L.wload(Wo[:, :, s2 * 512:(s2 + 1) * 512], w_o[:, :, s2 * 512:(s2 + 1) * 512])
    for qb in range(4):
        for j in range(8):
            ps = L.ps[6 + (j % 2)]
            for pb in range(8):
                P.mm(ps[:, :512], Wo[:, pb, j * 128:(j + 1) * 128], OT[:, pb, qb * 512:(qb + 1) * 512],
                     start=(pb == 0), stop=(pb == 7))
            P.stt("dve", xT[:, j, qb * 512:(qb + 1) * 512], ps[:, :512], mod3[:, 16 + j, 0:1],
                  xT[:, j, qb * 512:(qb + 1) * 512], ALU.mult, ALU.add)
    hT = abf[:, 0:8 * NL].rearrange("p (k n) -> p k n", k=8)
    L.wslots = [abf[:, 16384 + i * 4096:16384 + (i + 1) * 4096] for i in range(4)]
    L.uT0 = abf[:, 32768:34816].rearrange("p (k n) -> p k n", k=4)
    L.uT1 = abf[:, 34816:36864].rearrange("p (k n) -> p k n", k=4)
    blocksF = [(tb * 512, tb * 512, 512, 0) for tb in range(4)]
    L.ffn(xT, hT, blocksF, 3, mod3, G2)
    out = L.outp("out", [NL, 1024])
    ost = [af[:, i * 1024:(i + 1) * 1024] for i in range(2)]
    for r in range(16):
        st = ost[r % 2]
        for half in range(2):
            ps = L.ps[(r * 2 + half) % 4]
            for kk in range(4):
                k = half * 4 + kk
                P.transpose(ps[:, kk * 128:(kk + 1) * 128], xT[:, k, r * 128:(r + 1) * 128], L.ident[:, :])
            P.copy("act" if half else "dve", st[:, half * 512:(half + 1) * 512], ps[:, 0:512])
        P.dma(L.q(), out[r * 128:(r + 1) * 128, :], st[:, :])
    return L


def prep_D(I, c, resC):
    b, half = c // 2, c % 2
    o = c ^ 1
    C32, S32 = rope_tables(32, half * 2048, 2048)
    C = np.ones((128, 2048), np.float32)
    S = np.zeros((128, 2048), np.float32)
    C[64:96] = C32
    S[64:96] = S32
    gq = np.zeros((128, 1), np.float32)
    gq[0:96, 0] = I["mla_q_norm"][0]
    return {
        "ident_in": np.eye(128, dtype=np.float32),
        "xT_in": resC[c]["xT_out"][:, :, 0:2048],
        "mod3_in": resC[c]["mod_out"],
        "n1g3": cols(I["norm1_g"][3], 8), "n2g3": cols(I["norm2_g"][3], 8),
        "mla_wdq": I["mla_w_dq"][0], "mla_wuq": I["mla_w_uq"][0], "mla_wo": I["mla_w_o"][0],
        "mla_qlgc": cols(I["mla_q_lora_norm"][0], 6), "mla_qgc": gq,
        "perm96": perm_matrix(32, 1, base=64),
        "ropeC96": C, "ropeS96": S,
        "kT3_own": resC[c]["kT3_out"], "kT3_oth": resC[o]["kT3_out"][:, :, 0:2048],
        "v3_own": resC[c]["v3_out"], "v3_oth": resC[o]["v3_out"][:, 0:16, :],
        "wff1_3": I["w_ff1"][3], "wff2_3": I["w_ff2"][3],
    }


def kernel(**inputs):
    I = {k: np.asarray(v) for k, v in inputs.items()}
    resA = run_launch("A", build_A, [prep_A(I, c) for c in range(CORES)])
    resB = run_launch("B", build_B, [prep_B(I, c, resA) for c in range(CORES)])
    resC = run_launch("C", build_C, [prep_C(I, c, resB) for c in range(CORES)])
    resD = run_launch("D", build_D, [prep_D(I, c, resC) for c in range(CORES)])
    out = np.empty((4, 4096, 1024), np.float32)
    for c in range(CORES):
        out[c // 2, (c % 2) * 2048:(c % 2 + 1) * 2048, :] = resD[c]["out"]
    return out
```
